# Optimizing a Trainium2 kernel written in Bass

```python
import math
import jax, jax.numpy as jnp
from jax import lax
import numpy as np

D_MODEL = 1024
BATCH = 8
SEQ = 4096
DEPTH = 2

HEAD_DIM = 64
FOX_HEADS = 8
SB_HEADS = 8
Q_BLOCK = 128
SSD_HEADS = 16
SSD_HEAD_DIM = 64
SSD_GROUPS = 2
SSD_STATE = 128
SSD_CONV = 4
SSD_CHUNK = 128
D_FOX = FOX_HEADS * HEAD_DIM
D_SB = SB_HEADS * HEAD_DIM
D_SSD = SSD_HEADS * SSD_HEAD_DIM
D_MIX = D_FOX + D_SB + D_SSD
D_BC = SSD_GROUPS * SSD_STATE
SSD_CONV_CH = D_SSD + 2 * D_BC
D_FF = ((8 * D_MODEL // 3 + 127) // 128) * 128
FFN_CONV = 3
NORM_EPS = 1e-6
IN_SIZES = [D_FOX, D_FOX, D_FOX, FOX_HEADS,
            D_SB, D_SB, D_SB,
            D_SSD, SSD_CONV_CH, SSD_HEADS]
N_IN = sum(IN_SIZES)
SPLIT_POINTS = np.cumsum(IN_SIZES)[:-1].tolist()

kernel_name = "hybrid_fox_ssd_stickbreak_block"


def rms_norm(x, g):
    xf = x.astype(jnp.float32)
    y = xf * lax.rsqrt(jnp.mean(xf * xf, axis=-1, keepdims=True) + NORM_EPS)
    return (y * g.astype(jnp.float32)).astype(x.dtype)


def grouped_rms_norm(x, g, n_groups):
    shp = x.shape
    xg = x.reshape(shp[:-1] + (n_groups, shp[-1] // n_groups))
    gg = g.reshape(n_groups, shp[-1] // n_groups)
    return rms_norm(xg, gg).reshape(shp)


def causal_depthwise_conv(x, w, b):
    width = w.shape[0]
    y = lax.conv_general_dilated(
        x, w[:, None, :].astype(x.dtype), window_strides=(1,), padding=[(width - 1, 0)],
        dimension_numbers=('NWC', 'WIO', 'NWC'), feature_group_count=x.shape[-1])
    return y + b.astype(x.dtype)


def forgetting_attention(q, k, v, log_f):
    b, s, h, dh = q.shape
    nb = s // Q_BLOCK
    q, k, v = (t.transpose(0, 2, 1, 3) for t in (q, k, v))
    c = jnp.cumsum(log_f, axis=1).transpose(0, 2, 1)
    qb = q.reshape(b, h, nb, Q_BLOCK, dh).transpose(2, 0, 1, 3, 4)
    cb = c.reshape(b, h, nb, Q_BLOCK).transpose(2, 0, 1, 3)
    pos = jnp.arange(s)
    qpos = pos.reshape(nb, Q_BLOCK)
    scale = dh ** -0.5

    def block(args):
        qi, ci, ti = args
        logits = jnp.einsum('bhqd,bhkd->bhqk', qi, k).astype(jnp.float32) * scale
        logits = logits + ci[..., :, None] - c[:, :, None, :]
        logits = jnp.where(ti[:, None] >= pos[None, :], logits, -jnp.inf)
        p = jax.nn.softmax(logits, axis=-1)
        return jnp.einsum('bhqk,bhkd->bhqd', p.astype(v.dtype), v)

    out = lax.map(block, (qb, cb, qpos))
    return out.transpose(1, 0, 3, 2, 4).reshape(b, s, h * dh)


def stick_breaking_attention(q, k, v):
    b, s, h, dh = q.shape
    nb = s // Q_BLOCK
    q, k, v = (t.transpose(0, 2, 1, 3) for t in (q, k, v))
    qb = q.reshape(b, h, nb, Q_BLOCK, dh).transpose(2, 0, 1, 3, 4)
    pos = jnp.arange(s)
    qpos = pos.reshape(nb, Q_BLOCK)
    scale = dh ** -0.5

    def block(args):
        qi, ti = args
        z = jnp.einsum('bhqd,bhkd->bhqk', qi, k).astype(jnp.float32) * scale
        mask = pos[None, :] < ti[:, None]
        log_keep = jnp.where(mask, jax.nn.log_sigmoid(-z), 0.0)
        cum = jnp.cumsum(log_keep, axis=-1)
        log_w = jax.nn.log_sigmoid(z) + cum[..., -1:] - cum
        w = jnp.where(mask, jnp.exp(log_w), 0.0)
        return jnp.einsum('bhqk,bhkd->bhqd', w.astype(v.dtype), v)

    out = lax.map(block, (qb, qpos))
    return out.transpose(1, 0, 3, 2, 4).reshape(b, s, h * dh)


def ssd_mixer(xbc, z, dt_raw, conv_w, conv_b, dt_bias, a_log, d_skip, norm_g):
    f32 = jnp.float32
    b, s, _ = xbc.shape
    hpg = SSD_HEADS // SSD_GROUPS
    nc = s // SSD_CHUNK
    xbc = jax.nn.silu(causal_depthwise_conv(xbc, conv_w, conv_b))
    xs, bm, cm = jnp.split(xbc, [D_SSD, D_SSD + D_BC], axis=-1)
    xs = xs.astype(f32).reshape(b, s, SSD_HEADS, SSD_HEAD_DIM)
    dt = jax.nn.softplus(dt_raw.astype(f32) + dt_bias.astype(f32))
    a = -jnp.exp(a_log.astype(f32))
    x_c = (xs * dt[..., None]).reshape(b, nc, SSD_CHUNK, SSD_GROUPS, hpg, SSD_HEAD_DIM)
    b_c = bm.astype(f32).reshape(b, nc, SSD_CHUNK, SSD_GROUPS, SSD_STATE)
    c_c = cm.astype(f32).reshape(b, nc, SSD_CHUNK, SSD_GROUPS, SSD_STATE)
    a_cs = jnp.cumsum((dt * a).reshape(b, nc, SSD_CHUNK, SSD_GROUPS, hpg), axis=2)
    a_t = a_cs.transpose(0, 1, 3, 4, 2)
    causal = jnp.tril(jnp.ones((SSD_CHUNK, SSD_CHUNK), dtype=bool))
    seg = jnp.exp(jnp.where(causal, a_t[..., :, None] - a_t[..., None, :], -jnp.inf))
    cb = jnp.einsum('bclgn,bcsgn->bcgls', c_c, b_c)
    y_diag = jnp.einsum('bcghls,bcsghp->bclghp', cb[:, :, :, None] * seg, x_c)
    decay_end = jnp.exp(a_cs[:, :, -1:] - a_cs)
    states = jnp.einsum('bclgn,bclghp->bcghpn', b_c, x_c * decay_end[..., None])
    chunk_decay = jnp.exp(a_cs[:, :, -1])

    def step(hstate, inp):
        dec, st = inp
        return dec[..., None, None] * hstate + st, hstate

    _, prev = lax.scan(step, jnp.zeros_like(states[:, 0]),
                       (jnp.moveaxis(chunk_decay, 1, 0), jnp.moveaxis(states, 1, 0)))
    prev = jnp.moveaxis(prev, 0, 1)
    y_off = jnp.einsum('bclgn,bcghpn->bclghp', c_c, prev) * jnp.exp(a_cs)[..., None]
    y = (y_diag + y_off).reshape(b, s, SSD_HEADS, SSD_HEAD_DIM) + xs * d_skip.astype(f32)[:, None]
    y = y.reshape(b, s, D_SSD) * jax.nn.silu(z.astype(f32))
    return grouped_rms_norm(y, norm_g, SSD_GROUPS).astype(z.dtype)


def hybrid_layer(x, mix_g, w_in, fox_f_bias, fox_out_g, sb_out_g, ssd_conv_w, ssd_conv_b,
                 ssd_dt_bias, ssd_a_log, ssd_d, ssd_norm_g, w_out, ffn_g, w_up,
                 ffn_conv_w, ffn_conv_b, w_down):
    b, s, _ = x.shape
    h = rms_norm(x, mix_g)
    proj = jnp.einsum('bsd,de->bse', h, w_in)
    fq, fk, fv, ff, sq, sk, sv, z, xbc, dt = jnp.split(proj, SPLIT_POINTS, axis=-1)
    heads = lambda t, n: t.reshape(b, s, n, HEAD_DIM)
    log_f = jax.nn.log_sigmoid(ff.astype(jnp.float32) + fox_f_bias.astype(jnp.float32))
    y_fox = grouped_rms_norm(
        forgetting_attention(heads(fq, FOX_HEADS), heads(fk, FOX_HEADS), heads(fv, FOX_HEADS), log_f),
        fox_out_g, FOX_HEADS)
    y_sb = grouped_rms_norm(
        stick_breaking_attention(heads(sq, SB_HEADS), heads(sk, SB_HEADS), heads(sv, SB_HEADS)),
        sb_out_g, SB_HEADS)
    y_ssd = ssd_mixer(xbc, z, dt, ssd_conv_w, ssd_conv_b, ssd_dt_bias, ssd_a_log, ssd_d, ssd_norm_g)
    y = jnp.concatenate([y_fox.astype(h.dtype), y_sb.astype(h.dtype), y_ssd], axis=-1)
    x = x + jnp.einsum('bse,ed->bsd', y, w_out)
    h = rms_norm(x, ffn_g)
    u = causal_depthwise_conv(jnp.einsum('bsd,df->bsf', h, w_up), ffn_conv_w, ffn_conv_b)
    gate, val = jnp.split(u, 2, axis=-1)
    return x + jnp.einsum('bsf,fd->bsd', jax.nn.silu(gate) * val, w_down)


def setup_inputs(seed: int = 0) -> dict:
    key = jax.random.key(seed)
    ks = jax.random.split(key, 24)
    f32 = jnp.float32
    nrm = lambda k, shape, sc: sc * jax.random.normal(k, shape, f32)
    gain = lambda k, shape: 1.0 + 0.02 * jax.random.normal(k, shape, f32)
    dt0 = jnp.exp(jax.random.uniform(ks[9], (DEPTH, SSD_HEADS), f32, math.log(1e-3), math.log(1e-1)))
    return {
        "x": nrm(ks[0], (BATCH, SEQ, D_MODEL), 1.0),
        "mix_norm_g": gain(ks[1], (DEPTH, D_MODEL)),
        "w_in": nrm(ks[2], (DEPTH, D_MODEL, N_IN), D_MODEL ** -0.5),
        "fox_f_bias": jax.random.uniform(ks[3], (DEPTH, FOX_HEADS), f32, 1.0, 6.0),
        "fox_out_g": gain(ks[4], (DEPTH, D_FOX)),
        "sb_out_g": gain(ks[5], (DEPTH, D_SB)),
        "ssd_conv_w": nrm(ks[6], (DEPTH, SSD_CONV, SSD_CONV_CH), SSD_CONV ** -0.5),
        "ssd_conv_b": nrm(ks[7], (DEPTH, SSD_CONV_CH), 0.02),
        "ssd_dt_bias": dt0 + jnp.log(-jnp.expm1(-dt0)),
        "ssd_a_log": jnp.log(jax.random.uniform(ks[10], (DEPTH, SSD_HEADS), f32, 1.0, 16.0)),
        "ssd_d": gain(ks[11], (DEPTH, SSD_HEADS)),
        "ssd_norm_g": gain(ks[12], (DEPTH, D_SSD)),
        "w_out": nrm(ks[13], (DEPTH, D_MIX, D_MODEL), D_MIX ** -0.5),
        "ffn_norm_g": gain(ks[14], (DEPTH, D_MODEL)),
        "w_up": nrm(ks[15], (DEPTH, D_MODEL, 2 * D_FF), D_MODEL ** -0.5),
        "ffn_conv_w": nrm(ks[16], (DEPTH, FFN_CONV, 2 * D_FF), FFN_CONV ** -0.5),
        "ffn_conv_b": nrm(ks[17], (DEPTH, 2 * D_FF), 0.02),
        "w_down": nrm(ks[18], (DEPTH, D_FF, D_MODEL), D_FF ** -0.5),
        "final_norm_g": gain(ks[19], (D_MODEL,)),
    }


def reference(x, mix_norm_g, w_in, fox_f_bias, fox_out_g, sb_out_g, ssd_conv_w, ssd_conv_b,
              ssd_dt_bias, ssd_a_log, ssd_d, ssd_norm_g, w_out, ffn_norm_g, w_up,
              ffn_conv_w, ffn_conv_b, w_down, final_norm_g):
    for l in range(DEPTH):
        x = hybrid_layer(x, mix_norm_g[l], w_in[l], fox_f_bias[l], fox_out_g[l], sb_out_g[l],
                         ssd_conv_w[l], ssd_conv_b[l], ssd_dt_bias[l], ssd_a_log[l], ssd_d[l],
                         ssd_norm_g[l], w_out[l], ffn_norm_g[l], w_up[l], ffn_conv_w[l],
                         ffn_conv_b[l], w_down[l])
    return rms_norm(x, final_norm_g)
```

```python
import numpy as np
import concourse.bass as bass
import concourse.mybir as mybir
from concourse.bass_utils import run_bass_kernel_spmd
from contextlib import ExitStack

F32 = mybir.dt.float32
BF16 = mybir.dt.bfloat16
AF = mybir.ActivationFunctionType
ALU = mybir.AluOpType

S = 4096
D = 1024
NT = S // 128
NB = S // 512
NIN = 5656
DFF = 2816
C_FQ, C_FK, C_FV, C_FF, C_SQ, C_SK, C_SV, C_Z, C_XBC, C_DT = 0, 512, 1024, 1536, 1544, 2056, 2568, 3080, 4104, 5640
EPS = 1e-6
NEG = -30000.0
PROF_HEADS = 8


class Buf:
    def __init__(self, name):
        self.name = name
        self.w = None
        self.r = {}
        self.slot = None
        self.persist = False


class EngS:
    def __init__(self, name, eng, sem, is_pe=False):
        self.name, self.eng, self.sem = name, eng, sem
        self.count = 0
        self.waited = {}
        self.is_pe = is_pe

    def wait(self, ev):
        if ev is None:
            return
        sem, val = ev
        if self.is_pe and sem is self.sem:
            return
        if self.waited.get(id(sem), 0) < val:
            self.eng.wait_ge(sem, val)
            self.waited[id(sem)] = val


class FW:
    def __init__(self, nc, es):
        self.nc, self.es = nc, es
        self.E = {}
        for name, eng, pe in (("pe", nc.tensor, True), ("act", nc.scalar, False),
                              ("dve", nc.vector, False), ("pool", nc.gpsimd, False),
                              ("sp", nc.sync, False)):
            sem = es.enter_context(nc.semaphore("s_" + name))
            self.E[name] = EngS(name, eng, sem, pe)
        self.nbuf = 0
        self.dbufs = []
        self.slots = []
        self.free_slots = []

    def buf(self, name=None, persist=False):
        self.nbuf += 1
        b = Buf((name or "b") + f"_{self.nbuf}")
        b.persist = persist
        return b

    def _pre(self, E, reads, writes):
        for b in reads:
            E.wait(b.w)
        for b in writes:
            E.wait(b.w)
            for ev in list(b.r.values()):
                E.wait(ev)

    def _post(self, ev, reads, writes):
        for b in reads:
            b.r[id(ev[0])] = ev
        for b in writes:
            b.w = ev
            b.r = {}

    def op(self, en, fns, reads=(), writes=(), signal=True):
        E = self.E[en]
        self._pre(E, reads, writes)
        if not isinstance(fns, (list, tuple)):
            fns = [fns]
        ins = None
        for f in fns:
            ins = f(E.eng)
        if signal:
            E.count += 1
            ins.then_inc(E.sem, 1)
            self._post((E.sem, E.count), reads, writes)
        else:
            self._post((E.sem, E.count + 1), reads, writes)

    def dma(self, qn, out, in_, reads=(), writes=(), **kw):
        Q = self.E[qn]
        self._pre(Q, reads, writes)
        d = writes[0]
        if d.slot is None:
            if self.free_slots:
                d.slot = self.free_slots.pop()
            else:
                d.slot = [self.es.enter_context(self.nc.semaphore(f"dq{len(self.slots)}")), 0]
                self.slots.append(d.slot)
            self.dbufs.append(d)
        d.slot[1] += 16
        Q.eng.dma_start(out=out, in_=in_, **kw).then_inc(d.slot[0], 16)
        self._post((d.slot[0], d.slot[1]), reads, writes)

    def barrier(self):
        evs = [(E.sem, E.count) for E in self.E.values() if E.count > 0]
        evs += [(sl[0], sl[1]) for sl in self.slots]
        for E in self.E.values():
            for ev in evs:
                E.wait(ev)
        keep = []
        for b in self.dbufs:
            if b.persist:
                keep.append(b)
            else:
                self.free_slots.append(b.slot)
                b.slot = None
        self.dbufs = keep

    def finish(self, bufs):
        E = self.E["sp"]
        for b in bufs:
            E.wait(b.w)
            for ev in list(b.r.values()):
                E.wait(ev)


class Rot:
    def __init__(self, fw, es, nc, name, shape, dt, n, psum=False):
        self.items = []
        for i in range(n):
            fw.nbuf += 1
            if psum:
                t = es.enter_context(nc.psum_tensor(f"{name}{i}_u{fw.nbuf}", shape, dt))
            else:
                t = es.enter_context(nc.sbuf_tensor(f"{name}{i}_u{fw.nbuf}", shape, dt))
            self.items.append((t, fw.buf(f"{name}{i}")))
        self.i = 0

    def next(self):
        it = self.items[self.i % len(self.items)]
        self.i += 1
        return it


def build(depth=2, stop_after=None, dbg=False, seq=4096):
    global S, NT, NB
    S, NT, NB = seq, seq // 128, seq // 512
    nc = bass.Bass("TRN2", target_bir_lowering=False)
    skind = "ExternalOutput" if dbg else "Internal"

    def din(name, shape):
        return nc.dram_tensor(name, list(shape), F32, kind="ExternalInput").ap()

    x_in = din("x", [S, D])
    mix_g = din("mix_norm_g", [depth, D])
    w_in = din("w_in", [depth, D, NIN])
    fox_fb = din("fox_f_bias", [depth, 8])
    fox_g = din("fox_out_g", [depth, 512])
    sb_g = din("sb_out_g", [depth, 512])
    conv_w = din("ssd_conv_w", [depth, 4, 1536])
    conv_b = din("ssd_conv_b", [depth, 1536])
    dt_bias = din("ssd_dt_bias", [depth, 16])
    a_log = din("ssd_a_log", [depth, 16])
    ssd_d = din("ssd_d", [depth, 16])
    ssd_ng = din("ssd_norm_g", [depth, 1024])
    w_out = din("w_out", [depth, 2048, D])
    ffn_g = din("ffn_norm_g", [depth, D])
    w_up = din("w_up", [depth, D, 2 * DFF])
    fconv_w = din("ffn_conv_w", [depth, 3, 2 * DFF])
    fconv_b = din("ffn_conv_b", [depth, 2 * DFF])
    w_down = din("w_down", [depth, DFF, D])
    fin_g = din("final_norm_g", [D])
    out = nc.dram_tensor("out", [S, D], F32, kind="ExternalOutput").ap()

    def scr(name, shape, dt):
        return nc.dram_tensor(name, list(shape), dt, kind=skind).ap()

    QF = scr("QF", [8, 70, S], BF16)
    KF = scr("KF", [8, 70, S], BF16)
    VF = scr("VF", [S, 8, 65], BF16)
    QS = scr("QS", [8, 64, S], BF16)
    KS = scr("KS", [8, 64, S], BF16)
    VS = scr("VS", [S, 512], BF16)
    ZS = scr("ZS", [S, 1024], F32)
    XS = scr("XS", [S, 1024], F32)
    DTS = scr("DTS", [S, 16], F32)
    BT = scr("BT", [2, 128, S], BF16)
    CT = scr("CT", [2, 128, S], BF16)
    BTOK = scr("BTOK", [S, 2, 128], BF16)
    YT = scr("YT", [2048, S], BF16)
    X1 = scr("X1", [S, D], F32)
    X2 = scr("X2", [S, D], F32)
    GTS = scr("GTS", [22, 128, S], BF16)

    with ExitStack() as es:
        fw = FW(nc, es)
        op, dma = fw.op, fw.dma

        def sb(st, name, shape, dt):
            fw.nbuf += 1
            return st.enter_context(nc.sbuf_tensor(f"{name}_u{fw.nbuf}", list(shape), dt))

        B_QF, B_KF, B_VF, B_QS, B_KS, B_VS = (fw.buf(n, True) for n in ("QF", "KF", "VF", "QS", "KS", "VS"))
        B_ZS, B_XS, B_DTS, B_BT, B_CT, B_BTOK = (fw.buf(n, True) for n in ("ZS", "XS", "DTS", "BT", "CT", "BTOK"))
        B_YT, B_X1, B_X2, B_OUT, B_GTS = (fw.buf(n, True) for n in ("YT", "X1", "X2", "OUT", "GTS"))

        PS = [es.enter_context(nc.psum_tensor(f"ps{i}", [128, 512], F32)) for i in range(8)]
        BPS = [fw.buf(f"ps{i}") for i in range(8)]

        class PRot:
            def __init__(self, idxs):
                self.idxs, self.i = idxs, 0

            def next(self):
                k = self.idxs[self.i % len(self.idxs)]
                self.i += 1
                return PS[k], BPS[k]

        identb = sb(es, "identb", [128, 128], BF16)
        identf = sb(es, "identf", [128, 128], F32)
        tle = sb(es, "tle", [128, 128], F32)
        onesf = sb(es, "onesf", [128, 128], F32)
        negi = sb(es, "negi", [128, 128], BF16)
        ustr = sb(es, "ustr", [128, 128], BF16)
        ntri = sb(es, "ntri", [128, 128], BF16)
        uge = sb(es, "uge", [128, 128], BF16)
        ult = sb(es, "ult", [128, 32, 32], BF16)
        sel = sb(es, "sel", [64, 32, 128], BF16)
        negrow = sb(es, "negrow", [1, 128], BF16)
        onescol = sb(es, "onescol", [128, 1], BF16)
        zrow = sb(es, "zrow", [1, 512], BF16)
        nwf = sb(es, "nwf", [65, 64], F32)
        onesb = sb(es, "onesb", [8, 512], BF16)
        B_C = fw.buf("consts")

        def mk(tile_ap, val, pattern=None, cm=None, cmp=None, eng="pool"):
            op(eng, lambda e: e.memset(tile_ap, val), writes=[B_C])
            if pattern is not None:
                op(eng, lambda e: e.affine_select(out=tile_ap, in_=tile_ap, pattern=pattern, compare_op=cmp,
                                                  fill=0.0, base=0, channel_multiplier=cm),
                   reads=[B_C], writes=[B_C])

        mk(identb[:], 1.0, [[1, 128]], -1, ALU.is_equal)
        mk(identf[:], 1.0, [[1, 128]], -1, ALU.is_equal)
        mk(tle[:], 1.0, [[1, 128]], -1, ALU.is_ge)
        mk(onesf[:], 1.0)
        mk(negi[:], NEG, [[1, 128]], -1, ALU.is_equal)
        mk(ustr[:], 1.0, [[-1, 128]], 1, ALU.is_gt)
        mk(ntri[:], -1.0, [[-1, 128]], 1, ALU.is_ge)
        mk(uge[:], 1.0, [[-1, 128]], 1, ALU.is_ge)
        mk(ult[:], 1.0, [[1, 32], [-1, 32]], 0, ALU.is_gt)
        mk(sel[0:32], -1.0, [[-1, 32], [0, 128]], 1, ALU.is_equal)
        mk(sel[32:64], -1.0, [[-1, 32], [0, 128]], 1, ALU.is_equal)
        mk(negrow[:], -1.0)
        mk(onescol[:], 1.0)
        mk(zrow[:], 0.0)
        mk(nwf[0:64, :], 1.0 / 64)
        mk(nwf[64:65, :], EPS)
        mk(onesb[:], 1.0)
        for i in range(3):
            for tb in range(NB):
                dma("sp", QF[:, 67 + i, tb * 512:(tb + 1) * 512], onesb[:], reads=[B_C], writes=[B_QF])
                dma("sp", KF[:, 64 + i, tb * 512:(tb + 1) * 512], onesb[:], reads=[B_C], writes=[B_KF])
        fw.barrier()

        def load_T(st, name, rows, C, blk, prot):
            R = sum(r.shape[0] for r in rows)
            nblk = C // blk
            stg = sb(st, name + "_stg", [R, C], F32)
            Bs = fw.buf(name + "_stg")
            r0 = 0
            for r in rows:
                dma("sp", stg[r0:r0 + r.shape[0], :], r, writes=[Bs])
                r0 += r.shape[0]
            outt = sb(st, name, [blk, nblk, R], F32)
            Bo = fw.buf(name)
            per = max(1, 512 // R)
            b0 = 0
            while b0 < nblk:
                nb_ = min(per, nblk - b0)
                ps, bps = prot.next()
                op("pe", [lambda e, j=j, b0=b0: e.transpose(ps[0:blk, (j - b0) * R:(j - b0 + 1) * R],
                                                          stg[0:R, j * blk:(j + 1) * blk], identf[0:R, 0:R])
                          for j in range(b0, b0 + nb_)], reads=[Bs, B_C], writes=[bps])
                op("dve", lambda e, b0=b0, nb_=nb_: e.tensor_copy(
                    outt[:, b0:b0 + nb_, :], ps[0:blk, 0:nb_ * R].rearrange("p (a r) -> p a r", r=R)),
                   reads=[bps], writes=[Bo])
                b0 += nb_
            return outt, Bo

        def load_bc(st, name, row_ap, n):
            t = sb(st, name, [128, n], F32)
            Bt = fw.buf(name)
            dma("sp", t[:], row_ap.to_broadcast([128, n]), writes=[Bt])
            return t, Bt

        def load_weight(st, name, w_ap, K, N, gT=None, Bg=None, piece=1024):
            kc = K // 128
            wt = sb(st, name, [128, kc, N], BF16)
            Bw = fw.buf(name)
            stg = Rot(fw, st, nc, name + "_s", [128, piece], F32, 3)
            cnt = 0
            for k in range(kc):
                c0 = 0
                while c0 < N:
                    cw = min(piece, N - c0)
                    t, Bt = stg.next()
                    dma("sp", t[:, 0:cw], w_ap[k * 128:(k + 1) * 128, c0:c0 + cw], writes=[Bt])
                    eng = ("dve", "act")[cnt % 2]
                    cnt += 1
                    if gT is None:
                        if eng == "act":
                            op(eng, lambda e, t=t, k=k, c0=c0, cw=cw: e.copy(wt[:, k, c0:c0 + cw], t[:, 0:cw]),
                               reads=[Bt], writes=[Bw])
                        else:
                            op(eng, lambda e, t=t, k=k, c0=c0, cw=cw: e.tensor_copy(wt[:, k, c0:c0 + cw], t[:, 0:cw]),
                               reads=[Bt], writes=[Bw])
                    elif eng == "act":
                        op(eng, lambda e, t=t, k=k, c0=c0, cw=cw: e.activation(
                            out=wt[:, k, c0:c0 + cw], in_=t[:, 0:cw], func=AF.Copy, scale=gT[:, k:k + 1]),
                           reads=[Bt, Bg], writes=[Bw])
                    else:
                        op(eng, lambda e, t=t, k=k, c0=c0, cw=cw: e.tensor_scalar(
                            out=wt[:, k, c0:c0 + cw], in0=t[:, 0:cw], scalar1=gT[:, k:k + 1], scalar2=None,
                            op0=ALU.mult), reads=[Bt, Bg], writes=[Bw])
                    c0 += cw
            return wt, Bw

        def rmsnorm_T(st_tiles, xt, Bx, ss, Bss, junk, Bj, hb, Bhb, hT, BhT, tt, prot):
            op("act", lambda e: e.activation(out=junk[:], in_=xt[:], func=AF.Square, accum_out=ss[:]),
               reads=[Bx], writes=[Bj, Bss])
            op("act", lambda e: e.activation(out=ss[:], in_=ss[:], func=AF.Ln, scale=1.0 / D, bias=EPS),
               reads=[Bss], writes=[Bss])
            op("act", lambda e: e.activation(out=ss[:], in_=ss[:], func=AF.Exp, scale=-0.5),
               reads=[Bss], writes=[Bss])
            op("dve", lambda e: e.tensor_scalar(out=hb[:], in0=xt[:], scalar1=ss[:, 0:1], scalar2=None, op0=ALU.mult),
               reads=[Bx, Bss], writes=[Bhb])
            ps, bps = prot.next()
            psb = ps[:].bitcast(BF16)
            op("pe", [lambda e, k=k: e.transpose(psb[:, k * 128:(k + 1) * 128], hb[:, k * 128:(k + 1) * 128], identb[:])
                      for k in range(8)], reads=[Bhb, B_C], writes=[bps])
            op("dve", lambda e: e.tensor_copy(hT[:, :, tt * 128:(tt + 1) * 128],
                                              psb.rearrange("p (k t) -> p k t", t=128)),
               reads=[bps], writes=[BhT])

        x_cur = x_in
        B_xcur = fw.buf("xin")
        for L in range(depth):
            last = (L == depth - 1)
            with ExitStack() as st:
                prot_t = PRot([6, 7])
                g1T, Bg1 = load_T(st, "g1T", [mix_g[L].rearrange("(a b) -> a b", b=128)], 128, 128, prot_t)
                cwT, Bcw = load_T(st, "cwT", [conv_w[L], conv_b[L:L + 1, :]], 1536, 128, prot_t)
                nfb = sb(st, "nfb", [8, 1], F32)
                Bnfb = fw.buf("nfb")
                dma("sp", nfb[:], fox_fb[L].rearrange("(a b) -> a b", b=1), writes=[Bnfb])
                op("dve", lambda e: e.tensor_scalar(out=nfb[:], in0=nfb[:], scalar1=-1.0, scalar2=None, op0=ALU.mult),
                   reads=[Bnfb], writes=[Bnfb])
                dtb, Bdtb = load_bc(st, "dtb", dt_bias[L:L + 1, :], 16)
                Wb, BW = load_weight(st, "Wb", w_in[L], D, NIN, gT=g1T[:, 0, :], Bg=Bg1, piece=707)

                xts = Rot(fw, st, nc, "xt", [128, D], F32, 2)
                junk = sb(st, "junk", [128, D], BF16); Bjunk = fw.buf("junk")
                sss = Rot(fw, st, nc, "ss", [128, 1], F32, 2)
                hbs = Rot(fw, st, nc, "hb", [128, D], BF16, 2)
                hTs = Rot(fw, st, nc, "hT", [128, 8, 512], BF16, 2)
                ev_b = Rot(fw, st, nc, "evb", [128, 512], BF16, 6)
                ev_va = Rot(fw, st, nc, "eva", [128, 8, 65], BF16, 2)
                for tva, Bva in ev_va.items:
                    op("dve", lambda e, tva=tva: e.memset(tva[:], 1.0), writes=[Bva])
                ev_z = Rot(fw, st, nc, "evz", [128, 1024], F32, 1)
                ev_dt = Rot(fw, st, nc, "evdt", [128, 16], F32, 2)
                Us = Rot(fw, st, nc, "U", [128, 515], F32, 2)
                accs = Rot(fw, st, nc, "acc", [128, 512], F32, 4)
                silf = Rot(fw, st, nc, "silf", [128, 512], F32, 3)
                xtok = Rot(fw, st, nc, "xtok", [128, 4, 128], F32, 2)
                btok = Rot(fw, st, nc, "btok", [128, 4, 128], BF16, 2)
                halo = sb(st, "halo", [128, 12, 3], F32); Bhalo = fw.buf("halo")
                op("dve", lambda e: e.memset(halo[:], 0.0), writes=[Bhalo])
                ffe = sb(st, "ffe", [8, 512], F32); Bffe = fw.buf("ffe")
                ones8 = sb(st, "ones8", [8, 512], F32); Bones8 = fw.buf("ones8")
                op("dve", lambda e: e.memset(ones8[:], 1.0), writes=[Bones8])
                CSs = Rot(fw, st, nc, "CSb", [8, 512], F32, 2)
                cks = Rot(fw, st, nc, "ck", [8, 3, 512], BF16, 1)
                cqs = Rot(fw, st, nc, "cq", [8, 3, 512], BF16, 1)
                carry = sb(st, "carry", [8, 1], F32); Bcarry = fw.buf("carry")
                prot_fm = PRot([0, 1, 2])
                prot_tm = PRot([3, 4, 5])

                def p1_norm(tb):
                    hT, BhT = hTs.next()
                    for tt in range(4):
                        ti = tb * 4 + tt
                        xt, Bx = xts.next()
                        dma("sp", xt[:], x_cur[ti * 128:(ti + 1) * 128, :], reads=[B_xcur], writes=[Bx])
                        ss, Bss = sss.next()
                        hb, Bhb = hbs.next()
                        rmsnorm_T(None, xt, Bx, ss, Bss, junk, Bjunk, hb, Bhb, hT, BhT, tt, prot_t)
                    return hT, BhT

                nxt = p1_norm(0)
                for tb in range(NB):
                    hT, BhT = nxt
                    if tb + 1 < NB:
                        nxt = p1_norm(tb + 1)
                    tsl = slice(tb * 512, (tb + 1) * 512)

                    def fm_mm(col0, M):
                        ps, bps = prot_fm.next()
                        op("pe", [lambda e, k=k: e.matmul(ps[0:M, :], lhsT=Wb[:, k, col0:col0 + M], rhs=hT[:, k, :],
                                                          start=(k == 0), stop=(k == 7)) for k in range(8)],
                           reads=[BW, BhT], writes=[bps])
                        return ps, bps

                    for (col, dst, Bdst, scale) in ((C_FQ, QF, B_QF, 0.125), (C_FK, KF, B_KF, 1.0),
                                                   (C_SQ, QS, B_QS, 0.125), (C_SK, KS, B_KS, 1.0)):
                        for j in range(4):
                            ps, bps = fm_mm(col + j * 128, 128)
                            t, Bt = ev_b.next()
                            op("act", lambda e, t=t, ps=ps, scale=scale: e.activation(out=t[:], in_=ps[:], func=AF.Identity,
                                                                                     scale=scale),
                               reads=[bps], writes=[Bt])
                            for hh in range(2):
                                dma("sp", dst[2 * j + hh, 0:64, tsl], t[hh * 64:(hh + 1) * 64, :], reads=[Bt], writes=[Bdst])
                    ps, bps = fm_mm(C_FF, 8)
                    op("act", lambda e, ps=ps: e.activation(out=ffe[:], in_=ps[0:8, :], func=AF.Exp, scale=-1.0,
                                                            bias=nfb[:, 0:1]), reads=[bps, Bnfb], writes=[Bffe])
                    op("act", lambda e: e.activation(out=ffe[:], in_=ffe[:], func=AF.Ln, bias=1.0),
                       reads=[Bffe], writes=[Bffe])
                    CSb, BCS = CSs.next()
                    init = 0.0 if tb == 0 else carry[:, 0:1]
                    op("dve", lambda e, init=init, CSb=CSb: e.tensor_tensor_scan(
                        out=CSb[:], data0=ones8[:], data1=ffe[:], initial=init, op0=ALU.mult, op1=ALU.add),
                       reads=[Bffe, Bones8, Bcarry], writes=[BCS])
                    op("dve", lambda e, CSb=CSb: e.tensor_copy(carry[:], CSb[:, 511:512]), reads=[BCS], writes=[Bcarry])
                    ck, Bck = cks.next()
                    cq, Bcq = cqs.next()
                    for i in range(3):
                        op("dve", lambda e, i=i, ck=ck, CSb=CSb: e.tensor_copy(ck[:, i, :], CSb[:]), reads=[BCS], writes=[Bck])
                        if i < 2:
                            op("dve", lambda e, i=i, ck=ck, CSb=CSb: e.tensor_tensor(out=CSb[:], in0=CSb[:], in1=ck[:, i, :],
                                                                                   op=ALU.subtract),
                               reads=[BCS, Bck], writes=[BCS])
                    op("dve", lambda e, ck=ck, cq=cq: e.tensor_scalar(out=cq[:], in0=ck[:], scalar1=-1.0, scalar2=None,
                                                                    op0=ALU.mult), reads=[Bck], writes=[Bcq])
                    for i in range(3):
                        dma("sp", QF[:, 64 + i, tsl], cq[:, i, :], reads=[Bcq], writes=[B_QF])
                        dma("sp", KF[:, 67 + i, tsl], ck[:, i, :], reads=[Bck], writes=[B_KF])
                    pend = []

                    def conv_tail(cc, acc, Bacc):
                        if cc < 8:
                            sf, Bsf = silf.next()
                            op("act", lambda e: e.activation(out=sf[:], in_=acc[:], func=AF.Silu), reads=[Bacc], writes=[Bsf])

                            def t2():
                                ps2, bps2 = prot_t.next()
                                op("pe", [lambda e, q=q: e.transpose(ps2[:, q * 128:(q + 1) * 128], sf[:, q * 128:(q + 1) * 128], identf[:])
                                          for q in range(4)], reads=[Bsf, B_C], writes=[bps2])
                                xk, Bxk = xtok.next()
                                op("dve", lambda e: e.tensor_copy(xk[:], ps2[:].rearrange("p (q c) -> p q c", c=128)),
                                   reads=[bps2], writes=[Bxk])
                                dma("sp", XS[tsl, cc * 128:(cc + 1) * 128].rearrange("(q p) c -> p q c", p=128), xk[:],
                                    reads=[Bxk], writes=[B_XS])
                            return t2
                        t, Bt = ev_b.next()
                        op("act", lambda e: e.activation(out=t[:], in_=acc[:], func=AF.Silu), reads=[Bacc], writes=[Bt])
                        g = (cc - 8) % 2
                        if cc >= 10:
                            dma("sp", CT[g, :, tsl], t[:], reads=[Bt], writes=[B_CT])
                            return None
                        dma("sp", BT[g, :, tsl], t[:], reads=[Bt], writes=[B_BT])

                        def t2():
                            ps2, bps2 = prot_t.next()
                            ps2b = ps2[:].bitcast(BF16)
                            op("pe", [lambda e, q=q: e.transpose(ps2b[:, q * 128:(q + 1) * 128], t[:, q * 128:(q + 1) * 128], identb[:])
                                      for q in range(4)], reads=[Bt, B_C], writes=[bps2])
                            bk, Bbk = btok.next()
                            op("dve", lambda e: e.tensor_copy(bk[:], ps2b[:, 0:512].rearrange("p (q c) -> p q c", c=128)),
                               reads=[bps2], writes=[Bbk])
                            dma("sp", BTOK[tsl, g, :].rearrange("(q p) c -> p q c", p=128), bk[:], reads=[Bbk], writes=[B_BTOK])
                        return t2

                    def conv_tick():
                        todo = [p for p in pend if p[0] <= 0]
                        for p in todo:
                            pend.remove(p)
                        for p in pend:
                            p[0] -= 1
                        for p in todo:
                            r = p[1]()
                            if r is not None:
                                pend.append([0, r])

                    for cc in range(12):
                        ps, bps = fm_mm(C_XBC + cc * 128, 128)
                        U, BU = Us.next()
                        op("act", lambda e, U=U, ps=ps: e.copy(U[:, 3:515], ps[:]), reads=[bps], writes=[BU])
                        op("act", lambda e, U=U, cc=cc: e.copy(U[:, 0:3], halo[:, cc, :]), reads=[Bhalo], writes=[BU])
                        op("act", lambda e, U=U, cc=cc: e.copy(halo[:, cc, :], U[:, 512:515]), reads=[BU], writes=[Bhalo])
                        acc, Bacc = accs.next()
                        op("act", lambda e, ps=ps, acc=acc, cc=cc: e.activation(
                            out=acc[:], in_=ps[:], func=AF.Identity, scale=cwT[:, cc, 3:4], bias=cwT[:, cc, 4:5]),
                           reads=[bps, Bcw], writes=[Bacc])
                        for kk in (2, 1, 0):
                            op("dve", lambda e, U=U, acc=acc, cc=cc, kk=kk: e.scalar_tensor_tensor(
                                out=acc[:], in0=U[:, kk:kk + 512], scalar=cwT[:, cc, kk:kk + 1], in1=acc[:],
                                op0=ALU.mult, op1=ALU.add), reads=[BU, Bcw, Bacc], writes=[Bacc])
                        conv_tick()
                        pend.append([0, lambda cc=cc, acc=acc, Bacc=Bacc: conv_tail(cc, acc, Bacc)])
                    while pend:
                        conv_tick()
                    for tt in range(4):
                        ti = tb * 4 + tt
                        rows = slice(ti * 128, (ti + 1) * 128)

                        def tm_mm(col0, N):
                            ps, bps = prot_tm.next()
                            op("pe", [lambda e, k=k: e.matmul(ps[:, 0:N], lhsT=hT[:, k, tt * 128:(tt + 1) * 128],
                                                              rhs=Wb[:, k, col0:col0 + N], start=(k == 0), stop=(k == 7))
                                      for k in range(8)], reads=[BW, BhT], writes=[bps])
                            return ps, bps

                        ps, bps = tm_mm(C_FV, 512)
                        va, Bva = ev_va.next()
                        op("dve", lambda e, va=va, ps=ps: e.tensor_copy(va[:, :, 0:64],
                                                                         ps[:].rearrange("p (h c) -> p h c", c=64)),
                           reads=[bps], writes=[Bva])
                        dma("sp", VF[rows, :, :], va[:], reads=[Bva], writes=[B_VF])
                        ps, bps = tm_mm(C_SV, 512)
                        t, Bt = ev_b.next()
                        op("act", lambda e, t=t, ps=ps: e.copy(t[:], ps[:]), reads=[bps], writes=[Bt])
                        dma("sp", VS[rows, :], t[:], reads=[Bt], writes=[B_VS])
                        zt, Bzt = ev_z.next()
                        for hh in range(2):
                            ps, bps = tm_mm(C_Z + hh * 512, 512)
                            op("act", lambda e, zt=zt, ps=ps, hh=hh: e.activation(out=zt[:, hh * 512:(hh + 1) * 512], in_=ps[:],
                                                                                 func=AF.Silu), reads=[bps], writes=[Bzt])
                        dma("sp", ZS[rows, :], zt[:], reads=[Bzt], writes=[B_ZS])
                        ps, bps = tm_mm(C_DT, 16)
                        dtt, Bdtt = ev_dt.next()
                        op("dve", lambda e, dtt=dtt, ps=ps: e.tensor_tensor(out=dtt[:], in0=ps[:, 0:16], in1=dtb[:], op=ALU.add),
                           reads=[bps, Bdtb], writes=[Bdtt])
                        dma("sp", DTS[rows, :], dtt[:], reads=[Bdtt], writes=[B_DTS])
                fw.barrier()
            if stop_after == "P1":
                break
            with ExitStack() as st:
                prot_t = PRot([6, 7])
                gfx, Bgfx = load_T(st, "gfx", [fox_g[L].rearrange("(h c) -> h c", c=64)], 64, 64, prot_t)
                KAs = Rot(fw, st, nc, "KA", [70, S], BF16, 2)
                QAs = Rot(fw, st, nc, "QA", [70, 512], BF16, 3)
                VAs = Rot(fw, st, nc, "VA", [128, NT, 65], BF16, 2)
                Pbs = Rot(fw, st, nc, "Pb", [128, 512], BF16, 4)
                sqs = Rot(fw, st, nc, "sq", [65, 512], F32, 2)
                rss = Rot(fw, st, nc, "rs", [64, 512], F32, 2)
                yos = Rot(fw, st, nc, "yo", [64, 512], BF16, 2)
                osbs = Rot(fw, st, nc, "osb", [64, 512], F32, 2)
                prot_s = PRot([0, 1, 2])
                prot_o = PRot([3, 4])
                prot_n = PRot([5])
                groups = [(h, qb) for h in range(PROF_HEADS) for qb in range(NB)]
                G = {}

                def f_loads(gi):
                    if gi >= len(groups):
                        return
                    h, qb = groups[gi]
                    d = {"h": h, "qb": qb}
                    if qb == 0:
                        KA, BKA = KAs.next()
                        dma("sp", KA[:], KF[h], reads=[B_KF], writes=[BKA])
                        VA, BVA = VAs.next()
                        for q4 in range(4):
                            k0, k1 = q4 * NT // 4, (q4 + 1) * NT // 4
                            dma("sp", VA[:, k0:k1, :], VF[k0 * 128:k1 * 128, h, :].rearrange("(kb p) c -> p kb c", p=128),
                                reads=[B_VF], writes=[BVA])
                        d.update(KA=KA, BKA=BKA, VA=VA, BVA=BVA)
                    else:
                        p = G[gi - 1]
                        d.update(KA=p["KA"], BKA=p["BKA"], VA=p["VA"], BVA=p["BVA"])
                    QA, BQA = QAs.next()
                    dma("sp", QA[:], QF[h, :, qb * 512:(qb + 1) * 512], reads=[B_QF], writes=[BQA])
                    d.update(QA=QA, BQA=BQA)
                    G[gi] = d

                blocks = []
                for gi, (h, qb) in enumerate(groups):
                    nkb = 4 * qb + 4
                    for kb in range(nkb):
                        blocks.append(dict(gi=gi, kb=kb, first=(kb == 0), last=(kb == nkb - 1), c0=max(0, kb - 4 * qb) * 128,
                                           diag=(kb - 4 * qb >= 0)))
                deferred = []

                def defer(k, fn):
                    deferred.append([k, fn])

                def run_deferred():
                    todo = [d for d in deferred if d[0] <= 0]
                    for d in todo:
                        deferred.remove(d)
                    for d in deferred:
                        d[0] -= 1
                    for d in todo:
                        d[1]()

                def f_A(b):
                    g = G[b["gi"]]
                    if b["first"]:
                        f_loads(b["gi"] + 2)
                        g["O"], g["BO"] = prot_o.next()
                    Sp, BS = prot_s.next()
                    b["Sp"], b["BS"] = Sp, BS
                    c0, kb = b["c0"], b["kb"]
                    fns = [lambda e: e.matmul(Sp[:, c0:512], lhsT=g["KA"][:, kb * 128:(kb + 1) * 128], rhs=g["QA"][:, c0:512],
                                              start=True, stop=not b["diag"])]
                    if b["diag"]:
                        fns.append(lambda e: e.matmul(Sp[:, c0:c0 + 128], lhsT=negi[:], rhs=ustr[:], start=False, stop=True))
                    op("pe", fns, reads=[g["BKA"], g["BQA"], B_C], writes=[BS])

                def f_B(b):
                    Sp, BS, c0 = b["Sp"], b["BS"], b["c0"]
                    Pb, BP = Pbs.next()
                    b["Pb"], b["BP"] = Pb, BP
                    op("act", lambda e: e.activation(out=Pb[:, c0:512], in_=Sp[:, c0:512], func=AF.Exp), reads=[BS], writes=[BP])

                def f_F(b):
                    g = G[b["gi"]]
                    O, BO, c0, kb = g["O"], g["BO"], b["c0"], b["kb"]
                    Pb, BP = b["Pb"], b["BP"]
                    op("pe", lambda e: e.matmul(O[0:65, c0:512], lhsT=g["VA"][:, kb, :], rhs=Pb[:, c0:512], start=b["first"],
                                                stop=b["last"]), reads=[g["BVA"], BP], writes=[BO], signal=b["last"])
                    if b["last"]:
                        h, qb = g["h"], g["qb"]
                        sq, Bsq = sqs.next()
                        Np, BN = prot_n.next()
                        rs, Brs = rss.next()
                        yo, Byo = yos.next()
                        op("act", lambda e: e.activation(out=sq[:], in_=O[0:65, :], func=AF.Square), reads=[BO], writes=[Bsq])
                        osb, Bosb = osbs.next()
                        op("act", lambda e: e.copy(osb[:], O[0:64, :]), reads=[BO], writes=[Bosb])

                        def e2():
                            op("pe", lambda e: e.matmul(Np[0:64, :], lhsT=nwf[0:65, :], rhs=sq[0:65, :], start=True, stop=True),
                               reads=[Bsq, B_C], writes=[BN])

                        def e3():
                            op("act", lambda e: e.activation(out=rs[:], in_=Np[0:64, :], func=AF.Ln), reads=[BN], writes=[Brs])
                            op("act", lambda e: e.activation(out=rs[:], in_=rs[:], func=AF.Exp, scale=-0.5), reads=[Brs], writes=[Brs])

                        def e4():
                            op("dve", lambda e: e.scalar_tensor_tensor(out=yo[:], in0=osb[:], scalar=gfx[:, 0, h:h + 1], in1=rs[:],
                                                                       op0=ALU.mult, op1=ALU.mult), reads=[Bosb, Brs, Bgfx], writes=[Byo])
                            dma("sp", YT[h * 64:(h + 1) * 64, qb * 512:(qb + 1) * 512], yo[:], reads=[Byo], writes=[B_YT])
                        defer(0, e2)
                        defer(1, e3)
                        defer(2, e4)

                f_loads(0)
                f_loads(1)
                nblk = len(blocks)
                f_A(blocks[0])
                for i in range(nblk + 4):
                    if i + 1 < nblk:
                        f_A(blocks[i + 1])
                    if i < nblk:
                        f_B(blocks[i])
                    run_deferred()
                    if 1 <= i <= nblk:
                        f_F(blocks[i - 1])
                while deferred:
                    run_deferred()
                fw.barrier()
            if stop_after == "P2":
                break
            with ExitStack() as st:
                prot_t = PRot([6, 7])
                gsb, Bgsb = load_T(st, "gsb", [sb_g[L].rearrange("(h c) -> h c", c=64)], 64, 64, prot_t)
                KAs = Rot(fw, st, nc, "KAs", [128, S], BF16, 2)
                QAs = Rot(fw, st, nc, "QAs", [128, 512], BF16, 4)
                for KA_, BKA_ in KAs.items:
                    op("dve", lambda e, KA_=KA_: e.tensor_copy(KA_[0:64, :], sel[:, 0:NT, :].rearrange("k a s -> k (a s)")),
                       reads=[B_C], writes=[BKA_])
                VAs = Rot(fw, st, nc, "VAs", [128, NT, 64], BF16, 2)
                Efs = Rot(fw, st, nc, "Ef", [128, 512], F32, 2)
                SPAs = [[(sb(st, f"SPA{r}_{k}", [128, 512], BF16), fw.buf(f"SPA{r}_{k}")) for k in range(NT)] for r in range(3)]
                Wbs = Rot(fw, st, nc, "Wsb", [128, 512], BF16, 4)
                rtmp = Rot(fw, st, nc, "rtmp", [32, 512], F32, 2)
                sqs = Rot(fw, st, nc, "sqs", [64, 512], F32, 2)
                rss = Rot(fw, st, nc, "rss", [64, 512], F32, 2)
                yos = Rot(fw, st, nc, "yos", [64, 512], BF16, 2)
                osbs = Rot(fw, st, nc, "osbs", [64, 512], F32, 2)
                prot_z = PRot([0, 1, 2])
                prot_o = PRot([3, 4])
                prot_r = PRot([5, 6])
                prot_n = PRot([7])
                groups = [(h, qb) for h in range(PROF_HEADS) for qb in range(NB)]
                G = {}

                def s_loads(gi):
                    if gi >= len(groups):
                        return
                    h, qb = groups[gi]
                    d = {"h": h, "qb": qb, "nkb": 4 * qb + 4, "spa": SPAs[gi % 3]}
                    if qb == 0:
                        KA, BKA = KAs.next()
                        dma("sp", KA[64:128, :], KS[h], reads=[B_KS], writes=[BKA])
                        VA, BVA = VAs.next()
                        for q4 in range(4):
                            k0, k1 = q4 * NT // 4, (q4 + 1) * NT // 4
                            dma("sp", VA[:, k0:k1, :], VS[k0 * 128:k1 * 128, h * 64:(h + 1) * 64].rearrange("(kb p) c -> p kb c", p=128),
                                reads=[B_VS], writes=[BVA])
                        d.update(KA=KA, BKA=BKA, VA=VA, BVA=BVA)
                    else:
                        p = G[gi - 1]
                        d.update(KA=p["KA"], BKA=p["BKA"], VA=p["VA"], BVA=p["BVA"])
                    QA, BQA = QAs.next()
                    dma("sp", QA[64:128, :], QS[h, :, qb * 512:(qb + 1) * 512], reads=[B_QS], writes=[BQA])
                    d.update(QA=QA, BQA=BQA)
                    G[gi] = d

                def mk_blocks(gi, ps):
                    h, qb = groups[gi]
                    nkb = 4 * qb + 4
                    return [dict(ps=ps, gi=gi, kb=kb, first=(kb == 0), last=(kb == nkb - 1), c0=max(0, kb - 4 * qb) * 128,
                                 diag=(kb - 4 * qb >= 0)) for kb in range(nkb)]

                def merge(a, b):
                    out_, i, j = [], 0, 0
                    while i < len(a) or j < len(b):
                        if i < len(a) and (j >= len(b) or i * len(b) <= j * len(a)):
                            out_.append(a[i]); i += 1
                        else:
                            out_.append(b[j]); j += 1
                    return out_

                tasks = mk_blocks(0, 1)
                for gi in range(len(groups)):
                    nxt1 = mk_blocks(gi + 1, 1) if gi + 1 < len(groups) else [dict(ps=0), dict(ps=0)]
                    tasks += nxt1[:2] + merge(nxt1[2:], mk_blocks(gi, 2))
                deferred = []

                def defer(k, fn):
                    deferred.append([k, fn])

                def run_deferred():
                    todo = [d for d in deferred if d[0] <= 0]
                    for d in todo:
                        deferred.remove(d)
                    for d in deferred:
                        d[0] -= 1
                    for d in todo:
                        d[1]()

                def s_A(b):
                    if b["ps"] == 0:
                        return
                    g = G[b["gi"]]
                    c0, kb = b["c0"], b["kb"]
                    if b["ps"] == 1 and b["first"]:
                        s_loads(b["gi"] + 2)
                        g["RP"], g["BRP"] = prot_r.next()
                    if b["ps"] == 2 and b["first"]:
                        g["O"], g["BO"] = prot_o.next()
                    Zp, BZ = prot_z.next()
                    b["Zp"], b["BZ"] = Zp, BZ
                    r0 = 64 if b["ps"] == 1 else 0
                    fns = [lambda e: e.matmul(Zp[:, c0:512], lhsT=g["KA"][r0:128, kb * 128:(kb + 1) * 128], rhs=g["QA"][r0:128, c0:512],
                                              start=True, stop=True)]
                    if b["diag"]:
                        fns.append(lambda e: e.matmul(Zp[:, c0:c0 + 128], lhsT=negi[:], rhs=uge[:], start=False, stop=True,
                                                      skip_group_check=True))
                    reads = [g["BKA"], g["BQA"], B_C]
                    if b["ps"] == 2:
                        SPb, BSP = g["spa"][kb]
                        fns.append(lambda e: e.matmul(Zp[:, c0:512], lhsT=ntri[:], rhs=SPb[:, c0:512], start=False, stop=True,
                                                      skip_group_check=True))
                        reads += [BSP]
                    op("pe", fns, reads=reads, writes=[BZ])

                def s_B(b):
                    if b["ps"] == 0:
                        return
                    g = G[b["gi"]]
                    Zp, BZ, c0, kb = b["Zp"], b["BZ"], b["c0"], b["kb"]
                    if b["ps"] == 1:
                        Ef, BEf = Efs.next()
                        SPb, BSP = g["spa"][kb]
                        op("act", lambda e: e.activation(out=Ef[:, c0:512], in_=Zp[:, c0:512], func=AF.Exp), reads=[BZ], writes=[BEf])
                        op("act", lambda e: e.activation(out=SPb[:, c0:512], in_=Ef[:, c0:512], func=AF.Ln, bias=1.0),
                           reads=[BEf], writes=[BSP])
                    else:
                        Wt, BWt = Wbs.next()
                        b["Wt"], b["BWt"] = Wt, BWt
                        op("act", lambda e: e.activation(out=Wt[:, c0:512], in_=Zp[:, c0:512], func=AF.Exp), reads=[BZ], writes=[BWt])

                def s_C(b):
                    if b["ps"] == 0:
                        return
                    g = G[b["gi"]]
                    c0, kb = b["c0"], b["kb"]
                    if b["ps"] == 1:
                        RP, BRP = g["RP"], g["BRP"]
                        SPb, BSP = g["spa"][kb]
                        op("pe", lambda e: e.matmul(RP[0:32, c0:512], lhsT=ult[:, kb, :], rhs=SPb[:, c0:512], start=b["first"], stop=True,
                                                    skip_group_check=True), reads=[BSP, B_C], writes=[BRP], signal=b["last"])
                        if b["last"]:
                            rt, Brt = rtmp.next()
                            QA, BQA = g["QA"], g["BQA"]
                            op("dve", lambda e: e.tensor_copy(QA[0:32, :], RP[0:32, :]), reads=[BRP], writes=[BQA])
                            op("dve", lambda e: e.tensor_tensor(out=rt[:], in0=RP[0:32, :], in1=QA[0:32, :], op=ALU.subtract),
                               reads=[BRP, BQA], writes=[Brt])
                            op("dve", lambda e: e.tensor_copy(QA[32:64, :], rt[:]), reads=[Brt], writes=[BQA])
                    else:
                        O, BO = g["O"], g["BO"]
                        Wt, BWt = b["Wt"], b["BWt"]
                        op("pe", lambda e: e.matmul(O[0:64, c0:512], lhsT=g["VA"][:, kb, :], rhs=Wt[:, c0:512], start=b["first"], stop=True,
                                                    skip_group_check=True), reads=[g["BVA"], BWt], writes=[BO], signal=b["last"])
                        if b["last"]:
                            h, qb = g["h"], g["qb"]
                            sq, Bsq = sqs.next()
                            Np, BN = prot_n.next()
                            rs, Brs = rss.next()
                            yo, Byo = yos.next()
                            osb, Bosb = osbs.next()
                            op("act", lambda e: e.activation(out=sq[:], in_=O[0:64, :], func=AF.Square), reads=[BO], writes=[Bsq])
                            op("act", lambda e: e.copy(osb[:], O[0:64, :]), reads=[BO], writes=[Bosb])

                            def e2():
                                op("pe", lambda e: e.matmul(Np[0:64, :], lhsT=nwf[0:64, :], rhs=sq[0:64, :], start=True, stop=True),
                                   reads=[Bsq, B_C], writes=[BN])

                            def e3():
                                op("act", lambda e: e.activation(out=rs[:], in_=Np[0:64, :], func=AF.Ln, bias=EPS), reads=[BN], writes=[Brs])
                                op("act", lambda e: e.activation(out=rs[:], in_=rs[:], func=AF.Exp, scale=-0.5), reads=[Brs], writes=[Brs])

                            def e4():
                                op("dve", lambda e: e.scalar_tensor_tensor(out=yo[:], in0=osb[:], scalar=gsb[:, 0, h:h + 1], in1=rs[:],
                                                                           op0=ALU.mult, op1=ALU.mult), reads=[Bosb, Brs, Bgsb], writes=[Byo])
                                dma("sp", YT[512 + h * 64:512 + (h + 1) * 64, qb * 512:(qb + 1) * 512], yo[:], reads=[Byo], writes=[B_YT])
                            defer(0, e2)
                            defer(1, e3)
                            defer(2, e4)

                s_loads(0)
                s_loads(1)
                nt_ = len(tasks)
                s_A(tasks[0])
                for i in range(nt_ + 1):
                    if 1 <= i <= nt_:
                        s_C(tasks[i - 1])
                    if i + 1 < nt_:
                        s_A(tasks[i + 1])
                    if i < nt_:
                        s_B(tasks[i])
                    run_deferred()
                for _ in range(5):
                    run_deferred()
                fw.barrier()
            if stop_after == "P3":
                break
            with ExitStack() as st:
                abc, Babc = load_bc(st, "abc", a_log[L:L + 1, :], 16)
                op("act", lambda e: e.activation(out=abc[:], in_=abc[:], func=AF.Exp), reads=[Babc], writes=[Babc])
                op("dve", lambda e: e.tensor_scalar(out=abc[:], in0=abc[:], scalar1=-1.0, scalar2=None, op0=ALU.mult),
                   reads=[Babc], writes=[Babc])
                dbc, Bdbc = load_bc(st, "dbc", ssd_d[L:L + 1, :], 16)
                Dfull = sb(st, "Dfull", [128, 1024], F32); BDf = fw.buf("Dfull")
                op("dve", lambda e: e.tensor_copy(Dfull[:].rearrange("p (h c) -> p h c", c=64),
                                                   dbc[:].unsqueeze(2).to_broadcast([128, 16, 64])), reads=[Bdbc], writes=[BDf])
                NG, BNG = load_bc(st, "NG", ssd_ng[L:L + 1, :], 1024)
                prev = sb(st, "prev", [128, 2, 512], F32); Bprev = fw.buf("prev")
                prevb = sb(st, "prevb", [128, 2, 512], BF16); Bprevb = fw.buf("prevb")
                op("dve", lambda e: e.memset(prev[:], 0.0), writes=[Bprev])
                op("dve", lambda e: e.memset(prevb[:], 0.0), writes=[Bprevb])
                R3 = 3
                xss = Rot(fw, st, nc, "xs", [128, 1024], F32, R3)
                zss = Rot(fw, st, nc, "zs", [128, 1024], F32, R3)
                dtrs = Rot(fw, st, nc, "dtr", [128, 16], F32, R3)
                bts = Rot(fw, st, nc, "bt", [128, 2, 128], BF16, R3)
                cts = Rot(fw, st, nc, "ct", [128, 2, 128], BF16, R3)
                btks = Rot(fw, st, nc, "btk", [128, 2, 128], BF16, R3)
                dts = Rot(fw, st, nc, "dt", [128, 16], F32, R3)
                pass_ = Rot(fw, st, nc, "pas", [128, 32], F32, R3)
                dAs = Rot(fw, st, nc, "dA", [128, 16], F32, R3)
                dAbs = Rot(fw, st, nc, "dAb", [128, 16, 128], F32, R3)
                nacss = Rot(fw, st, nc, "nacs", [128, 16], F32, R3)
                eats = Rot(fw, st, nc, "eat", [128, 16], F32, R3)
                decs = Rot(fw, st, nc, "dec", [128, 16], F32, R3)
                des = Rot(fw, st, nc, "de", [128, 16], F32, R3)
                xcs = Rot(fw, st, nc, "xc", [128, 1024], BF16, R3)
                xds = Rot(fw, st, nc, "xd", [128, 1024], BF16, R3)
                cbs = Rot(fw, st, nc, "cb", [128, 128], F32, 2)
                segs = Rot(fw, st, nc, "seg", [128, 128], F32, 4)
                Mts = Rot(fw, st, nc, "Mt", [128, 4, 128], BF16, 2)
                yds = Rot(fw, st, nc, "ydsb", [128, 512], F32, 4)
                t1s = Rot(fw, st, nc, "t1", [128, 512], F32, 4)
                t2s = Rot(fw, st, nc, "t2", [128, 512], F32, 2)
                ssn = Rot(fw, st, nc, "ssn", [128, 1], F32, 2)
                jk = sb(st, "jk", [128, 512], BF16); Bjk = fw.buf("jk")
                yns = Rot(fw, st, nc, "yn", [128, 512], BF16, 4)
                yts = Rot(fw, st, nc, "ytt", [128, 4, 128], BF16, 2)
                prot_a = PRot([0, 1])
                prot_R = PRot([2, 3])
                prot_yd = PRot([4, 5])
                prot_yo = PRot([6])
                prot_st = PRot([7])
                CH = {}

                def ssd_A(c):
                    rows = slice(c * 128, (c + 1) * 128)
                    xs, Bxs = xss.next(); zs, Bzs = zss.next(); dtr, Bdtr = dtrs.next()
                    bt, Bbt = bts.next(); ct, Bct = cts.next(); btk, Bbtk = btks.next()
                    dt, Bdt = dts.next(); dA, BdA = dAs.next(); dAb, BdAb = dAbs.next()
                    pas, Bpas = pass_.next(); nacs, Bnacs = nacss.next(); eat, Beat = eats.next()
                    dec, Bdec = decs.next(); de, Bde = des.next(); xc, Bxc = xcs.next(); xd, Bxd = xds.next()
                    CH[c] = dict(rows=rows, xs=xs, Bxs=Bxs, zs=zs, Bzs=Bzs, bt=bt, Bbt=Bbt, ct=ct, Bct=Bct, btk=btk, Bbtk=Bbtk,
                                 dAb=dAb, BdAb=BdAb, nacs=nacs, Bnacs=Bnacs, eat=eat, Beat=Beat, dec=dec, Bdec=Bdec,
                                 xc=xc, Bxc=Bxc, xd=xd, Bxd=Bxd, yd=[None, None])
                    st_ = {}

                    def a0():
                        dma("sp", xs[:], XS[rows, :], reads=[B_XS], writes=[Bxs])
                        dma("sp", zs[:], ZS[rows, :], reads=[B_ZS], writes=[Bzs])
                        dma("sp", dtr[:], DTS[rows, :], reads=[B_DTS], writes=[Bdtr])
                        dma("sp", bt[:], BT[:, :, rows].rearrange("g n t -> n g t"), reads=[B_BT], writes=[Bbt])
                        dma("sp", ct[:], CT[:, :, rows].rearrange("g n t -> n g t"), reads=[B_CT], writes=[Bct])
                        dma("sp", btk[:], BTOK[rows, :, :], reads=[B_BTOK], writes=[Bbtk])
                        op("act", lambda e: e.activation(out=dt[:], in_=dtr[:], func=AF.Exp), reads=[Bdtr], writes=[Bdt])
                        op("act", lambda e: e.activation(out=dt[:], in_=dt[:], func=AF.Ln, bias=1.0), reads=[Bdt], writes=[Bdt])

                    def a1():
                        op("dve", lambda e: e.tensor_tensor(out=dA[:], in0=dt[:], in1=abc[:], op=ALU.mult), reads=[Bdt, Babc], writes=[BdA])
                        op("dve", lambda e: e.tensor_copy(dAb[:], dA[:].unsqueeze(2).to_broadcast([128, 16, 128])),
                           reads=[BdA], writes=[BdAb])

                    def a2():
                        pa, Bpa = prot_a.next()
                        st_["pa"], st_["Bpa"] = pa, Bpa
                        op("pe", [lambda e: e.matmul(pa[:, 0:16], lhsT=tle[:], rhs=dA[:], start=True, stop=True),
                                  lambda e: e.matmul(pa[:, 16:32], lhsT=onesf[:], rhs=dA[:], start=True, stop=True)],
                           reads=[BdA, B_C], writes=[Bpa])

                    def a3():
                        pa, Bpa = st_["pa"], st_["Bpa"]
                        op("act", lambda e: e.copy(pas[:], pa[:, 0:32]), reads=[Bpa], writes=[Bpas])

                    def a4():
                        op("dve", lambda e: e.tensor_scalar(out=nacs[:], in0=pas[:, 0:16], scalar1=-1.0, scalar2=None, op0=ALU.mult),
                           reads=[Bpas], writes=[Bnacs])
                        op("dve", lambda e: e.tensor_tensor(out=de[:], in0=pas[:, 16:32], in1=nacs[:], op=ALU.add),
                           reads=[Bpas, Bnacs], writes=[Bde])

                    def a5():
                        op("act", lambda e: e.activation(out=eat[:], in_=pas[:, 0:16], func=AF.Exp), reads=[Bpas], writes=[Beat])
                        op("act", lambda e: e.activation(out=dec[:], in_=pas[:, 16:32], func=AF.Exp), reads=[Bpas], writes=[Bdec])
                        op("act", lambda e: e.activation(out=de[:], in_=de[:], func=AF.Exp), reads=[Bde], writes=[Bde])

                    def a6():
                        op("dve", lambda e: e.tensor_tensor(out=de[:], in0=de[:], in1=dt[:], op=ALU.mult), reads=[Bde, Bdt], writes=[Bde])
                        op("dve", lambda e: e.tensor_tensor(
                            out=xc[:].rearrange("p (h c) -> p h c", c=64), in0=xs[:].rearrange("p (h c) -> p h c", c=64),
                            in1=dt[:].unsqueeze(2).to_broadcast([128, 16, 64]), op=ALU.mult), reads=[Bxs, Bdt], writes=[Bxc])
                        op("dve", lambda e: e.tensor_tensor(
                            out=xd[:].rearrange("p (h c) -> p h c", c=64), in0=xs[:].rearrange("p (h c) -> p h c", c=64),
                            in1=de[:].unsqueeze(2).to_broadcast([128, 16, 64]), op=ALU.mult), reads=[Bxs, Bde], writes=[Bxd])
                    return [a0, a1, a2, a3, a4, a5, a6]

                a_steps = []

                def tick():
                    if a_steps:
                        a_steps.pop(0)()

                def ssd_B(c):
                    d = CH[c]
                    bt, Bbt, ct, Bct, dAb, BdAb, nacs, Bnacs, xc, Bxc = (d[k] for k in (
                        "bt", "Bbt", "ct", "Bct", "dAb", "BdAb", "nacs", "Bnacs", "xc", "Bxc"))
                    cbl = []
                    for g in range(2):
                        pc, Bpc = prot_a.next()
                        op("pe", lambda e, pc=pc, g=g: e.matmul(pc[:, 0:128], lhsT=bt[:, g, :], rhs=ct[:, g, :], start=True, stop=True),
                           reads=[Bbt, Bct], writes=[Bpc])
                        cb, Bcb = cbs.next()
                        op("act", lambda e, cb=cb, pc=pc: e.copy(cb[:], pc[:, 0:128]), reads=[Bpc], writes=[Bcb])
                        cbl.append((cb, Bcb))
                    tick()
                    Yds = [prot_yd.next(), prot_yd.next()]
                    units = [(g, hq) for g in range(2) for hq in range(2)]

                    def emit_R(g, hq):
                        R, BR = prot_R.next()
                        fns = []
                        for h4 in range(4):
                            h = g * 8 + hq * 4 + h4
                            fns.append(lambda e, R=R, h4=h4, h=h: e.matmul(
                                R[:, h4 * 128:(h4 + 1) * 128], lhsT=dAb[:, h, :], rhs=tle[:], start=True, stop=False))
                            fns.append(lambda e, R=R, h4=h4: e.matmul(
                                R[:, h4 * 128:(h4 + 1) * 128], lhsT=negi[:], rhs=ustr[:], start=False, stop=True))
                        op("pe", fns, reads=[BdAb, B_C], writes=[BR])
                        return R, BR

                    def emit_rest(g, hq, R, BR):
                        cb, Bcb = cbl[g]
                        Yd, BYd = Yds[g]
                        Mt, BMt = Mts.next()
                        for h4 in range(4):
                            h = g * 8 + hq * 4 + h4
                            seg, Bseg = segs.next()
                            op("act", lambda e, seg=seg, h4=h4, h=h: e.activation(
                                out=seg[:], in_=R[:, h4 * 128:(h4 + 1) * 128], func=AF.Exp, bias=nacs[:, h:h + 1]),
                               reads=[BR, Bnacs], writes=[Bseg])
                            op("dve", lambda e, seg=seg, h4=h4: e.tensor_tensor(
                                out=Mt[:, h4, :], in0=seg[:], in1=cb[:], op=ALU.mult), reads=[Bseg, Bcb], writes=[BMt])
                        op("pe", [lambda e, h4=h4: e.matmul(
                            Yd[:, (hq * 4 + h4) * 64:(hq * 4 + h4 + 1) * 64], lhsT=Mt[:, h4, :],
                            rhs=xc[:, (g * 8 + hq * 4 + h4) * 64:(g * 8 + hq * 4 + h4 + 1) * 64], start=True, stop=True)
                            for h4 in range(4)], reads=[BMt, Bxc], writes=[BYd])
                        if hq == 1:
                            yd, Byd = yds.next()
                            op("act", lambda e: e.copy(yd[:], Yd[:, :]), reads=[BYd], writes=[Byd])
                            d["yd"][g] = (yd, Byd)

                    cur = emit_R(*units[0])
                    for i, (g, hq) in enumerate(units):
                        nxt = emit_R(*units[i + 1]) if i + 1 < len(units) else None
                        emit_rest(g, hq, *cur)
                        cur = nxt
                        tick()

                def ssd_C(c):
                    d = CH[c]
                    rows = d["rows"]
                    xs, Bxs, zs, Bzs, ct, Bct, btk, Bbtk, eat, Beat, dec, Bdec, xd, Bxd = (d[k] for k in (
                        "xs", "Bxs", "zs", "Bzs", "ct", "Bct", "btk", "Bbtk", "eat", "Beat", "dec", "Bdec", "xd", "Bxd"))
                    mm = []
                    for g in range(2):
                        Yo, BYo = prot_yo.next()
                        op("pe", lambda e, Yo=Yo, g=g: e.matmul(Yo[:, :], lhsT=ct[:, g, :], rhs=prevb[:, g, :], start=True, stop=True),
                           reads=[Bct, Bprevb], writes=[BYo])
                        St, BSt = prot_st.next()
                        op("pe", lambda e, St=St, g=g: e.matmul(St[:, :], lhsT=btk[:, g, :], rhs=xd[:, g * 512:(g + 1) * 512],
                                                                start=True, stop=True), reads=[Bbtk, Bxd], writes=[BSt])
                        t1, Bt1 = t1s.next()
                        op("dve", lambda e, t1=t1, Yo=Yo, g=g: e.tensor_tensor(
                            out=t1[:].rearrange("p (h c) -> p h c", c=64), in0=Yo[:, :].rearrange("p (h c) -> p h c", c=64),
                            in1=eat[:, g * 8:(g + 1) * 8].unsqueeze(2).to_broadcast([128, 8, 64]), op=ALU.mult),
                           reads=[BYo, Beat], writes=[Bt1])
                        op("dve", lambda e, g=g: e.tensor_tensor(
                            out=prev[:, g, :].rearrange("p (h c) -> p h c", c=64), in0=prev[:, g, :].rearrange("p (h c) -> p h c", c=64),
                            in1=dec[:, g * 8:(g + 1) * 8].unsqueeze(2).to_broadcast([128, 8, 64]), op=ALU.mult),
                           reads=[Bprev, Bdec], writes=[Bprev])
                        op("dve", lambda e, St=St, g=g: e.tensor_tensor(out=prev[:, g, :], in0=prev[:, g, :], in1=St[:, :], op=ALU.add),
                           reads=[Bprev, BSt], writes=[Bprev])
                        op("act", lambda e, g=g: e.copy(prevb[:, g, :], prev[:, g, :]), reads=[Bprev], writes=[Bprevb])
                        mm.append((t1, Bt1))
                    tick()
                    outs = []
                    for g in range(2):
                        yd, Byd = d["yd"][g]
                        t1, Bt1 = mm[g]
                        op("dve", lambda e, t1=t1, yd=yd: e.tensor_tensor(out=t1[:], in0=t1[:], in1=yd[:], op=ALU.add),
                           reads=[Bt1, Byd], writes=[Bt1])
                        t2, Bt2 = t2s.next()
                        op("dve", lambda e, t2=t2, g=g: e.tensor_tensor(out=t2[:], in0=xs[:, g * 512:(g + 1) * 512],
                                                                        in1=Dfull[:, g * 512:(g + 1) * 512], op=ALU.mult),
                           reads=[Bxs, BDf], writes=[Bt2])
                        op("dve", lambda e, t1=t1, t2=t2: e.tensor_tensor(out=t1[:], in0=t1[:], in1=t2[:], op=ALU.add),
                           reads=[Bt1, Bt2], writes=[Bt1])
                        op("dve", lambda e, t1=t1, g=g: e.tensor_tensor(out=t1[:], in0=t1[:], in1=zs[:, g * 512:(g + 1) * 512], op=ALU.mult),
                           reads=[Bt1, Bzs], writes=[Bt1])
                        sn, Bsn = ssn.next()
                        op("act", lambda e, t1=t1, sn=sn: e.activation(out=jk[:], in_=t1[:], func=AF.Square, accum_out=sn[:]),
                           reads=[Bt1], writes=[Bjk, Bsn])
                        op("act", lambda e, sn=sn: e.activation(out=sn[:], in_=sn[:], func=AF.Ln, scale=1.0 / 512, bias=EPS),
                           reads=[Bsn], writes=[Bsn])
                        op("act", lambda e, sn=sn: e.activation(out=sn[:], in_=sn[:], func=AF.Exp, scale=-0.5), reads=[Bsn], writes=[Bsn])
                        yn, Byn = yns.next()
                        op("dve", lambda e, yn=yn, t1=t1, sn=sn, g=g: e.scalar_tensor_tensor(
                            out=yn[:], in0=t1[:], scalar=sn[:, 0:1], in1=NG[:, g * 512:(g + 1) * 512], op0=ALU.mult, op1=ALU.mult),
                           reads=[Bt1, Bsn, BNG], writes=[Byn])
                        outs.append((g, yn, Byn))
                    del CH[c]

                    def tail():
                        for g, yn, Byn in outs:
                            pt, Bpt = prot_a.next()
                            ptb = pt[:].bitcast(BF16)
                            op("pe", [lambda e, ptb=ptb, yn=yn, q=q: e.transpose(ptb[:, q * 128:(q + 1) * 128], yn[:, q * 128:(q + 1) * 128],
                                                                               identb[:]) for q in range(4)], reads=[Byn, B_C], writes=[Bpt])
                            ytt, Bytt = yts.next()
                            op("act", lambda e, ytt=ytt, ptb=ptb: e.copy(ytt[:], ptb[:, 0:512].rearrange("p (q t) -> p q t", t=128)),
                               reads=[Bpt], writes=[Bytt])
                            dma("sp", YT[1024 + g * 512:1024 + (g + 1) * 512, rows].rearrange("(q p) t -> p q t", p=128), ytt[:],
                                reads=[Bytt], writes=[B_YT])
                    return tail

                for f in ssd_A(0):
                    f()
                if NT > 1:
                    for f in ssd_A(1):
                        f()
                ssd_B(0)
                tail_prev = None
                for c in range(NT):
                    if c + 2 < NT:
                        a_steps.extend(ssd_A(c + 2))
                        tick()
                    if c + 1 < NT:
                        ssd_B(c + 1)
                    tl = ssd_C(c)
                    while a_steps:
                        tick()
                    if tail_prev is not None:
                        tail_prev()
                    tail_prev = tl
                tail_prev()
                fw.barrier()
            if stop_after == "P4":
                break
            with ExitStack() as st:
                Wo, BWo = load_weight(st, "Wo", w_out[L], 2048, D)
                yts = Rot(fw, st, nc, "yT", [128, 16, 512], BF16, 2)
                xts = Rot(fw, st, nc, "xt5", [128, D], F32, 2)
                x1s = Rot(fw, st, nc, "x1t", [128, D], F32, 2)
                prot = PRot([0, 1, 2, 3])
                for tb in range(NB):
                    tsl = slice(tb * 512, (tb + 1) * 512)
                    yT, ByT = yts.next()
                    for eh in range(4):
                        dma("sp", yT[:, eh * 4:(eh + 1) * 4, :], YT[eh * 512:(eh + 1) * 512, tsl].rearrange("(e p) t -> p e t", p=128),
                            reads=[B_YT], writes=[ByT])
                    for tt in range(4):
                        ti = tb * 4 + tt
                        rows = slice(ti * 128, (ti + 1) * 128)
                        xt, Bx = xts.next()
                        dma("sp", xt[:], x_cur[rows, :], reads=[B_xcur], writes=[Bx])
                        x1t, Bx1 = x1s.next()
                        for half in range(2):
                            ps, bps = prot.next()
                            op("pe", [lambda e, ps=ps, e_=e_, tt=tt, half=half, yT=yT: e.matmul(
                                ps[:, :], lhsT=yT[:, e_, tt * 128:(tt + 1) * 128], rhs=Wo[:, e_, half * 512:(half + 1) * 512],
                                start=(e_ == 0), stop=(e_ == 15)) for e_ in range(16)], reads=[ByT, BWo], writes=[bps])
                            op("dve", lambda e, ps=ps, x1t=x1t, xt=xt, half=half: e.tensor_tensor(
                                out=x1t[:, half * 512:(half + 1) * 512], in0=ps[:, :], in1=xt[:, half * 512:(half + 1) * 512], op=ALU.add),
                               reads=[bps, Bx], writes=[Bx1])
                        dma("sp", X1[rows, :], x1t[:], reads=[Bx1], writes=[B_X1])
                fw.barrier()
            if stop_after == "P5a":
                break
            with ExitStack() as st:
                prot_t = PRot([6, 7])
                g2T, Bg2 = load_T(st, "g2T", [ffn_g[L].rearrange("(a b) -> a b", b=128)], 128, 128, prot_t)
                fcT, Bfc = load_T(st, "fcT", [fconv_w[L], fconv_b[L:L + 1, :]], 2 * DFF, 128, prot_t)
                Wu, BWu = load_weight(st, "Wu", w_up[L], D, 2 * DFF, gT=g2T[:, 0, :], Bg=Bg2, piece=512)
                xts = Rot(fw, st, nc, "xt6", [128, D], F32, 2)
                junk = sb(st, "junk6", [128, D], BF16); Bjunk = fw.buf("junk6")
                sss = Rot(fw, st, nc, "ss6", [128, 1], F32, 2)
                hbs = Rot(fw, st, nc, "hb6", [128, D], BF16, 2)
                hT = sb(st, "hT6", [128, 8, 512], BF16); BhT = fw.buf("hT6")
                gts = Rot(fw, st, nc, "gt6", [128, 512], BF16, 3)
                Us = Rot(fw, st, nc, "U6", [128, 514], F32, 2)
                accs = Rot(fw, st, nc, "acc6", [128, 512], F32, 6)
                ffn_pend = []
                sgs = Rot(fw, st, nc, "sg6", [128, 512], F32, 2)
                halo2 = sb(st, "halo2", [128, 44, 2], F32); Bh2 = fw.buf("halo2")
                op("dve", lambda e: e.memset(halo2[:], 0.0), writes=[Bh2])
                prot_u = PRot([0, 1, 2])
                for tb in range(NB):
                    tsl = slice(tb * 512, (tb + 1) * 512)
                    for tt in range(4):
                        ti = tb * 4 + tt
                        xt, Bx = xts.next()
                        dma("sp", xt[:], X1[ti * 128:(ti + 1) * 128, :], reads=[B_X1], writes=[Bx])
                        ss, Bss = sss.next()
                        hb, Bhb = hbs.next()
                        rmsnorm_T(None, xt, Bx, ss, Bss, junk, Bjunk, hb, Bhb, hT, BhT, tt, prot_t)
                    for c in range(22):
                        pair = []
                        for cc in (c, c + 22):
                            ps, bps = prot_u.next()
                            op("pe", [lambda e, ps=ps, k=k, cc=cc: e.matmul(ps[:, :], lhsT=Wu[:, k, cc * 128:(cc + 1) * 128], rhs=hT[:, k, :],
                                                                          start=(k == 0), stop=(k == 7)) for k in range(8)],
                               reads=[BWu, BhT], writes=[bps])
                            U, BU = Us.next()
                            op("act", lambda e, U=U, ps=ps: e.copy(U[:, 2:514], ps[:, :]), reads=[bps], writes=[BU])
                            op("act", lambda e, U=U, cc=cc: e.copy(U[:, 0:2], halo2[:, cc, :]), reads=[Bh2], writes=[BU])
                            op("act", lambda e, U=U, cc=cc: e.copy(halo2[:, cc, :], U[:, 512:514]), reads=[BU], writes=[Bh2])
                            acc, Bacc = accs.next()
                            op("act", lambda e, ps=ps, acc=acc, cc=cc: e.activation(
                                out=acc[:], in_=ps[:, :], func=AF.Identity, scale=fcT[:, cc, 2:3], bias=fcT[:, cc, 3:4]),
                               reads=[bps, Bfc], writes=[Bacc])
                            for kk in (1, 0):
                                op("dve", lambda e, U=U, acc=acc, cc=cc, kk=kk: e.scalar_tensor_tensor(
                                    out=acc[:], in0=U[:, kk:kk + 512], scalar=fcT[:, cc, kk:kk + 1], in1=acc[:],
                                    op0=ALU.mult, op1=ALU.add), reads=[BU, Bfc, Bacc], writes=[Bacc])
                            pair.append((acc, Bacc))
                        (ag, Bag), (av, Bav) = pair

                        def ffn_tail(c=c, ag=ag, Bag=Bag, av=av, Bav=Bav, tsl=tsl):
                            sg, Bsg = sgs.next()
                            op("act", lambda e: e.activation(out=sg[:], in_=ag[:], func=AF.Silu), reads=[Bag], writes=[Bsg])
                            gt, Bgt = gts.next()
                            op("dve", lambda e: e.tensor_tensor(out=gt[:], in0=sg[:], in1=av[:], op=ALU.mult),
                               reads=[Bsg, Bav], writes=[Bgt])
                            dma("sp", GTS[c, :, tsl], gt[:], reads=[Bgt], writes=[B_GTS])
                        if ffn_pend:
                            ffn_pend.pop(0)()
                        ffn_pend.append(ffn_tail)
                while ffn_pend:
                    ffn_pend.pop(0)()
                fw.barrier()
            with ExitStack() as st:
                if last:
                    FG, BFG = load_bc(st, "FG", fin_g.rearrange("(a b) -> a b", a=1), D)
                Wd, BWd = load_weight(st, "Wd", w_down[L], DFF, D, piece=512)
                xts = Rot(fw, st, nc, "xt7", [128, D], F32, 2)
                x2s = Rot(fw, st, nc, "x2t", [128, D], F32, 2)
                GTs = Rot(fw, st, nc, "GT", [128, 22, 512], BF16, 2)
                junk = sb(st, "junk7", [128, D], BF16); Bjunk = fw.buf("junk7")
                sss = Rot(fw, st, nc, "ss7", [128, 1], F32, 2)
                prot_d = PRot([0, 1, 2, 3])
                for tb in range(NB):
                    tsl = slice(tb * 512, (tb + 1) * 512)
                    GT, BGT = GTs.next()
                    for c4 in range(0, 22, 2):
                        dma("sp", GT[:, c4:c4 + 2, :], GTS[c4:c4 + 2, :, tsl].rearrange("c p t -> p c t"), reads=[B_GTS], writes=[BGT])
                    for tt in range(4):
                        ti = tb * 4 + tt
                        rows = slice(ti * 128, (ti + 1) * 128)
                        xt, Bx = xts.next()
                        dma("sp", xt[:], X1[rows, :], reads=[B_X1], writes=[Bx])
                        x2t, Bx2 = x2s.next()
                        for half in range(2):
                            ps, bps = prot_d.next()
                            op("pe", [lambda e, ps=ps, c=c, tt=tt, half=half: e.matmul(
                                ps[:, :], lhsT=GT[:, c, tt * 128:(tt + 1) * 128], rhs=Wd[:, c, half * 512:(half + 1) * 512],
                                start=(c == 0), stop=(c == 21)) for c in range(22)], reads=[BGT, BWd], writes=[bps])
                            op("dve", lambda e, ps=ps, x2t=x2t, xt=xt, half=half: e.tensor_tensor(
                                out=x2t[:, half * 512:(half + 1) * 512], in0=ps[:, :], in1=xt[:, half * 512:(half + 1) * 512], op=ALU.add),
                               reads=[bps, Bx], writes=[Bx2])
                        if not last:
                            dma("sp", X2[rows, :], x2t[:], reads=[Bx2], writes=[B_X2])
                        else:
                            ss, Bss = sss.next()
                            op("act", lambda e, x2t=x2t, ss=ss: e.activation(out=junk[:], in_=x2t[:], func=AF.Square, accum_out=ss[:]),
                               reads=[Bx2], writes=[Bjunk, Bss])
                            op("act", lambda e, ss=ss: e.activation(out=ss[:], in_=ss[:], func=AF.Ln, scale=1.0 / D, bias=EPS),
                               reads=[Bss], writes=[Bss])
                            op("act", lambda e, ss=ss: e.activation(out=ss[:], in_=ss[:], func=AF.Exp, scale=-0.5), reads=[Bss], writes=[Bss])
                            op("dve", lambda e, x2t=x2t, ss=ss: e.scalar_tensor_tensor(
                                out=x2t[:], in0=x2t[:], scalar=ss[:, 0:1], in1=FG[:], op0=ALU.mult, op1=ALU.mult),
                               reads=[Bx2, Bss, BFG], writes=[Bx2])
                            dma("sp", out[rows, :], x2t[:], reads=[Bx2], writes=[B_OUT])
                fw.barrier()
            x_cur = X2
            B_xcur = B_X2
        fw.finish([B_OUT, B_QF, B_KF, B_VF, B_QS, B_KS, B_VS, B_ZS, B_XS, B_DTS, B_BT, B_CT, B_BTOK, B_YT, B_X1, B_X2])
    return nc


_INPUT_NAMES = ["x", "mix_norm_g", "w_in", "fox_f_bias", "fox_out_g", "sb_out_g", "ssd_conv_w", "ssd_conv_b",
                "ssd_dt_bias", "ssd_a_log", "ssd_d", "ssd_norm_g", "w_out", "ffn_norm_g", "w_up", "ffn_conv_w",
                "ffn_conv_b", "w_down", "final_norm_g"]


def kernel(**inputs):
    nc = build(depth=2)
    shared = {k: np.ascontiguousarray(np.asarray(inputs[k], dtype=np.float32)) for k in _INPUT_NAMES if k != "x"}
    x = np.asarray(inputs["x"], dtype=np.float32)
    in_maps = []
    for c in range(8):
        m = dict(shared)
        m["x"] = np.ascontiguousarray(x[c])
        in_maps.append(m)
    res = run_bass_kernel_spmd(nc, in_maps, core_ids=list(range(8)))
    return np.stack([np.asarray(r["out"], dtype=np.float32) for r in res.results], axis=0)
```

```python
import numpy as np
import concourse.bass as bass
import concourse.mybir as mybir
from concourse.bass_utils import run_bass_kernel_spmd
from contextlib import ExitStack

F32 = mybir.dt.float32
BF16 = mybir.dt.bfloat16
AF = mybir.ActivationFunctionType
ALU = mybir.AluOpType

S = 4096
D = 1024
NT = S // 128
NB = S // 512
NIN = 5656
DFF = 2816
C_FQ, C_FK, C_FV, C_FF, C_SQ, C_SK, C_SV, C_Z, C_XBC, C_DT = 0, 512, 1024, 1536, 1544, 2056, 2568, 3080, 4104, 5640
EPS = 1e-6
NEG = -30000.0
PROF_HEADS = 8


class Buf:
    def __init__(self, name):
        self.name = name
        self.w = None
        self.r = {}
        self.slot = None
        self.persist = False


class EngS:
    def __init__(self, name, eng, sem, is_pe=False):
        self.name, self.eng, self.sem = name, eng, sem
        self.count = 0
        self.waited = {}
        self.is_pe = is_pe

    def wait(self, ev):
        if ev is None:
            return
        sem, val = ev
        if self.is_pe and sem is self.sem:
            return
        if self.waited.get(id(sem), 0) < val:
            self.eng.wait_ge(sem, val)
            self.waited[id(sem)] = val


class FW:
    def __init__(self, nc, es):
        self.nc, self.es = nc, es
        self.E = {}
        for name, eng, pe in (("pe", nc.tensor, True), ("act", nc.scalar, False),
                              ("dve", nc.vector, False), ("pool", nc.gpsimd, False),
                              ("sp", nc.sync, False)):
            sem = es.enter_context(nc.semaphore("s_" + name))
            self.E[name] = EngS(name, eng, sem, pe)
        self.nbuf = 0
        self.dbufs = []
        self.slots = []
        self.free_slots = []

    def buf(self, name=None, persist=False):
        self.nbuf += 1
        b = Buf((name or "b") + f"_{self.nbuf}")
        b.persist = persist
        return b

    def _pre(self, E, reads, writes):
        for b in reads:
            E.wait(b.w)
        for b in writes:
            E.wait(b.w)
            for ev in list(b.r.values()):
                E.wait(ev)

    def _post(self, ev, reads, writes):
        for b in reads:
            b.r[id(ev[0])] = ev
        for b in writes:
            b.w = ev
            b.r = {}

    def op(self, en, fns, reads=(), writes=(), signal=True):
        E = self.E[en]
        self._pre(E, reads, writes)
        if not isinstance(fns, (list, tuple)):
            fns = [fns]
        ins = None
        for f in fns:
            ins = f(E.eng)
        if signal:
            E.count += 1
            ins.then_inc(E.sem, 1)
            self._post((E.sem, E.count), reads, writes)
        else:
            self._post((E.sem, E.count + 1), reads, writes)

    def dma(self, qn, out, in_, reads=(), writes=(), **kw):
        Q = self.E[qn]
        self._pre(Q, reads, writes)
        d = writes[0]
        if d.slot is None:
            if self.free_slots:
                d.slot = self.free_slots.pop()
            else:
                d.slot = [self.es.enter_context(self.nc.semaphore(f"dq{len(self.slots)}")), 0]
                self.slots.append(d.slot)
            self.dbufs.append(d)
        d.slot[1] += 16
        Q.eng.dma_start(out=out, in_=in_, **kw).then_inc(d.slot[0], 16)
        self._post((d.slot[0], d.slot[1]), reads, writes)

    def barrier(self):
        evs = [(E.sem, E.count) for E in self.E.values() if E.count > 0]
        evs += [(sl[0], sl[1]) for sl in self.slots]
        for E in self.E.values():
            for ev in evs:
                E.wait(ev)
        keep = []
        for b in self.dbufs:
            if b.persist:
                keep.append(b)
            else:
                self.free_slots.append(b.slot)
                b.slot = None
        self.dbufs = keep

    def finish(self, bufs):
        E = self.E["sp"]
        for b in bufs:
            E.wait(b.w)
            for ev in list(b.r.values()):
                E.wait(ev)


class Rot:
    def __init__(self, fw, es, nc, name, shape, dt, n, psum=False):
        self.items = []
        for i in range(n):
            fw.nbuf += 1
            if psum:
                t = es.enter_context(nc.psum_tensor(f"{name}{i}_u{fw.nbuf}", shape, dt))
            else:
                t = es.enter_context(nc.sbuf_tensor(f"{name}{i}_u{fw.nbuf}", shape, dt))
            self.items.append((t, fw.buf(f"{name}{i}")))
        self.i = 0

    def next(self):
        it = self.items[self.i % len(self.items)]
        self.i += 1
        return it


def build(depth=2, stop_after=None, dbg=False, seq=4096):
    global S, NT, NB
    S, NT, NB = seq, seq // 128, seq // 512
    nc = bass.Bass("TRN2", target_bir_lowering=False)
    skind = "ExternalOutput" if dbg else "Internal"

    def din(name, shape):
        return nc.dram_tensor(name, list(shape), F32, kind="ExternalInput").ap()

    x_in = din("x", [S, D])
    mix_g = din("mix_norm_g", [depth, D])
    w_in = din("w_in", [depth, D, NIN])
    fox_fb = din("fox_f_bias", [depth, 8])
    fox_g = din("fox_out_g", [depth, 512])
    sb_g = din("sb_out_g", [depth, 512])
    conv_w = din("ssd_conv_w", [depth, 4, 1536])
    conv_b = din("ssd_conv_b", [depth, 1536])
    dt_bias = din("ssd_dt_bias", [depth, 16])
    a_log = din("ssd_a_log", [depth, 16])
    ssd_d = din("ssd_d", [depth, 16])
    ssd_ng = din("ssd_norm_g", [depth, 1024])
    w_out = din("w_out", [depth, 2048, D])
    ffn_g = din("ffn_norm_g", [depth, D])
    w_up = din("w_up", [depth, D, 2 * DFF])
    fconv_w = din("ffn_conv_w", [depth, 3, 2 * DFF])
    fconv_b = din("ffn_conv_b", [depth, 2 * DFF])
    w_down = din("w_down", [depth, DFF, D])
    fin_g = din("final_norm_g", [D])
    out = nc.dram_tensor("out", [S, D], F32, kind="ExternalOutput").ap()

    def scr(name, shape, dt):
        return nc.dram_tensor(name, list(shape), dt, kind=skind).ap()

    QF = scr("QF", [8, 70, S], BF16)
    KF = scr("KF", [8, 70, S], BF16)
    VF = scr("VF", [S, 8, 65], BF16)
    QS = scr("QS", [8, 64, S], BF16)
    KS = scr("KS", [8, 64, S], BF16)
    VS = scr("VS", [S, 512], BF16)
    ZS = scr("ZS", [S, 1024], F32)
    XS = scr("XS", [S, 1024], F32)
    DTS = scr("DTS", [S, 16], F32)
    BT = scr("BT", [2, 128, S], BF16)
    CT = scr("CT", [2, 128, S], BF16)
    BTOK = scr("BTOK", [S, 2, 128], BF16)
    YT = scr("YT", [2048, S], BF16)
    X1 = scr("X1", [S, D], F32)
    X2 = scr("X2", [S, D], F32)
    GTS = scr("GTS", [22, 128, S], BF16)

    with ExitStack() as es:
        fw = FW(nc, es)
        op, dma = fw.op, fw.dma

        def sb(st, name, shape, dt):
            fw.nbuf += 1
            return st.enter_context(nc.sbuf_tensor(f"{name}_u{fw.nbuf}", list(shape), dt))

        B_QF, B_KF, B_VF, B_QS, B_KS, B_VS = (fw.buf(n, True) for n in ("QF", "KF", "VF", "QS", "KS", "VS"))
        B_ZS, B_XS, B_DTS, B_BT, B_CT, B_BTOK = (fw.buf(n, True) for n in ("ZS", "XS", "DTS", "BT", "CT", "BTOK"))
        B_YT, B_X1, B_X2, B_OUT, B_GTS = (fw.buf(n, True) for n in ("YT", "X1", "X2", "OUT", "GTS"))

        PS = [es.enter_context(nc.psum_tensor(f"ps{i}", [128, 512], F32)) for i in range(8)]
        BPS = [fw.buf(f"ps{i}") for i in range(8)]

        class PRot:
            def __init__(self, idxs):
                self.idxs, self.i = idxs, 0

            def next(self):
                k = self.idxs[self.i % len(self.idxs)]
                self.i += 1
                return PS[k], BPS[k]

        identb = sb(es, "identb", [128, 128], BF16)
        identf = sb(es, "identf", [128, 128], F32)
        tle = sb(es, "tle", [128, 128], F32)
        onesf = sb(es, "onesf", [128, 128], F32)
        negi = sb(es, "negi", [128, 128], BF16)
        ustr = sb(es, "ustr", [128, 128], BF16)
        ntri = sb(es, "ntri", [128, 128], BF16)
        uge = sb(es, "uge", [128, 128], BF16)
        ult = sb(es, "ult", [128, 32, 32], BF16)
        sel = sb(es, "sel", [64, 32, 128], BF16)
        negrow = sb(es, "negrow", [1, 128], BF16)
        onescol = sb(es, "onescol", [128, 1], BF16)
        zrow = sb(es, "zrow", [1, 512], BF16)
        nwf = sb(es, "nwf", [65, 64], F32)
        onesb = sb(es, "onesb", [8, 512], BF16)
        B_C = fw.buf("consts")

        def mk(tile_ap, val, pattern=None, cm=None, cmp=None, eng="pool"):
            op(eng, lambda e: e.memset(tile_ap, val), writes=[B_C])
            if pattern is not None:
                op(eng, lambda e: e.affine_select(out=tile_ap, in_=tile_ap, pattern=pattern, compare_op=cmp,
                                                  fill=0.0, base=0, channel_multiplier=cm),
                   reads=[B_C], writes=[B_C])

        mk(identb[:], 1.0, [[1, 128]], -1, ALU.is_equal)
        mk(identf[:], 1.0, [[1, 128]], -1, ALU.is_equal)
        mk(tle[:], 1.0, [[1, 128]], -1, ALU.is_ge)
        mk(onesf[:], 1.0)
        mk(negi[:], NEG, [[1, 128]], -1, ALU.is_equal)
        mk(ustr[:], 1.0, [[-1, 128]], 1, ALU.is_gt)
        mk(ntri[:], -1.0, [[-1, 128]], 1, ALU.is_ge)
        mk(uge[:], 1.0, [[-1, 128]], 1, ALU.is_ge)
        mk(ult[:], 1.0, [[1, 32], [-1, 32]], 0, ALU.is_gt)
        mk(sel[0:32], -1.0, [[-1, 32], [0, 128]], 1, ALU.is_equal)
        mk(sel[32:64], -1.0, [[-1, 32], [0, 128]], 1, ALU.is_equal)
        mk(negrow[:], -1.0)
        mk(onescol[:], 1.0)
        mk(zrow[:], 0.0)
        mk(nwf[0:64, :], 1.0 / 64)
        mk(nwf[64:65, :], EPS)
        mk(onesb[:], 1.0)
        for i in range(3):
            for tb in range(NB):
                dma("sp", QF[:, 67 + i, tb * 512:(tb + 1) * 512], onesb[:], reads=[B_C], writes=[B_QF])
                dma("sp", KF[:, 64 + i, tb * 512:(tb + 1) * 512], onesb[:], reads=[B_C], writes=[B_KF])
        fw.barrier()

        def load_T(st, name, rows, C, blk, prot):
            R = sum(r.shape[0] for r in rows)
            nblk = C // blk
            stg = sb(st, name + "_stg", [R, C], F32)
            Bs = fw.buf(name + "_stg")
            r0 = 0
            for r in rows:
                dma("sp", stg[r0:r0 + r.shape[0], :], r, writes=[Bs])
                r0 += r.shape[0]
            outt = sb(st, name, [blk, nblk, R], F32)
            Bo = fw.buf(name)
            per = max(1, 512 // R)
            b0 = 0
            while b0 < nblk:
                nb_ = min(per, nblk - b0)
                ps, bps = prot.next()
                op("pe", [lambda e, j=j, b0=b0: e.transpose(ps[0:blk, (j - b0) * R:(j - b0 + 1) * R],
                                                          stg[0:R, j * blk:(j + 1) * blk], identf[0:R, 0:R])
                          for j in range(b0, b0 + nb_)], reads=[Bs, B_C], writes=[bps])
                op("dve", lambda e, b0=b0, nb_=nb_: e.tensor_copy(
                    outt[:, b0:b0 + nb_, :], ps[0:blk, 0:nb_ * R].rearrange("p (a r) -> p a r", r=R)),
                   reads=[bps], writes=[Bo])
                b0 += nb_
            return outt, Bo

        def load_bc(st, name, row_ap, n):
            t = sb(st, name, [128, n], F32)
            Bt = fw.buf(name)
            dma("sp", t[:], row_ap.to_broadcast([128, n]), writes=[Bt])
            return t, Bt

        def load_weight(st, name, w_ap, K, N, gT=None, Bg=None, piece=1024):
            kc = K // 128
            wt = sb(st, name, [128, kc, N], BF16)
            Bw = fw.buf(name)
            stg = Rot(fw, st, nc, name + "_s", [128, piece], F32, 3)
            cnt = 0
            for k in range(kc):
                c0 = 0
                while c0 < N:
                    cw = min(piece, N - c0)
                    t, Bt = stg.next()
                    dma("sp", t[:, 0:cw], w_ap[k * 128:(k + 1) * 128, c0:c0 + cw], writes=[Bt])
                    eng = ("dve", "act")[cnt % 2]
                    cnt += 1
                    if gT is None:
                        if eng == "act":
                            op(eng, lambda e, t=t, k=k, c0=c0, cw=cw: e.copy(wt[:, k, c0:c0 + cw], t[:, 0:cw]),
                               reads=[Bt], writes=[Bw])
                        else:
                            op(eng, lambda e, t=t, k=k, c0=c0, cw=cw: e.tensor_copy(wt[:, k, c0:c0 + cw], t[:, 0:cw]),
                               reads=[Bt], writes=[Bw])
                    elif eng == "act":
                        op(eng, lambda e, t=t, k=k, c0=c0, cw=cw: e.activation(
                            out=wt[:, k, c0:c0 + cw], in_=t[:, 0:cw], func=AF.Copy, scale=gT[:, k:k + 1]),
                           reads=[Bt, Bg], writes=[Bw])
                    else:
                        op(eng, lambda e, t=t, k=k, c0=c0, cw=cw: e.tensor_scalar(
                            out=wt[:, k, c0:c0 + cw], in0=t[:, 0:cw], scalar1=gT[:, k:k + 1], scalar2=None,
                            op0=ALU.mult), reads=[Bt, Bg], writes=[Bw])
                    c0 += cw
            return wt, Bw

        def rmsnorm_T(st_tiles, xt, Bx, ss, Bss, junk, Bj, hb, Bhb, hT, BhT, tt, prot):
            op("act", lambda e: e.activation(out=junk[:], in_=xt[:], func=AF.Square, accum_out=ss[:]),
               reads=[Bx], writes=[Bj, Bss])
            op("act", lambda e: e.activation(out=ss[:], in_=ss[:], func=AF.Ln, scale=1.0 / D, bias=EPS),
               reads=[Bss], writes=[Bss])
            op("act", lambda e: e.activation(out=ss[:], in_=ss[:], func=AF.Exp, scale=-0.5),
               reads=[Bss], writes=[Bss])
            op("dve", lambda e: e.tensor_scalar(out=hb[:], in0=xt[:], scalar1=ss[:, 0:1], scalar2=None, op0=ALU.mult),
               reads=[Bx, Bss], writes=[Bhb])
            ps, bps = prot.next()
            psb = ps[:].bitcast(BF16)
            op("pe", [lambda e, k=k: e.transpose(psb[:, k * 128:(k + 1) * 128], hb[:, k * 128:(k + 1) * 128], identb[:])
                      for k in range(8)], reads=[Bhb, B_C], writes=[bps])
            op("dve", lambda e: e.tensor_copy(hT[:, :, tt * 128:(tt + 1) * 128],
                                              psb.rearrange("p (k t) -> p k t", t=128)),
               reads=[bps], writes=[BhT])

        x_cur = x_in
        B_xcur = fw.buf("xin")
        for L in range(depth):
            last = (L == depth - 1)
            with ExitStack() as st:
                prot_t = PRot([6, 7])
                g1T, Bg1 = load_T(st, "g1T", [mix_g[L].rearrange("(a b) -> a b", b=128)], 128, 128, prot_t)
                cwT, Bcw = load_T(st, "cwT", [conv_w[L], conv_b[L:L + 1, :]], 1536, 128, prot_t)
                nfb = sb(st, "nfb", [8, 1], F32)
                Bnfb = fw.buf("nfb")
                dma("sp", nfb[:], fox_fb[L].rearrange("(a b) -> a b", b=1), writes=[Bnfb])
                op("dve", lambda e: e.tensor_scalar(out=nfb[:], in0=nfb[:], scalar1=-1.0, scalar2=None, op0=ALU.mult),
                   reads=[Bnfb], writes=[Bnfb])
                dtb, Bdtb = load_bc(st, "dtb", dt_bias[L:L + 1, :], 16)
                Wb, BW = load_weight(st, "Wb", w_in[L], D, NIN, gT=g1T[:, 0, :], Bg=Bg1, piece=707)

                xts = Rot(fw, st, nc, "xt", [128, D], F32, 2)
                junk = sb(st, "junk", [128, D], BF16); Bjunk = fw.buf("junk")
                sss = Rot(fw, st, nc, "ss", [128, 1], F32, 2)
                hbs = Rot(fw, st, nc, "hb", [128, D], BF16, 2)
                hTs = Rot(fw, st, nc, "hT", [128, 8, 512], BF16, 2)
                ev_b = Rot(fw, st, nc, "evb", [128, 512], BF16, 6)
                ev_va = Rot(fw, st, nc, "eva", [128, 8, 65], BF16, 2)
                for tva, Bva in ev_va.items:
                    op("dve", lambda e, tva=tva: e.memset(tva[:], 1.0), writes=[Bva])
                ev_z = Rot(fw, st, nc, "evz", [128, 1024], F32, 1)
                ev_dt = Rot(fw, st, nc, "evdt", [128, 16], F32, 2)
                Us = Rot(fw, st, nc, "U", [128, 515], F32, 2)
                accs = Rot(fw, st, nc, "acc", [128, 512], F32, 4)
                silf = Rot(fw, st, nc, "silf", [128, 512], F32, 3)
                xtok = Rot(fw, st, nc, "xtok", [128, 4, 128], F32, 2)
                btok = Rot(fw, st, nc, "btok", [128, 4, 128], BF16, 2)
                halo = sb(st, "halo", [128, 12, 3], F32); Bhalo = fw.buf("halo")
                op("dve", lambda e: e.memset(halo[:], 0.0), writes=[Bhalo])
                ffe = sb(st, "ffe", [8, 512], F32); Bffe = fw.buf("ffe")
                ones8 = sb(st, "ones8", [8, 512], F32); Bones8 = fw.buf("ones8")
                op("dve", lambda e: e.memset(ones8[:], 1.0), writes=[Bones8])
                CSs = Rot(fw, st, nc, "CSb", [8, 512], F32, 2)
                cks = Rot(fw, st, nc, "ck", [8, 3, 512], BF16, 1)
                cqs = Rot(fw, st, nc, "cq", [8, 3, 512], BF16, 1)
                carry = sb(st, "carry", [8, 1], F32); Bcarry = fw.buf("carry")
                prot_fm = PRot([0, 1, 2])
                prot_tm = PRot([3, 4, 5])

                def p1_norm(tb):
                    hT, BhT = hTs.next()
                    for tt in range(4):
                        ti = tb * 4 + tt
                        xt, Bx = xts.next()
                        dma("sp", xt[:], x_cur[ti * 128:(ti + 1) * 128, :], reads=[B_xcur], writes=[Bx])
                        ss, Bss = sss.next()
                        hb, Bhb = hbs.next()
                        rmsnorm_T(None, xt, Bx, ss, Bss, junk, Bjunk, hb, Bhb, hT, BhT, tt, prot_t)
                    return hT, BhT

                nxt = p1_norm(0)
                for tb in range(NB):
                    hT, BhT = nxt
                    if tb + 1 < NB:
                        nxt = p1_norm(tb + 1)
                    tsl = slice(tb * 512, (tb + 1) * 512)

                    def fm_mm(col0, M):
                        ps, bps = prot_fm.next()
                        op("pe", [lambda e, k=k: e.matmul(ps[0:M, :], lhsT=Wb[:, k, col0:col0 + M], rhs=hT[:, k, :],
                                                          start=(k == 0), stop=(k == 7)) for k in range(8)],
                           reads=[BW, BhT], writes=[bps])
                        return ps, bps

                    for (col, dst, Bdst, scale) in ((C_FQ, QF, B_QF, 0.125), (C_FK, KF, B_KF, 1.0),
                                                   (C_SQ, QS, B_QS, 0.125), (C_SK, KS, B_KS, 1.0)):
                        for j in range(4):
                            ps, bps = fm_mm(col + j * 128, 128)
                            t, Bt = ev_b.next()
                            op("act", lambda e, t=t, ps=ps, scale=scale: e.activation(out=t[:], in_=ps[:], func=AF.Identity,
                                                                                     scale=scale),
                               reads=[bps], writes=[Bt])
                            for hh in range(2):
                                dma("sp", dst[2 * j + hh, 0:64, tsl], t[hh * 64:(hh + 1) * 64, :], reads=[Bt], writes=[Bdst])
                    ps, bps = fm_mm(C_FF, 8)
                    op("act", lambda e, ps=ps: e.activation(out=ffe[:], in_=ps[0:8, :], func=AF.Exp, scale=-1.0,
                                                            bias=nfb[:, 0:1]), reads=[bps, Bnfb], writes=[Bffe])
                    op("act", lambda e: e.activation(out=ffe[:], in_=ffe[:], func=AF.Ln, bias=1.0),
                       reads=[Bffe], writes=[Bffe])
                    CSb, BCS = CSs.next()
                    init = 0.0 if tb == 0 else carry[:, 0:1]
                    op("dve", lambda e, init=init, CSb=CSb: e.tensor_tensor_scan(
                        out=CSb[:], data0=ones8[:], data1=ffe[:], initial=init, op0=ALU.mult, op1=ALU.add),
                       reads=[Bffe, Bones8, Bcarry], writes=[BCS])
                    op("dve", lambda e, CSb=CSb: e.tensor_copy(carry[:], CSb[:, 511:512]), reads=[BCS], writes=[Bcarry])
                    ck, Bck = cks.next()
                    cq, Bcq = cqs.next()
                    for i in range(3):
                        op("dve", lambda e, i=i, ck=ck, CSb=CSb: e.tensor_copy(ck[:, i, :], CSb[:]), reads=[BCS], writes=[Bck])
                        if i < 2:
                            op("dve", lambda e, i=i, ck=ck, CSb=CSb: e.tensor_tensor(out=CSb[:], in0=CSb[:], in1=ck[:, i, :],
                                                                                   op=ALU.subtract),
                               reads=[BCS, Bck], writes=[BCS])
                    op("dve", lambda e, ck=ck, cq=cq: e.tensor_scalar(out=cq[:], in0=ck[:], scalar1=-1.0, scalar2=None,
                                                                    op0=ALU.mult), reads=[Bck], writes=[Bcq])
                    for i in range(3):
                        dma("sp", QF[:, 64 + i, tsl], cq[:, i, :], reads=[Bcq], writes=[B_QF])
                        dma("sp", KF[:, 67 + i, tsl], ck[:, i, :], reads=[Bck], writes=[B_KF])
                    pend = []

                    def conv_tail(cc, acc, Bacc):
                        if cc < 8:
                            sf, Bsf = silf.next()
                            op("act", lambda e: e.activation(out=sf[:], in_=acc[:], func=AF.Silu), reads=[Bacc], writes=[Bsf])

                            def t2():
                                ps2, bps2 = prot_t.next()
                                op("pe", [lambda e, q=q: e.transpose(ps2[:, q * 128:(q + 1) * 128], sf[:, q * 128:(q + 1) * 128], identf[:])
                                          for q in range(4)], reads=[Bsf, B_C], writes=[bps2])
                                xk, Bxk = xtok.next()
                                op("dve", lambda e: e.tensor_copy(xk[:], ps2[:].rearrange("p (q c) -> p q c", c=128)),
                                   reads=[bps2], writes=[Bxk])
                                dma("sp", XS[tsl, cc * 128:(cc + 1) * 128].rearrange("(q p) c -> p q c", p=128), xk[:],
                                    reads=[Bxk], writes=[B_XS])
                            return t2
                        t, Bt = ev_b.next()
                        op("act", lambda e: e.activation(out=t[:], in_=acc[:], func=AF.Silu), reads=[Bacc], writes=[Bt])
                        g = (cc - 8) % 2
                        if cc >= 10:
                            dma("sp", CT[g, :, tsl], t[:], reads=[Bt], writes=[B_CT])
                            return None
                        dma("sp", BT[g, :, tsl], t[:], reads=[Bt], writes=[B_BT])

                        def t2():
                            ps2, bps2 = prot_t.next()
                            ps2b = ps2[:].bitcast(BF16)
                            op("pe", [lambda e, q=q: e.transpose(ps2b[:, q * 128:(q + 1) * 128], t[:, q * 128:(q + 1) * 128], identb[:])
                                      for q in range(4)], reads=[Bt, B_C], writes=[bps2])
                            bk, Bbk = btok.next()
                            op("dve", lambda e: e.tensor_copy(bk[:], ps2b[:, 0:512].rearrange("p (q c) -> p q c", c=128)),
                               reads=[bps2], writes=[Bbk])
                            dma("sp", BTOK[tsl, g, :].rearrange("(q p) c -> p q c", p=128), bk[:], reads=[Bbk], writes=[B_BTOK])
                        return t2

                    def conv_tick():
                        todo = [p for p in pend if p[0] <= 0]
                        for p in todo:
                            pend.remove(p)
                        for p in pend:
                            p[0] -= 1
                        for p in todo:
                            r = p[1]()
                            if r is not None:
                                pend.append([0, r])

                    for cc in range(12):
                        ps, bps = fm_mm(C_XBC + cc * 128, 128)
                        U, BU = Us.next()
                        op("act", lambda e, U=U, ps=ps: e.copy(U[:, 3:515], ps[:]), reads=[bps], writes=[BU])
                        op("act", lambda e, U=U, cc=cc: e.copy(U[:, 0:3], halo[:, cc, :]), reads=[Bhalo], writes=[BU])
                        op("act", lambda e, U=U, cc=cc: e.copy(halo[:, cc, :], U[:, 512:515]), reads=[BU], writes=[Bhalo])
                        acc, Bacc = accs.next()
                        op("act", lambda e, ps=ps, acc=acc, cc=cc: e.activation(
                            out=acc[:], in_=ps[:], func=AF.Identity, scale=cwT[:, cc, 3:4], bias=cwT[:, cc, 4:5]),
                           reads=[bps, Bcw], writes=[Bacc])
                        for kk in (2, 1, 0):
                            op("dve", lambda e, U=U, acc=acc, cc=cc, kk=kk: e.scalar_tensor_tensor(
                                out=acc[:], in0=U[:, kk:kk + 512], scalar=cwT[:, cc, kk:kk + 1], in1=acc[:],
                                op0=ALU.mult, op1=ALU.add), reads=[BU, Bcw, Bacc], writes=[Bacc])
                        conv_tick()
                        pend.append([0, lambda cc=cc, acc=acc, Bacc=Bacc: conv_tail(cc, acc, Bacc)])
                    while pend:
                        conv_tick()
                    for tt in range(4):
                        ti = tb * 4 + tt
                        rows = slice(ti * 128, (ti + 1) * 128)

                        def tm_mm(col0, N):
                            ps, bps = prot_tm.next()
                            op("pe", [lambda e, k=k: e.matmul(ps[:, 0:N], lhsT=hT[:, k, tt * 128:(tt + 1) * 128],
                                                              rhs=Wb[:, k, col0:col0 + N], start=(k == 0), stop=(k == 7))
                                      for k in range(8)], reads=[BW, BhT], writes=[bps])
                            return ps, bps

                        ps, bps = tm_mm(C_FV, 512)
                        va, Bva = ev_va.next()
                        op("dve", lambda e, va=va, ps=ps: e.tensor_copy(va[:, :, 0:64],
                                                                         ps[:].rearrange("p (h c) -> p h c", c=64)),
                           reads=[bps], writes=[Bva])
                        dma("sp", VF[rows, :, :], va[:], reads=[Bva], writes=[B_VF])
                        ps, bps = tm_mm(C_SV, 512)
                        t, Bt = ev_b.next()
                        op("act", lambda e, t=t, ps=ps: e.copy(t[:], ps[:]), reads=[bps], writes=[Bt])
                        dma("sp", VS[rows, :], t[:], reads=[Bt], writes=[B_VS])
                        zt, Bzt = ev_z.next()
                        for hh in range(2):
                            ps, bps = tm_mm(C_Z + hh * 512, 512)
                            op("act", lambda e, zt=zt, ps=ps, hh=hh: e.activation(out=zt[:, hh * 512:(hh + 1) * 512], in_=ps[:],
                                                                                 func=AF.Silu), reads=[bps], writes=[Bzt])
                        dma("sp", ZS[rows, :], zt[:], reads=[Bzt], writes=[B_ZS])
                        ps, bps = tm_mm(C_DT, 16)
                        dtt, Bdtt = ev_dt.next()
                        op("dve", lambda e, dtt=dtt, ps=ps: e.tensor_tensor(out=dtt[:], in0=ps[:, 0:16], in1=dtb[:], op=ALU.add),
                           reads=[bps, Bdtb], writes=[Bdtt])
                        dma("sp", DTS[rows, :], dtt[:], reads=[Bdtt], writes=[B_DTS])
                fw.barrier()
            if stop_after == "P1":
                break
            with ExitStack() as st:
                prot_t = PRot([6, 7])
                gfx, Bgfx = load_T(st, "gfx", [fox_g[L].rearrange("(h c) -> h c", c=64)], 64, 64, prot_t)
                KAs = Rot(fw, st, nc, "KA", [70, S], BF16, 2)
                QAs = Rot(fw, st, nc, "QA", [70, 512], BF16, 3)
                VAs = Rot(fw, st, nc, "VA", [128, NT, 65], BF16, 2)
                Pbs = Rot(fw, st, nc, "Pb", [128, 512], BF16, 4)
                sqs = Rot(fw, st, nc, "sq", [65, 512], F32, 2)
                rss = Rot(fw, st, nc, "rs", [64, 512], F32, 2)
                yos = Rot(fw, st, nc, "yo", [64, 512], BF16, 2)
                osbs = Rot(fw, st, nc, "osb", [64, 512], F32, 2)
                prot_s = PRot([0, 1, 2])
                prot_o = PRot([3, 4])
                prot_n = PRot([5])
                groups = [(h, qb) for h in range(PROF_HEADS) for qb in range(NB)]
                G = {}

                def f_loads(gi):
                    if gi >= len(groups):
                        return
                    h, qb = groups[gi]
                    d = {"h": h, "qb": qb}
                    if qb == 0:
                        KA, BKA = KAs.next()
                        dma("sp", KA[:], KF[h], reads=[B_KF], writes=[BKA])
                        VA, BVA = VAs.next()
                        for q4 in range(4):
                            k0, k1 = q4 * NT // 4, (q4 + 1) * NT // 4
                            dma("sp", VA[:, k0:k1, :], VF[k0 * 128:k1 * 128, h, :].rearrange("(kb p) c -> p kb c", p=128),
                                reads=[B_VF], writes=[BVA])
                        d.update(KA=KA, BKA=BKA, VA=VA, BVA=BVA)
                    else:
                        p = G[gi - 1]
                        d.update(KA=p["KA"], BKA=p["BKA"], VA=p["VA"], BVA=p["BVA"])
                    QA, BQA = QAs.next()
                    dma("sp", QA[:], QF[h, :, qb * 512:(qb + 1) * 512], reads=[B_QF], writes=[BQA])
                    d.update(QA=QA, BQA=BQA)
                    G[gi] = d

                blocks = []
                for gi, (h, qb) in enumerate(groups):
                    nkb = 4 * qb + 4
                    for kb in range(nkb):
                        blocks.append(dict(gi=gi, kb=kb, first=(kb == 0), last=(kb == nkb - 1), c0=max(0, kb - 4 * qb) * 128,
                                           diag=(kb - 4 * qb >= 0)))
                deferred = []

                def defer(k, fn):
                    deferred.append([k, fn])

                def run_deferred():
                    todo = [d for d in deferred if d[0] <= 0]
                    for d in todo:
                        deferred.remove(d)
                    for d in deferred:
                        d[0] -= 1
                    for d in todo:
                        d[1]()

                def f_A(b):
                    g = G[b["gi"]]
                    if b["first"]:
                        f_loads(b["gi"] + 2)
                        g["O"], g["BO"] = prot_o.next()
                    Sp, BS = prot_s.next()
                    b["Sp"], b["BS"] = Sp, BS
                    c0, kb = b["c0"], b["kb"]
                    fns = [lambda e: e.matmul(Sp[:, c0:512], lhsT=g["KA"][:, kb * 128:(kb + 1) * 128], rhs=g["QA"][:, c0:512],
                                              start=True, stop=not b["diag"])]
                    if b["diag"]:
                        fns.append(lambda e: e.matmul(Sp[:, c0:c0 + 128], lhsT=negi[:], rhs=ustr[:], start=False, stop=True))
                    op("pe", fns, reads=[g["BKA"], g["BQA"], B_C], writes=[BS])

                def f_B(b):
                    Sp, BS, c0 = b["Sp"], b["BS"], b["c0"]
                    Pb, BP = Pbs.next()
                    b["Pb"], b["BP"] = Pb, BP
                    op("act", lambda e: e.activation(out=Pb[:, c0:512], in_=Sp[:, c0:512], func=AF.Exp), reads=[BS], writes=[BP])

                def f_F(b):
                    g = G[b["gi"]]
                    O, BO, c0, kb = g["O"], g["BO"], b["c0"], b["kb"]
                    Pb, BP = b["Pb"], b["BP"]
                    op("pe", lambda e: e.matmul(O[0:65, c0:512], lhsT=g["VA"][:, kb, :], rhs=Pb[:, c0:512], start=b["first"],
                                                stop=b["last"]), reads=[g["BVA"], BP], writes=[BO], signal=b["last"])
                    if b["last"]:
                        h, qb = g["h"], g["qb"]
                        sq, Bsq = sqs.next()
                        Np, BN = prot_n.next()
                        rs, Brs = rss.next()
                        yo, Byo = yos.next()
                        op("act", lambda e: e.activation(out=sq[:], in_=O[0:65, :], func=AF.Square), reads=[BO], writes=[Bsq])
                        osb, Bosb = osbs.next()
                        op("act", lambda e: e.copy(osb[:], O[0:64, :]), reads=[BO], writes=[Bosb])

                        def e2():
                            op("pe", lambda e: e.matmul(Np[0:64, :], lhsT=nwf[0:65, :], rhs=sq[0:65, :], start=True, stop=True),
                               reads=[Bsq, B_C], writes=[BN])

                        def e3():
                            op("act", lambda e: e.activation(out=rs[:], in_=Np[0:64, :], func=AF.Ln), reads=[BN], writes=[Brs])
                            op("act", lambda e: e.activation(out=rs[:], in_=rs[:], func=AF.Exp, scale=-0.5), reads=[Brs], writes=[Brs])

                        def e4():
                            op("dve", lambda e: e.scalar_tensor_tensor(out=yo[:], in0=osb[:], scalar=gfx[:, 0, h:h + 1], in1=rs[:],
                                                                       op0=ALU.mult, op1=ALU.mult), reads=[Bosb, Brs, Bgfx], writes=[Byo])
                            dma("sp", YT[h * 64:(h + 1) * 64, qb * 512:(qb + 1) * 512], yo[:], reads=[Byo], writes=[B_YT])
                        defer(0, e2)
                        defer(1, e3)
                        defer(2, e4)

                f_loads(0)
                f_loads(1)
                nblk = len(blocks)
                f_A(blocks[0])
                for i in range(nblk + 4):
                    if i + 1 < nblk:
                        f_A(blocks[i + 1])
                    if i < nblk:
                        f_B(blocks[i])
                    run_deferred()
                    if 1 <= i <= nblk:
                        f_F(blocks[i - 1])
                while deferred:
                    run_deferred()
                fw.barrier()
            if stop_after == "P2":
                break
            with ExitStack() as st:
                prot_t = PRot([6, 7])
                gsb, Bgsb = load_T(st, "gsb", [sb_g[L].rearrange("(h c) -> h c", c=64)], 64, 64, prot_t)
                KAs = Rot(fw, st, nc, "KAs", [128, S], BF16, 2)
                QAs = Rot(fw, st, nc, "QAs", [128, 512], BF16, 4)
                for KA_, BKA_ in KAs.items:
                    op("dve", lambda e, KA_=KA_: e.tensor_copy(KA_[0:64, :], sel[:, 0:NT, :].rearrange("k a s -> k (a s)")),
                       reads=[B_C], writes=[BKA_])
                VAs = Rot(fw, st, nc, "VAs", [128, NT, 64], BF16, 2)
                Efs = Rot(fw, st, nc, "Ef", [128, 512], F32, 2)
                SPAs = [[(sb(st, f"SPA{r}_{k}", [128, 512], BF16), fw.buf(f"SPA{r}_{k}")) for k in range(NT)] for r in range(3)]
                Wbs = Rot(fw, st, nc, "Wsb", [128, 512], BF16, 4)
                rtmp = Rot(fw, st, nc, "rtmp", [32, 512], F32, 2)
                sqs = Rot(fw, st, nc, "sqs", [64, 512], F32, 2)
                rss = Rot(fw, st, nc, "rss", [64, 512], F32, 2)
                yos = Rot(fw, st, nc, "yos", [64, 512], BF16, 2)
                osbs = Rot(fw, st, nc, "osbs", [64, 512], F32, 2)
                prot_z = PRot([0, 1, 2])
                prot_o = PRot([3, 4])
                prot_r = PRot([5, 6])
                prot_n = PRot([7])
                groups = [(h, qb) for h in range(PROF_HEADS) for qb in range(NB)]
                G = {}

                def s_loads(gi):
                    if gi >= len(groups):
                        return
                    h, qb = groups[gi]
                    d = {"h": h, "qb": qb, "nkb": 4 * qb + 4, "spa": SPAs[gi % 3]}
                    if qb == 0:
                        KA, BKA = KAs.next()
                        dma("sp", KA[64:128, :], KS[h], reads=[B_KS], writes=[BKA])
                        VA, BVA = VAs.next()
                        for q4 in range(4):
                            k0, k1 = q4 * NT // 4, (q4 + 1) * NT // 4
                            dma("sp", VA[:, k0:k1, :], VS[k0 * 128:k1 * 128, h * 64:(h + 1) * 64].rearrange("(kb p) c -> p kb c", p=128),
                                reads=[B_VS], writes=[BVA])
                        d.update(KA=KA, BKA=BKA, VA=VA, BVA=BVA)
                    else:
                        p = G[gi - 1]
                        d.update(KA=p["KA"], BKA=p["BKA"], VA=p["VA"], BVA=p["BVA"])
                    QA, BQA = QAs.next()
                    dma("sp", QA[64:128, :], QS[h, :, qb * 512:(qb + 1) * 512], reads=[B_QS], writes=[BQA])
                    d.update(QA=QA, BQA=BQA)
                    G[gi] = d

                def mk_blocks(gi, ps):
                    h, qb = groups[gi]
                    nkb = 4 * qb + 4
                    return [dict(ps=ps, gi=gi, kb=kb, first=(kb == 0), last=(kb == nkb - 1), c0=max(0, kb - 4 * qb) * 128,
                                 diag=(kb - 4 * qb >= 0)) for kb in range(nkb)]

                def merge(a, b):
                    out_, i, j = [], 0, 0
                    while i < len(a) or j < len(b):
                        if i < len(a) and (j >= len(b) or i * len(b) <= j * len(a)):
                            out_.append(a[i]); i += 1
                        else:
                            out_.append(b[j]); j += 1
                    return out_

                tasks = mk_blocks(0, 1)
                for gi in range(len(groups)):
                    nxt1 = mk_blocks(gi + 1, 1) if gi + 1 < len(groups) else [dict(ps=0), dict(ps=0)]
                    tasks += nxt1[:2] + merge(nxt1[2:], mk_blocks(gi, 2))
                deferred = []

                def defer(k, fn):
                    deferred.append([k, fn])

                def run_deferred():
                    todo = [d for d in deferred if d[0] <= 0]
                    for d in todo:
                        deferred.remove(d)
                    for d in deferred:
                        d[0] -= 1
                    for d in todo:
                        d[1]()

                def s_A(b):
                    if b["ps"] == 0:
                        return
                    g = G[b["gi"]]
                    c0, kb = b["c0"], b["kb"]
                    if b["ps"] == 1 and b["first"]:
                        s_loads(b["gi"] + 2)
                        g["RP"], g["BRP"] = prot_r.next()
                    if b["ps"] == 2 and b["first"]:
                        g["O"], g["BO"] = prot_o.next()
                    Zp, BZ = prot_z.next()
                    b["Zp"], b["BZ"] = Zp, BZ
                    r0 = 64 if b["ps"] == 1 else 0
                    fns = [lambda e: e.matmul(Zp[:, c0:512], lhsT=g["KA"][r0:128, kb * 128:(kb + 1) * 128], rhs=g["QA"][r0:128, c0:512],
                                              start=True, stop=True)]
                    if b["diag"]:
                        fns.append(lambda e: e.matmul(Zp[:, c0:c0 + 128], lhsT=negi[:], rhs=uge[:], start=False, stop=True,
                                                      skip_group_check=True))
                    reads = [g["BKA"], g["BQA"], B_C]
                    if b["ps"] == 2:
                        SPb, BSP = g["spa"][kb]
                        fns.append(lambda e: e.matmul(Zp[:, c0:512], lhsT=ntri[:], rhs=SPb[:, c0:512], start=False, stop=True,
                                                      skip_group_check=True))
                        reads += [BSP]
                    op("pe", fns, reads=reads, writes=[BZ])

                def s_B(b):
                    if b["ps"] == 0:
                        return
                    g = G[b["gi"]]
                    Zp, BZ, c0, kb = b["Zp"], b["BZ"], b["c0"], b["kb"]
                    if b["ps"] == 1:
                        Ef, BEf = Efs.next()
                        SPb, BSP = g["spa"][kb]
                        op("act", lambda e: e.activation(out=Ef[:, c0:512], in_=Zp[:, c0:512], func=AF.Exp), reads=[BZ], writes=[BEf])
                        op("act", lambda e: e.activation(out=SPb[:, c0:512], in_=Ef[:, c0:512], func=AF.Ln, bias=1.0),
                           reads=[BEf], writes=[BSP])
                    else:
                        Wt, BWt = Wbs.next()
                        b["Wt"], b["BWt"] = Wt, BWt
                        op("act", lambda e: e.activation(out=Wt[:, c0:512], in_=Zp[:, c0:512], func=AF.Exp), reads=[BZ], writes=[BWt])

                def s_C(b):
                    if b["ps"] == 0:
                        return
                    g = G[b["gi"]]
                    c0, kb = b["c0"], b["kb"]
                    if b["ps"] == 1:
                        RP, BRP = g["RP"], g["BRP"]
                        SPb, BSP = g["spa"][kb]
                        op("pe", lambda e: e.matmul(RP[0:32, c0:512], lhsT=ult[:, kb, :], rhs=SPb[:, c0:512], start=b["first"], stop=True,
                                                    skip_group_check=True), reads=[BSP, B_C], writes=[BRP], signal=b["last"])
                        if b["last"]:
                            rt, Brt = rtmp.next()
                            QA, BQA = g["QA"], g["BQA"]
                            op("dve", lambda e: e.tensor_copy(QA[0:32, :], RP[0:32, :]), reads=[BRP], writes=[BQA])
                            op("dve", lambda e: e.tensor_tensor(out=rt[:], in0=RP[0:32, :], in1=QA[0:32, :], op=ALU.subtract),
                               reads=[BRP, BQA], writes=[Brt])
                            op("dve", lambda e: e.tensor_copy(QA[32:64, :], rt[:]), reads=[Brt], writes=[BQA])
                    else:
                        O, BO = g["O"], g["BO"]
                        Wt, BWt = b["Wt"], b["BWt"]
                        op("pe", lambda e: e.matmul(O[0:64, c0:512], lhsT=g["VA"][:, kb, :], rhs=Wt[:, c0:512], start=b["first"], stop=True,
                                                    skip_group_check=True), reads=[g["BVA"], BWt], writes=[BO], signal=b["last"])
                        if b["last"]:
                            h, qb = g["h"], g["qb"]
                            sq, Bsq = sqs.next()
                            Np, BN = prot_n.next()
                            rs, Brs = rss.next()
                            yo, Byo = yos.next()
                            osb, Bosb = osbs.next()
                            op("act", lambda e: e.activation(out=sq[:], in_=O[0:64, :], func=AF.Square), reads=[BO], writes=[Bsq])
                            op("act", lambda e: e.copy(osb[:], O[0:64, :]), reads=[BO], writes=[Bosb])

                            def e2():
                                op("pe", lambda e: e.matmul(Np[0:64, :], lhsT=nwf[0:64, :], rhs=sq[0:64, :], start=True, stop=True),
                                   reads=[Bsq, B_C], writes=[BN])

                            def e3():
                                op("act", lambda e: e.activation(out=rs[:], in_=Np[0:64, :], func=AF.Ln, bias=EPS), reads=[BN], writes=[Brs])
                                op("act", lambda e: e.activation(out=rs[:], in_=rs[:], func=AF.Exp, scale=-0.5), reads=[Brs], writes=[Brs])

                            def e4():
                                op("dve", lambda e: e.scalar_tensor_tensor(out=yo[:], in0=osb[:], scalar=gsb[:, 0, h:h + 1], in1=rs[:],
                                                                           op0=ALU.mult, op1=ALU.mult), reads=[Bosb, Brs, Bgsb], writes=[Byo])
                                dma("sp", YT[512 + h * 64:512 + (h + 1) * 64, qb * 512:(qb + 1) * 512], yo[:], reads=[Byo], writes=[B_YT])
                            defer(0, e2)
                            defer(1, e3)
                            defer(2, e4)

                s_loads(0)
                s_loads(1)
                nt_ = len(tasks)
                s_A(tasks[0])
                for i in range(nt_ + 1):
                    if 1 <= i <= nt_:
                        s_C(tasks[i - 1])
                    if i + 1 < nt_:
                        s_A(tasks[i + 1])
                    if i < nt_:
                        s_B(tasks[i])
                    run_deferred()
                for _ in range(5):
                    run_deferred()
                fw.barrier()
            if stop_after == "P3":
                break
            with ExitStack() as st:
                abc, Babc = load_bc(st, "abc", a_log[L:L + 1, :], 16)
                op("act", lambda e: e.activation(out=abc[:], in_=abc[:], func=AF.Exp), reads=[Babc], writes=[Babc])
                op("dve", lambda e: e.tensor_scalar(out=abc[:], in0=abc[:], scalar1=-1.0, scalar2=None, op0=ALU.mult),
                   reads=[Babc], writes=[Babc])
                dbc, Bdbc = load_bc(st, "dbc", ssd_d[L:L + 1, :], 16)
                Dfull = sb(st, "Dfull", [128, 1024], F32); BDf = fw.buf("Dfull")
                op("dve", lambda e: e.tensor_copy(Dfull[:].rearrange("p (h c) -> p h c", c=64),
                                                   dbc[:].unsqueeze(2).to_broadcast([128, 16, 64])), reads=[Bdbc], writes=[BDf])
                NG, BNG = load_bc(st, "NG", ssd_ng[L:L + 1, :], 1024)
                prev = sb(st, "prev", [128, 2, 512], F32); Bprev = fw.buf("prev")
                prevb = sb(st, "prevb", [128, 2, 512], BF16); Bprevb = fw.buf("prevb")
                op("dve", lambda e: e.memset(prev[:], 0.0), writes=[Bprev])
                op("dve", lambda e: e.memset(prevb[:], 0.0), writes=[Bprevb])
                R3 = 3
                xss = Rot(fw, st, nc, "xs", [128, 1024], F32, R3)
                zss = Rot(fw, st, nc, "zs", [128, 1024], F32, R3)
                dtrs = Rot(fw, st, nc, "dtr", [128, 16], F32, R3)
                bts = Rot(fw, st, nc, "bt", [128, 2, 128], BF16, R3)
                cts = Rot(fw, st, nc, "ct", [128, 2, 128], BF16, R3)
                btks = Rot(fw, st, nc, "btk", [128, 2, 128], BF16, R3)
                dts = Rot(fw, st, nc, "dt", [128, 16], F32, R3)
                pass_ = Rot(fw, st, nc, "pas", [128, 32], F32, R3)
                dAs = Rot(fw, st, nc, "dA", [128, 16], F32, R3)
                dAbs = Rot(fw, st, nc, "dAb", [128, 16, 128], F32, R3)
                nacss = Rot(fw, st, nc, "nacs", [128, 16], F32, R3)
                eats = Rot(fw, st, nc, "eat", [128, 16], F32, R3)
                decs = Rot(fw, st, nc, "dec", [128, 16], F32, R3)
                des = Rot(fw, st, nc, "de", [128, 16], F32, R3)
                xcs = Rot(fw, st, nc, "xc", [128, 1024], BF16, R3)
                xds = Rot(fw, st, nc, "xd", [128, 1024], BF16, R3)
                cbs = Rot(fw, st, nc, "cb", [128, 128], F32, 2)
                segs = Rot(fw, st, nc, "seg", [128, 128], F32, 4)
                Mts = Rot(fw, st, nc, "Mt", [128, 4, 128], BF16, 2)
                yds = Rot(fw, st, nc, "ydsb", [128, 512], F32, 4)
                t1s = Rot(fw, st, nc, "t1", [128, 512], F32, 4)
                t2s = Rot(fw, st, nc, "t2", [128, 512], F32, 2)
                ssn = Rot(fw, st, nc, "ssn", [128, 1], F32, 2)
                jk = sb(st, "jk", [128, 512], BF16); Bjk = fw.buf("jk")
                yns = Rot(fw, st, nc, "yn", [128, 512], BF16, 4)
                yts = Rot(fw, st, nc, "ytt", [128, 4, 128], BF16, 2)
                prot_a = PRot([0, 1])
                prot_R = PRot([2, 3])
                prot_yd = PRot([4, 5])
                prot_yo = PRot([6])
                prot_st = PRot([7])
                CH = {}

                def ssd_A(c):
                    rows = slice(c * 128, (c + 1) * 128)
                    xs, Bxs = xss.next(); zs, Bzs = zss.next(); dtr, Bdtr = dtrs.next()
                    bt, Bbt = bts.next(); ct, Bct = cts.next(); btk, Bbtk = btks.next()
                    dt, Bdt = dts.next(); dA, BdA = dAs.next(); dAb, BdAb = dAbs.next()
                    pas, Bpas = pass_.next(); nacs, Bnacs = nacss.next(); eat, Beat = eats.next()
                    dec, Bdec = decs.next(); de, Bde = des.next(); xc, Bxc = xcs.next(); xd, Bxd = xds.next()
                    CH[c] = dict(rows=rows, xs=xs, Bxs=Bxs, zs=zs, Bzs=Bzs, bt=bt, Bbt=Bbt, ct=ct, Bct=Bct, btk=btk, Bbtk=Bbtk,
                                 dAb=dAb, BdAb=BdAb, nacs=nacs, Bnacs=Bnacs, eat=eat, Beat=Beat, dec=dec, Bdec=Bdec,
                                 xc=xc, Bxc=Bxc, xd=xd, Bxd=Bxd, yd=[None, None])
                    st_ = {}

                    def a0():
                        dma("sp", xs[:], XS[rows, :], reads=[B_XS], writes=[Bxs])
                        dma("sp", zs[:], ZS[rows, :], reads=[B_ZS], writes=[Bzs])
                        dma("sp", dtr[:], DTS[rows, :], reads=[B_DTS], writes=[Bdtr])
                        dma("sp", bt[:], BT[:, :, rows].rearrange("g n t -> n g t"), reads=[B_BT], writes=[Bbt])
                        dma("sp", ct[:], CT[:, :, rows].rearrange("g n t -> n g t"), reads=[B_CT], writes=[Bct])
                        dma("sp", btk[:], BTOK[rows, :, :], reads=[B_BTOK], writes=[Bbtk])
                        op("act", lambda e: e.activation(out=dt[:], in_=dtr[:], func=AF.Exp), reads=[Bdtr], writes=[Bdt])
                        op("act", lambda e: e.activation(out=dt[:], in_=dt[:], func=AF.Ln, bias=1.0), reads=[Bdt], writes=[Bdt])

                    def a1():
                        op("dve", lambda e: e.tensor_tensor(out=dA[:], in0=dt[:], in1=abc[:], op=ALU.mult), reads=[Bdt, Babc], writes=[BdA])
                        op("dve", lambda e: e.tensor_copy(dAb[:], dA[:].unsqueeze(2).to_broadcast([128, 16, 128])),
                           reads=[BdA], writes=[BdAb])

                    def a2():
                        pa, Bpa = prot_a.next()
                        st_["pa"], st_["Bpa"] = pa, Bpa
                        op("pe", [lambda e: e.matmul(pa[:, 0:16], lhsT=tle[:], rhs=dA[:], start=True, stop=True),
                                  lambda e: e.matmul(pa[:, 16:32], lhsT=onesf[:], rhs=dA[:], start=True, stop=True)],
                           reads=[BdA, B_C], writes=[Bpa])

                    def a3():
                        pa, Bpa = st_["pa"], st_["Bpa"]
                        op("act", lambda e: e.copy(pas[:], pa[:, 0:32]), reads=[Bpa], writes=[Bpas])

                    def a4():
                        op("dve", lambda e: e.tensor_scalar(out=nacs[:], in0=pas[:, 0:16], scalar1=-1.0, scalar2=None, op0=ALU.mult),
                           reads=[Bpas], writes=[Bnacs])
                        op("dve", lambda e: e.tensor_tensor(out=de[:], in0=pas[:, 16:32], in1=nacs[:], op=ALU.add),
                           reads=[Bpas, Bnacs], writes=[Bde])

                    def a5():
                        op("act", lambda e: e.activation(out=eat[:], in_=pas[:, 0:16], func=AF.Exp), reads=[Bpas], writes=[Beat])
                        op("act", lambda e: e.activation(out=dec[:], in_=pas[:, 16:32], func=AF.Exp), reads=[Bpas], writes=[Bdec])
                        op("act", lambda e: e.activation(out=de[:], in_=de[:], func=AF.Exp), reads=[Bde], writes=[Bde])

                    def a6():
                        op("dve", lambda e: e.tensor_tensor(out=de[:], in0=de[:], in1=dt[:], op=ALU.mult), reads=[Bde, Bdt], writes=[Bde])
                        op("dve", lambda e: e.tensor_tensor(
                            out=xc[:].rearrange("p (h c) -> p h c", c=64), in0=xs[:].rearrange("p (h c) -> p h c", c=64),
                            in1=dt[:].unsqueeze(2).to_broadcast([128, 16, 64]), op=ALU.mult), reads=[Bxs, Bdt], writes=[Bxc])
                        op("dve", lambda e: e.tensor_tensor(
                            out=xd[:].rearrange("p (h c) -> p h c", c=64), in0=xs[:].rearrange("p (h c) -> p h c", c=64),
                            in1=de[:].unsqueeze(2).to_broadcast([128, 16, 64]), op=ALU.mult), reads=[Bxs, Bde], writes=[Bxd])
                    return [a0, a1, a2, a3, a4, a5, a6]

                a_steps = []

                def tick():
                    if a_steps:
                        a_steps.pop(0)()

                def ssd_B(c):
                    d = CH[c]
                    bt, Bbt, ct, Bct, dAb, BdAb, nacs, Bnacs, xc, Bxc = (d[k] for k in (
                        "bt", "Bbt", "ct", "Bct", "dAb", "BdAb", "nacs", "Bnacs", "xc", "Bxc"))
                    cbl = []
                    for g in range(2):
                        pc, Bpc = prot_a.next()
                        op("pe", lambda e, pc=pc, g=g: e.matmul(pc[:, 0:128], lhsT=bt[:, g, :], rhs=ct[:, g, :], start=True, stop=True),
                           reads=[Bbt, Bct], writes=[Bpc])
                        cb, Bcb = cbs.next()
                        op("act", lambda e, cb=cb, pc=pc: e.copy(cb[:], pc[:, 0:128]), reads=[Bpc], writes=[Bcb])
                        cbl.append((cb, Bcb))
                    tick()
                    Yds = [prot_yd.next(), prot_yd.next()]
                    units = [(g, hq) for g in range(2) for hq in range(2)]

                    def emit_R(g, hq):
                        R, BR = prot_R.next()
                        fns = []
                        for h4 in range(4):
                            h = g * 8 + hq * 4 + h4
                            fns.append(lambda e, R=R, h4=h4, h=h: e.matmul(
                                R[:, h4 * 128:(h4 + 1) * 128], lhsT=dAb[:, h, :], rhs=tle[:], start=True, stop=False))
                            fns.append(lambda e, R=R, h4=h4: e.matmul(
                                R[:, h4 * 128:(h4 + 1) * 128], lhsT=negi[:], rhs=ustr[:], start=False, stop=True))
                        op("pe", fns, reads=[BdAb, B_C], writes=[BR])
                        return R, BR

                    def emit_rest(g, hq, R, BR):
                        cb, Bcb = cbl[g]
                        Yd, BYd = Yds[g]
                        Mt, BMt = Mts.next()
                        for h4 in range(4):
                            h = g * 8 + hq * 4 + h4
                            seg, Bseg = segs.next()
                            op("act", lambda e, seg=seg, h4=h4, h=h: e.activation(
                                out=seg[:], in_=R[:, h4 * 128:(h4 + 1) * 128], func=AF.Exp, bias=nacs[:, h:h + 1]),
                               reads=[BR, Bnacs], writes=[Bseg])
                            op("dve", lambda e, seg=seg, h4=h4: e.tensor_tensor(
                                out=Mt[:, h4, :], in0=seg[:], in1=cb[:], op=ALU.mult), reads=[Bseg, Bcb], writes=[BMt])
                        op("pe", [lambda e, h4=h4: e.matmul(
                            Yd[:, (hq * 4 + h4) * 64:(hq * 4 + h4 + 1) * 64], lhsT=Mt[:, h4, :],
                            rhs=xc[:, (g * 8 + hq * 4 + h4) * 64:(g * 8 + hq * 4 + h4 + 1) * 64], start=True, stop=True)
                            for h4 in range(4)], reads=[BMt, Bxc], writes=[BYd])
                        if hq == 1:
                            yd, Byd = yds.next()
                            op("act", lambda e: e.copy(yd[:], Yd[:, :]), reads=[BYd], writes=[Byd])
                            d["yd"][g] = (yd, Byd)

                    cur = emit_R(*units[0])
                    for i, (g, hq) in enumerate(units):
                        nxt = emit_R(*units[i + 1]) if i + 1 < len(units) else None
                        emit_rest(g, hq, *cur)
                        cur = nxt
                        tick()

                def ssd_C(c):
                    d = CH[c]
                    rows = d["rows"]
                    xs, Bxs, zs, Bzs, ct, Bct, btk, Bbtk, eat, Beat, dec, Bdec, xd, Bxd = (d[k] for k in (
                        "xs", "Bxs", "zs", "Bzs", "ct", "Bct", "btk", "Bbtk", "eat", "Beat", "dec", "Bdec", "xd", "Bxd"))
                    mm = []
                    for g in range(2):
                        Yo, BYo = prot_yo.next()
                        op("pe", lambda e, Yo=Yo, g=g: e.matmul(Yo[:, :], lhsT=ct[:, g, :], rhs=prevb[:, g, :], start=True, stop=True),
                           reads=[Bct, Bprevb], writes=[BYo])
                        St, BSt = prot_st.next()
                        op("pe", lambda e, St=St, g=g: e.matmul(St[:, :], lhsT=btk[:, g, :], rhs=xd[:, g * 512:(g + 1) * 512],
                                                                start=True, stop=True), reads=[Bbtk, Bxd], writes=[BSt])
                        t1, Bt1 = t1s.next()
                        op("dve", lambda e, t1=t1, Yo=Yo, g=g: e.tensor_tensor(
                            out=t1[:].rearrange("p (h c) -> p h c", c=64), in0=Yo[:, :].rearrange("p (h c) -> p h c", c=64),
                            in1=eat[:, g * 8:(g + 1) * 8].unsqueeze(2).to_broadcast([128, 8, 64]), op=ALU.mult),
                           reads=[BYo, Beat], writes=[Bt1])
                        op("dve", lambda e, g=g: e.tensor_tensor(
                            out=prev[:, g, :].rearrange("p (h c) -> p h c", c=64), in0=prev[:, g, :].rearrange("p (h c) -> p h c", c=64),
                            in1=dec[:, g * 8:(g + 1) * 8].unsqueeze(2).to_broadcast([128, 8, 64]), op=ALU.mult),
                           reads=[Bprev, Bdec], writes=[Bprev])
                        op("dve", lambda e, St=St, g=g: e.tensor_tensor(out=prev[:, g, :], in0=prev[:, g, :], in1=St[:, :], op=ALU.add),
                           reads=[Bprev, BSt], writes=[Bprev])
                        op("act", lambda e, g=g: e.copy(prevb[:, g, :], prev[:, g, :]), reads=[Bprev], writes=[Bprevb])
                        mm.append((t1, Bt1))
                    tick()
                    outs = []
                    for g in range(2):
                        yd, Byd = d["yd"][g]
                        t1, Bt1 = mm[g]
                        op("dve", lambda e, t1=t1, yd=yd: e.tensor_tensor(out=t1[:], in0=t1[:], in1=yd[:], op=ALU.add),
                           reads=[Bt1, Byd], writes=[Bt1])
                        t2, Bt2 = t2s.next()
                        op("dve", lambda e, t2=t2, g=g: e.tensor_tensor(out=t2[:], in0=xs[:, g * 512:(g + 1) * 512],
                                                                        in1=Dfull[:, g * 512:(g + 1) * 512], op=ALU.mult),
                           reads=[Bxs, BDf], writes=[Bt2])
                        op("dve", lambda e, t1=t1, t2=t2: e.tensor_tensor(out=t1[:], in0=t1[:], in1=t2[:], op=ALU.add),
                           reads=[Bt1, Bt2], writes=[Bt1])
                        op("dve", lambda e, t1=t1, g=g: e.tensor_tensor(out=t1[:], in0=t1[:], in1=zs[:, g * 512:(g + 1) * 512], op=ALU.mult),
                           reads=[Bt1, Bzs], writes=[Bt1])
                        sn, Bsn = ssn.next()
                        op("act", lambda e, t1=t1, sn=sn: e.activation(out=jk[:], in_=t1[:], func=AF.Square, accum_out=sn[:]),
                           reads=[Bt1], writes=[Bjk, Bsn])
                        op("act", lambda e, sn=sn: e.activation(out=sn[:], in_=sn[:], func=AF.Ln, scale=1.0 / 512, bias=EPS),
                           reads=[Bsn], writes=[Bsn])
                        op("act", lambda e, sn=sn: e.activation(out=sn[:], in_=sn[:], func=AF.Exp, scale=-0.5), reads=[Bsn], writes=[Bsn])
                        yn, Byn = yns.next()
                        op("dve", lambda e, yn=yn, t1=t1, sn=sn, g=g: e.scalar_tensor_tensor(
                            out=yn[:], in0=t1[:], scalar=sn[:, 0:1], in1=NG[:, g * 512:(g + 1) * 512], op0=ALU.mult, op1=ALU.mult),
                           reads=[Bt1, Bsn, BNG], writes=[Byn])
                        outs.append((g, yn, Byn))
                    del CH[c]

                    def tail():
                        for g, yn, Byn in outs:
                            pt, Bpt = prot_a.next()
                            ptb = pt[:].bitcast(BF16)
                            op("pe", [lambda e, ptb=ptb, yn=yn, q=q: e.transpose(ptb[:, q * 128:(q + 1) * 128], yn[:, q * 128:(q + 1) * 128],
                                                                               identb[:]) for q in range(4)], reads=[Byn, B_C], writes=[Bpt])
                            ytt, Bytt = yts.next()
                            op("act", lambda e, ytt=ytt, ptb=ptb: e.copy(ytt[:], ptb[:, 0:512].rearrange("p (q t) -> p q t", t=128)),
                               reads=[Bpt], writes=[Bytt])
                            dma("sp", YT[1024 + g * 512:1024 + (g + 1) * 512, rows].rearrange("(q p) t -> p q t", p=128), ytt[:],
                                reads=[Bytt], writes=[B_YT])
                    return tail

                for f in ssd_A(0):
                    f()
                if NT > 1:
                    for f in ssd_A(1):
                        f()
                ssd_B(0)
                tail_prev = None
                for c in range(NT):
                    if c + 2 < NT:
                        a_steps.extend(ssd_A(c + 2))
                        tick()
                    if c + 1 < NT:
                        ssd_B(c + 1)
                    tl = ssd_C(c)
                    while a_steps:
                        tick()
                    if tail_prev is not None:
                        tail_prev()
                    tail_prev = tl
                tail_prev()
                fw.barrier()
            if stop_after == "P4":
                break
            with ExitStack() as st:
                Wo, BWo = load_weight(st, "Wo", w_out[L], 2048, D)
                yts = Rot(fw, st, nc, "yT", [128, 16, 512], BF16, 2)
                xts = Rot(fw, st, nc, "xt5", [128, D], F32, 2)
                x1s = Rot(fw, st, nc, "x1t", [128, D], F32, 2)
                prot = PRot([0, 1, 2, 3])
                def load_yT(tb):
                    tsl_ = slice(tb * 512, (tb + 1) * 512)
                    yT_, ByT_ = yts.next()
                    for eh in range(4):
                        dma("sp", yT_[:, eh * 4:(eh + 1) * 4, :], YT[eh * 512:(eh + 1) * 512, tsl_].rearrange("(e p) t -> p e t", p=128),
                            reads=[B_YT], writes=[ByT_])
                    return yT_, ByT_

                nxt_y = load_yT(0)
                for tb in range(NB):
                    tsl = slice(tb * 512, (tb + 1) * 512)
                    yT, ByT = nxt_y
                    if tb + 1 < NB:
                        nxt_y = load_yT(tb + 1)
                    for tt in range(4):
                        ti = tb * 4 + tt
                        rows = slice(ti * 128, (ti + 1) * 128)
                        xt, Bx = xts.next()
                        dma("sp", xt[:], x_cur[rows, :], reads=[B_xcur], writes=[Bx])
                        x1t, Bx1 = x1s.next()
                        for half in range(2):
                            ps, bps = prot.next()
                            op("pe", [lambda e, ps=ps, e_=e_, tt=tt, half=half, yT=yT: e.matmul(
                                ps[:, :], lhsT=yT[:, e_, tt * 128:(tt + 1) * 128], rhs=Wo[:, e_, half * 512:(half + 1) * 512],
                                start=(e_ == 0), stop=(e_ == 15)) for e_ in range(16)], reads=[ByT, BWo], writes=[bps])
                            op("dve", lambda e, ps=ps, x1t=x1t, xt=xt, half=half: e.tensor_tensor(
                                out=x1t[:, half * 512:(half + 1) * 512], in0=ps[:, :], in1=xt[:, half * 512:(half + 1) * 512], op=ALU.add),
                               reads=[bps, Bx], writes=[Bx1])
                        dma("sp", X1[rows, :], x1t[:], reads=[Bx1], writes=[B_X1])
                fw.barrier()
            if stop_after == "P5a":
                break
            with ExitStack() as st:
                prot_t = PRot([6, 7])
                g2T, Bg2 = load_T(st, "g2T", [ffn_g[L].rearrange("(a b) -> a b", b=128)], 128, 128, prot_t)
                fcT, Bfc = load_T(st, "fcT", [fconv_w[L], fconv_b[L:L + 1, :]], 2 * DFF, 128, prot_t)
                Wu, BWu = load_weight(st, "Wu", w_up[L], D, 2 * DFF, gT=g2T[:, 0, :], Bg=Bg2, piece=512)
                xts = Rot(fw, st, nc, "xt6", [128, D], F32, 2)
                junk = sb(st, "junk6", [128, D], BF16); Bjunk = fw.buf("junk6")
                sss = Rot(fw, st, nc, "ss6", [128, 1], F32, 2)
                hbs = Rot(fw, st, nc, "hb6", [128, D], BF16, 2)
                hT = sb(st, "hT6", [128, 8, 512], BF16); BhT = fw.buf("hT6")
                gts = Rot(fw, st, nc, "gt6", [128, 512], BF16, 3)
                Us = Rot(fw, st, nc, "U6", [128, 514], F32, 2)
                accs = Rot(fw, st, nc, "acc6", [128, 512], F32, 6)
                ffn_pend = []
                sgs = Rot(fw, st, nc, "sg6", [128, 512], F32, 2)
                halo2 = sb(st, "halo2", [128, 44, 2], F32); Bh2 = fw.buf("halo2")
                op("dve", lambda e: e.memset(halo2[:], 0.0), writes=[Bh2])
                prot_u = PRot([0, 1, 2])
                for tb in range(NB):
                    tsl = slice(tb * 512, (tb + 1) * 512)
                    for tt in range(4):
                        ti = tb * 4 + tt
                        xt, Bx = xts.next()
                        dma("sp", xt[:], X1[ti * 128:(ti + 1) * 128, :], reads=[B_X1], writes=[Bx])
                        ss, Bss = sss.next()
                        hb, Bhb = hbs.next()
                        rmsnorm_T(None, xt, Bx, ss, Bss, junk, Bjunk, hb, Bhb, hT, BhT, tt, prot_t)
                    for c in range(22):
                        pair = []
                        for cc in (c, c + 22):
                            ps, bps = prot_u.next()
                            op("pe", [lambda e, ps=ps, k=k, cc=cc: e.matmul(ps[:, :], lhsT=Wu[:, k, cc * 128:(cc + 1) * 128], rhs=hT[:, k, :],
                                                                          start=(k == 0), stop=(k == 7)) for k in range(8)],
                               reads=[BWu, BhT], writes=[bps])
                            U, BU = Us.next()
                            op("act", lambda e, U=U, ps=ps: e.copy(U[:, 2:514], ps[:, :]), reads=[bps], writes=[BU])
                            op("act", lambda e, U=U, cc=cc: e.copy(U[:, 0:2], halo2[:, cc, :]), reads=[Bh2], writes=[BU])
                            op("act", lambda e, U=U, cc=cc: e.copy(halo2[:, cc, :], U[:, 512:514]), reads=[BU], writes=[Bh2])
                            acc, Bacc = accs.next()
                            op("act", lambda e, ps=ps, acc=acc, cc=cc: e.activation(
                                out=acc[:], in_=ps[:, :], func=AF.Identity, scale=fcT[:, cc, 2:3], bias=fcT[:, cc, 3:4]),
                               reads=[bps, Bfc], writes=[Bacc])
                            for kk in (1, 0):
                                op("dve", lambda e, U=U, acc=acc, cc=cc, kk=kk: e.scalar_tensor_tensor(
                                    out=acc[:], in0=U[:, kk:kk + 512], scalar=fcT[:, cc, kk:kk + 1], in1=acc[:],
                                    op0=ALU.mult, op1=ALU.add), reads=[BU, Bfc, Bacc], writes=[Bacc])
                            pair.append((acc, Bacc))
                        (ag, Bag), (av, Bav) = pair

                        def ffn_tail(c=c, ag=ag, Bag=Bag, av=av, Bav=Bav, tsl=tsl):
                            sg, Bsg = sgs.next()
                            op("act", lambda e: e.activation(out=sg[:], in_=ag[:], func=AF.Silu), reads=[Bag], writes=[Bsg])
                            gt, Bgt = gts.next()
                            op("dve", lambda e: e.tensor_tensor(out=gt[:], in0=sg[:], in1=av[:], op=ALU.mult),
                               reads=[Bsg, Bav], writes=[Bgt])
                            dma("sp", GTS[c, :, tsl], gt[:], reads=[Bgt], writes=[B_GTS])
                        if ffn_pend:
                            ffn_pend.pop(0)()
                        ffn_pend.append(ffn_tail)
                while ffn_pend:
                    ffn_pend.pop(0)()
                fw.barrier()
            with ExitStack() as st:
                if last:
                    FG, BFG = load_bc(st, "FG", fin_g.rearrange("(a b) -> a b", a=1), D)
                Wd, BWd = load_weight(st, "Wd", w_down[L], DFF, D, piece=512)
                xts = Rot(fw, st, nc, "xt7", [128, D], F32, 2)
                x2s = Rot(fw, st, nc, "x2t", [128, D], F32, 2)
                GTs = Rot(fw, st, nc, "GT", [128, 22, 512], BF16, 2)
                junk = sb(st, "junk7", [128, D], BF16); Bjunk = fw.buf("junk7")
                sss = Rot(fw, st, nc, "ss7", [128, 1], F32, 2)
                prot_d = PRot([0, 1, 2, 3])
                def load_GT(tb):
                    tsl_ = slice(tb * 512, (tb + 1) * 512)
                    GT_, BGT_ = GTs.next()
                    for c4 in range(0, 22, 2):
                        dma("sp", GT_[:, c4:c4 + 2, :], GTS[c4:c4 + 2, :, tsl_].rearrange("c p t -> p c t"), reads=[B_GTS], writes=[BGT_])
                    return GT_, BGT_

                nxt_g = load_GT(0)
                for tb in range(NB):
                    tsl = slice(tb * 512, (tb + 1) * 512)
                    GT, BGT = nxt_g
                    if tb + 1 < NB:
                        nxt_g = load_GT(tb + 1)
                    for tt in range(4):
                        ti = tb * 4 + tt
                        rows = slice(ti * 128, (ti + 1) * 128)
                        xt, Bx = xts.next()
                        dma("sp", xt[:], X1[rows, :], reads=[B_X1], writes=[Bx])
                        x2t, Bx2 = x2s.next()
                        for half in range(2):
                            ps, bps = prot_d.next()
                            op("pe", [lambda e, ps=ps, c=c, tt=tt, half=half: e.matmul(
                                ps[:, :], lhsT=GT[:, c, tt * 128:(tt + 1) * 128], rhs=Wd[:, c, half * 512:(half + 1) * 512],
                                start=(c == 0), stop=(c == 21)) for c in range(22)], reads=[BGT, BWd], writes=[bps])
                            op("dve", lambda e, ps=ps, x2t=x2t, xt=xt, half=half: e.tensor_tensor(
                                out=x2t[:, half * 512:(half + 1) * 512], in0=ps[:, :], in1=xt[:, half * 512:(half + 1) * 512], op=ALU.add),
                               reads=[bps, Bx], writes=[Bx2])
                        if not last:
                            dma("sp", X2[rows, :], x2t[:], reads=[Bx2], writes=[B_X2])
                        else:
                            ss, Bss = sss.next()
                            op("act", lambda e, x2t=x2t, ss=ss: e.activation(out=junk[:], in_=x2t[:], func=AF.Square, accum_out=ss[:]),
                               reads=[Bx2], writes=[Bjunk, Bss])
                            op("act", lambda e, ss=ss: e.activation(out=ss[:], in_=ss[:], func=AF.Ln, scale=1.0 / D, bias=EPS),
                               reads=[Bss], writes=[Bss])
                            op("act", lambda e, ss=ss: e.activation(out=ss[:], in_=ss[:], func=AF.Exp, scale=-0.5), reads=[Bss], writes=[Bss])
                            op("dve", lambda e, x2t=x2t, ss=ss: e.scalar_tensor_tensor(
                                out=x2t[:], in0=x2t[:], scalar=ss[:, 0:1], in1=FG[:], op0=ALU.mult, op1=ALU.mult),
                               reads=[Bx2, Bss, BFG], writes=[Bx2])
                            dma("sp", out[rows, :], x2t[:], reads=[Bx2], writes=[B_OUT])
                fw.barrier()
            x_cur = X2
            B_xcur = B_X2
        fw.finish([B_OUT, B_QF, B_KF, B_VF, B_QS, B_KS, B_VS, B_ZS, B_XS, B_DTS, B_BT, B_CT, B_BTOK, B_YT, B_X1, B_X2])
    return nc


_INPUT_NAMES = ["x", "mix_norm_g", "w_in", "fox_f_bias", "fox_out_g", "sb_out_g", "ssd_conv_w", "ssd_conv_b",
                "ssd_dt_bias", "ssd_a_log", "ssd_d", "ssd_norm_g", "w_out", "ffn_norm_g", "w_up", "ffn_conv_w",
                "ffn_conv_b", "w_down", "final_norm_g"]


def kernel(**inputs):
    nc = build(depth=2)
    shared = {k: np.ascontiguousarray(np.asarray(inputs[k], dtype=np.float32)) for k in _INPUT_NAMES if k != "x"}
    x = np.asarray(inputs["x"], dtype=np.float32)
    in_maps = []
    for c in range(8):
        m = dict(shared)
        m["x"] = np.ascontiguousarray(x[c])
        in_maps.append(m)
    res = run_bass_kernel_spmd(nc, in_maps, core_ids=list(range(8)))
    return np.stack([np.asarray(r["out"], dtype=np.float32) for r in res.results], axis=0)
```

```python
import numpy as np
import concourse.bass as bass
import concourse.mybir as mybir
from concourse.bass_utils import run_bass_kernel_spmd
from contextlib import ExitStack

F32 = mybir.dt.float32
BF16 = mybir.dt.bfloat16
AF = mybir.ActivationFunctionType
ALU = mybir.AluOpType

S = 4096
D = 1024
NT = S // 128
NB = S // 512
NIN = 5656
DFF = 2816
C_FQ, C_FK, C_FV, C_FF, C_SQ, C_SK, C_SV, C_Z, C_XBC, C_DT = 0, 512, 1024, 1536, 1544, 2056, 2568, 3080, 4104, 5640
EPS = 1e-6
NEG = -30000.0
PROF_HEADS = 8


class Buf:
    def __init__(self, name):
        self.name = name
        self.w = None
        self.r = {}
        self.slot = None
        self.persist = False


class EngS:
    def __init__(self, name, eng, sem, is_pe=False):
        self.name, self.eng, self.sem = name, eng, sem
        self.count = 0
        self.waited = {}
        self.is_pe = is_pe

    def wait(self, ev):
        if ev is None:
            return
        sem, val = ev
        if self.is_pe and sem is self.sem:
            return
        if self.waited.get(id(sem), 0) < val:
            self.eng.wait_ge(sem, val)
            self.waited[id(sem)] = val


class FW:
    def __init__(self, nc, es):
        self.nc, self.es = nc, es
        self.E = {}
        for name, eng, pe in (("pe", nc.tensor, True), ("act", nc.scalar, False),
                              ("dve", nc.vector, False), ("pool", nc.gpsimd, False),
                              ("sp", nc.sync, False)):
            sem = es.enter_context(nc.semaphore("s_" + name))
            self.E[name] = EngS(name, eng, sem, pe)
        self.nbuf = 0
        self.dbufs = []
        self.slots = []
        self.free_slots = []

    def buf(self, name=None, persist=False):
        self.nbuf += 1
        b = Buf((name or "b") + f"_{self.nbuf}")
        b.persist = persist
        return b

    def _pre(self, E, reads, writes):
        for b in reads:
            E.wait(b.w)
        for b in writes:
            E.wait(b.w)
            for ev in list(b.r.values()):
                E.wait(ev)

    def _post(self, ev, reads, writes):
        for b in reads:
            b.r[id(ev[0])] = ev
        for b in writes:
            b.w = ev
            b.r = {}

    def op(self, en, fns, reads=(), writes=(), signal=True):
        E = self.E[en]
        self._pre(E, reads, writes)
        if not isinstance(fns, (list, tuple)):
            fns = [fns]
        ins = None
        for f in fns:
            ins = f(E.eng)
        if signal:
            E.count += 1
            ins.then_inc(E.sem, 1)
            self._post((E.sem, E.count), reads, writes)
        else:
            self._post((E.sem, E.count + 1), reads, writes)

    def dma(self, qn, out, in_, reads=(), writes=(), **kw):
        Q = self.E[qn]
        self._pre(Q, reads, writes)
        d = writes[0]
        if d.slot is None:
            if self.free_slots:
                d.slot = self.free_slots.pop()
            else:
                d.slot = [self.es.enter_context(self.nc.semaphore(f"dq{len(self.slots)}")), 0]
                self.slots.append(d.slot)
            self.dbufs.append(d)
        d.slot[1] += 16
        Q.eng.dma_start(out=out, in_=in_, **kw).then_inc(d.slot[0], 16)
        self._post((d.slot[0], d.slot[1]), reads, writes)

    def barrier(self):
        evs = [(E.sem, E.count) for E in self.E.values() if E.count > 0]
        evs += [(sl[0], sl[1]) for sl in self.slots]
        for E in self.E.values():
            for ev in evs:
                E.wait(ev)
        keep = []
        for b in self.dbufs:
            if b.persist:
                keep.append(b)
            else:
                self.free_slots.append(b.slot)
                b.slot = None
        self.dbufs = keep

    def finish(self, bufs):
        E = self.E["sp"]
        for b in bufs:
            E.wait(b.w)
            for ev in list(b.r.values()):
                E.wait(ev)


class Rot:
    def __init__(self, fw, es, nc, name, shape, dt, n, psum=False):
        self.items = []
        for i in range(n):
            fw.nbuf += 1
            if psum:
                t = es.enter_context(nc.psum_tensor(f"{name}{i}_u{fw.nbuf}", shape, dt))
            else:
                t = es.enter_context(nc.sbuf_tensor(f"{name}{i}_u{fw.nbuf}", shape, dt))
            self.items.append((t, fw.buf(f"{name}{i}")))
        self.i = 0

    def next(self):
        it = self.items[self.i % len(self.items)]
        self.i += 1
        return it


def build(depth=2, stop_after=None, dbg=False, seq=4096):
    global S, NT, NB
    S, NT, NB = seq, seq // 128, seq // 512
    nc = bass.Bass("TRN2", target_bir_lowering=False)
    skind = "ExternalOutput" if dbg else "Internal"

    def din(name, shape):
        return nc.dram_tensor(name, list(shape), F32, kind="ExternalInput").ap()

    x_in = din("x", [S, D])
    mix_g = din("mix_norm_g", [depth, D])
    w_in = din("w_in", [depth, D, NIN])
    fox_fb = din("fox_f_bias", [depth, 8])
    fox_g = din("fox_out_g", [depth, 512])
    sb_g = din("sb_out_g", [depth, 512])
    conv_w = din("ssd_conv_w", [depth, 4, 1536])
    conv_b = din("ssd_conv_b", [depth, 1536])
    dt_bias = din("ssd_dt_bias", [depth, 16])
    a_log = din("ssd_a_log", [depth, 16])
    ssd_d = din("ssd_d", [depth, 16])
    ssd_ng = din("ssd_norm_g", [depth, 1024])
    w_out = din("w_out", [depth, 2048, D])
    ffn_g = din("ffn_norm_g", [depth, D])
    w_up = din("w_up", [depth, D, 2 * DFF])
    fconv_w = din("ffn_conv_w", [depth, 3, 2 * DFF])
    fconv_b = din("ffn_conv_b", [depth, 2 * DFF])
    w_down = din("w_down", [depth, DFF, D])
    fin_g = din("final_norm_g", [D])
    out = nc.dram_tensor("out", [S, D], F32, kind="ExternalOutput").ap()

    def scr(name, shape, dt):
        return nc.dram_tensor(name, list(shape), dt, kind=skind).ap()

    QF = scr("QF", [8, 70, S], BF16)
    KF = scr("KF", [8, 70, S], BF16)
    VF = scr("VF", [S, 8, 65], BF16)
    QS = scr("QS", [8, 64, S], BF16)
    KS = scr("KS", [8, 64, S], BF16)
    VS = scr("VS", [S, 512], BF16)
    ZS = scr("ZS", [S, 1024], F32)
    XS = scr("XS", [S, 1024], F32)
    DTS = scr("DTS", [S, 16], F32)
    BT = scr("BT", [2, 128, S], BF16)
    CT = scr("CT", [2, 128, S], BF16)
    BTOK = scr("BTOK", [S, 2, 128], BF16)
    YT = scr("YT", [2048, S], BF16)
    X1 = scr("X1", [S, D], F32)
    X2 = scr("X2", [S, D], F32)
    GTS = scr("GTS", [22, 128, S], BF16)

    with ExitStack() as es:
        fw = FW(nc, es)
        op, dma = fw.op, fw.dma

        def sb(st, name, shape, dt):
            fw.nbuf += 1
            return st.enter_context(nc.sbuf_tensor(f"{name}_u{fw.nbuf}", list(shape), dt))

        B_QF, B_KF, B_VF, B_QS, B_KS, B_VS = (fw.buf(n, True) for n in ("QF", "KF", "VF", "QS", "KS", "VS"))
        B_ZS, B_XS, B_DTS, B_BT, B_CT, B_BTOK = (fw.buf(n, True) for n in ("ZS", "XS", "DTS", "BT", "CT", "BTOK"))
        B_YT, B_X1, B_X2, B_OUT, B_GTS = (fw.buf(n, True) for n in ("YT", "X1", "X2", "OUT", "GTS"))

        PS = [es.enter_context(nc.psum_tensor(f"ps{i}", [128, 512], F32)) for i in range(8)]
        BPS = [fw.buf(f"ps{i}") for i in range(8)]

        class PRot:
            def __init__(self, idxs):
                self.idxs, self.i = idxs, 0

            def next(self):
                k = self.idxs[self.i % len(self.idxs)]
                self.i += 1
                return PS[k], BPS[k]

        identb = sb(es, "identb", [128, 128], BF16)
        identf = sb(es, "identf", [128, 128], F32)
        tle = sb(es, "tle", [128, 128], F32)
        onesf = sb(es, "onesf", [128, 128], F32)
        negi = sb(es, "negi", [128, 128], BF16)
        ustr = sb(es, "ustr", [128, 128], BF16)
        ntri = sb(es, "ntri", [128, 128], BF16)
        uge = sb(es, "uge", [128, 128], BF16)
        ult = sb(es, "ult", [128, 32, 32], BF16)
        sel = sb(es, "sel", [64, 32, 128], BF16)
        negrow = sb(es, "negrow", [1, 128], BF16)
        onescol = sb(es, "onescol", [128, 1], BF16)
        zrow = sb(es, "zrow", [1, 512], BF16)
        nwf = sb(es, "nwf", [65, 64], F32)
        onesb = sb(es, "onesb", [8, 512], BF16)
        B_C = fw.buf("consts")

        def mk(tile_ap, val, pattern=None, cm=None, cmp=None, eng="pool"):
            op(eng, lambda e: e.memset(tile_ap, val), writes=[B_C])
            if pattern is not None:
                op(eng, lambda e: e.affine_select(out=tile_ap, in_=tile_ap, pattern=pattern, compare_op=cmp,
                                                  fill=0.0, base=0, channel_multiplier=cm),
                   reads=[B_C], writes=[B_C])

        mk(identb[:], 1.0, [[1, 128]], -1, ALU.is_equal)
        mk(identf[:], 1.0, [[1, 128]], -1, ALU.is_equal)
        mk(tle[:], 1.0, [[1, 128]], -1, ALU.is_ge)
        mk(onesf[:], 1.0)
        mk(negi[:], NEG, [[1, 128]], -1, ALU.is_equal)
        mk(ustr[:], 1.0, [[-1, 128]], 1, ALU.is_gt)
        mk(ntri[:], -1.0, [[-1, 128]], 1, ALU.is_ge)
        mk(uge[:], 1.0, [[-1, 128]], 1, ALU.is_ge)
        mk(ult[:], 1.0, [[1, 32], [-1, 32]], 0, ALU.is_gt)
        mk(sel[0:32], -1.0, [[-1, 32], [0, 128]], 1, ALU.is_equal)
        mk(sel[32:64], -1.0, [[-1, 32], [0, 128]], 1, ALU.is_equal)
        mk(negrow[:], -1.0)
        mk(onescol[:], 1.0)
        mk(zrow[:], 0.0)
        mk(nwf[0:64, :], 1.0 / 64)
        mk(nwf[64:65, :], EPS)
        mk(onesb[:], 1.0)
        for i in range(3):
            for tb in range(NB):
                dma("sp", QF[:, 67 + i, tb * 512:(tb + 1) * 512], onesb[:], reads=[B_C], writes=[B_QF])
                dma("sp", KF[:, 64 + i, tb * 512:(tb + 1) * 512], onesb[:], reads=[B_C], writes=[B_KF])
        fw.barrier()

        def load_T(st, name, rows, C, blk, prot):
            R = sum(r.shape[0] for r in rows)
            nblk = C // blk
            stg = sb(st, name + "_stg", [R, C], F32)
            Bs = fw.buf(name + "_stg")
            r0 = 0
            for r in rows:
                dma("sp", stg[r0:r0 + r.shape[0], :], r, writes=[Bs])
                r0 += r.shape[0]
            outt = sb(st, name, [blk, nblk, R], F32)
            Bo = fw.buf(name)
            per = max(1, 512 // R)
            b0 = 0
            while b0 < nblk:
                nb_ = min(per, nblk - b0)
                ps, bps = prot.next()
                op("pe", [lambda e, j=j, b0=b0: e.transpose(ps[0:blk, (j - b0) * R:(j - b0 + 1) * R],
                                                          stg[0:R, j * blk:(j + 1) * blk], identf[0:R, 0:R])
                          for j in range(b0, b0 + nb_)], reads=[Bs, B_C], writes=[bps])
                op("dve", lambda e, b0=b0, nb_=nb_: e.tensor_copy(
                    outt[:, b0:b0 + nb_, :], ps[0:blk, 0:nb_ * R].rearrange("p (a r) -> p a r", r=R)),
                   reads=[bps], writes=[Bo])
                b0 += nb_
            return outt, Bo

        def load_bc(st, name, row_ap, n):
            t = sb(st, name, [128, n], F32)
            Bt = fw.buf(name)
            dma("sp", t[:], row_ap.to_broadcast([128, n]), writes=[Bt])
            return t, Bt

        def load_weight(st, name, w_ap, K, N, gT=None, Bg=None, piece=1024):
            kc = K // 128
            wt = sb(st, name, [128, kc, N], BF16)
            Bw = fw.buf(name)
            stg = Rot(fw, st, nc, name + "_s", [128, piece], F32, 3)
            cnt = 0
            for k in range(kc):
                c0 = 0
                while c0 < N:
                    cw = min(piece, N - c0)
                    t, Bt = stg.next()
                    dma("sp", t[:, 0:cw], w_ap[k * 128:(k + 1) * 128, c0:c0 + cw], writes=[Bt])
                    eng = ("dve", "act")[cnt % 2]
                    cnt += 1
                    if gT is None:
                        if eng == "act":
                            op(eng, lambda e, t=t, k=k, c0=c0, cw=cw: e.copy(wt[:, k, c0:c0 + cw], t[:, 0:cw]),
                               reads=[Bt], writes=[Bw])
                        else:
                            op(eng, lambda e, t=t, k=k, c0=c0, cw=cw: e.tensor_copy(wt[:, k, c0:c0 + cw], t[:, 0:cw]),
                               reads=[Bt], writes=[Bw])
                    elif eng == "act":
                        op(eng, lambda e, t=t, k=k, c0=c0, cw=cw: e.activation(
                            out=wt[:, k, c0:c0 + cw], in_=t[:, 0:cw], func=AF.Copy, scale=gT[:, k:k + 1]),
                           reads=[Bt, Bg], writes=[Bw])
                    else:
                        op(eng, lambda e, t=t, k=k, c0=c0, cw=cw: e.tensor_scalar(
                            out=wt[:, k, c0:c0 + cw], in0=t[:, 0:cw], scalar1=gT[:, k:k + 1], scalar2=None,
                            op0=ALU.mult), reads=[Bt, Bg], writes=[Bw])
                    c0 += cw
            return wt, Bw

        def weight_steps(st, name, w_ap, K, N, gT, Bg, piece=512):
            kc = K // 128
            wt = sb(st, name, [128, kc, N], BF16)
            Bw = fw.buf(name)
            stg = Rot(fw, st, nc, name + "_s", [128, piece], F32, 3)
            steps = []
            cnt = 0
            for k in range(kc):
                c0 = 0
                while c0 < N:
                    cw = min(piece, N - c0)
                    eng = ("act", "dve", "act")[cnt % 3]
                    cnt += 1

                    def stepf(k=k, c0=c0, cw=cw, eng=eng):
                        t, Bt = stg.next()
                        dma("sp", t[:, 0:cw], w_ap[k * 128:(k + 1) * 128, c0:c0 + cw], writes=[Bt])
                        if eng == "act":
                            op(eng, lambda e: e.activation(out=wt[:, k, c0:c0 + cw], in_=t[:, 0:cw], func=AF.Copy, scale=gT[:, k:k + 1]),
                               reads=[Bt, Bg], writes=[Bw])
                        else:
                            op(eng, lambda e: e.tensor_scalar(out=wt[:, k, c0:c0 + cw], in0=t[:, 0:cw], scalar1=gT[:, k:k + 1],
                                                              scalar2=None, op0=ALU.mult), reads=[Bt, Bg], writes=[Bw])
                    steps.append(stepf)
                    c0 += cw
            return wt, Bw, steps

        def rmsnorm_T(st_tiles, xt, Bx, ss, Bss, junk, Bj, hb, Bhb, hT, BhT, tt, prot):
            op("act", lambda e: e.activation(out=junk[:], in_=xt[:], func=AF.Square, accum_out=ss[:]),
               reads=[Bx], writes=[Bj, Bss])
            op("act", lambda e: e.activation(out=ss[:], in_=ss[:], func=AF.Ln, scale=1.0 / D, bias=EPS),
               reads=[Bss], writes=[Bss])
            op("act", lambda e: e.activation(out=ss[:], in_=ss[:], func=AF.Exp, scale=-0.5),
               reads=[Bss], writes=[Bss])
            op("dve", lambda e: e.tensor_scalar(out=hb[:], in0=xt[:], scalar1=ss[:, 0:1], scalar2=None, op0=ALU.mult),
               reads=[Bx, Bss], writes=[Bhb])
            ps, bps = prot.next()
            psb = ps[:].bitcast(BF16)
            op("pe", [lambda e, k=k: e.transpose(psb[:, k * 128:(k + 1) * 128], hb[:, k * 128:(k + 1) * 128], identb[:])
                      for k in range(8)], reads=[Bhb, B_C], writes=[bps])
            op("dve", lambda e: e.tensor_copy(hT[:, :, tt * 128:(tt + 1) * 128],
                                              psb.rearrange("p (k t) -> p k t", t=128)),
               reads=[bps], writes=[BhT])

        x_cur = x_in
        B_xcur = fw.buf("xin")
        for L in range(depth):
            last = (L == depth - 1)
            with ExitStack() as st:
                prot_t = PRot([6, 7])
                g1T, Bg1 = load_T(st, "g1T", [mix_g[L].rearrange("(a b) -> a b", b=128)], 128, 128, prot_t)
                cwT, Bcw = load_T(st, "cwT", [conv_w[L], conv_b[L:L + 1, :]], 1536, 128, prot_t)
                nfb = sb(st, "nfb", [8, 1], F32)
                Bnfb = fw.buf("nfb")
                dma("sp", nfb[:], fox_fb[L].rearrange("(a b) -> a b", b=1), writes=[Bnfb])
                op("dve", lambda e: e.tensor_scalar(out=nfb[:], in0=nfb[:], scalar1=-1.0, scalar2=None, op0=ALU.mult),
                   reads=[Bnfb], writes=[Bnfb])
                dtb, Bdtb = load_bc(st, "dtb", dt_bias[L:L + 1, :], 16)
                Wb, BW = load_weight(st, "Wb", w_in[L], D, NIN, gT=g1T[:, 0, :], Bg=Bg1, piece=707)

                xts = Rot(fw, st, nc, "xt", [128, D], F32, 2)
                junk = sb(st, "junk", [128, D], BF16); Bjunk = fw.buf("junk")
                sss = Rot(fw, st, nc, "ss", [128, 1], F32, 2)
                hbs = Rot(fw, st, nc, "hb", [128, D], BF16, 2)
                hTs = Rot(fw, st, nc, "hT", [128, 8, 512], BF16, 2)
                ev_b = Rot(fw, st, nc, "evb", [128, 512], BF16, 6)
                ev_va = Rot(fw, st, nc, "eva", [128, 8, 65], BF16, 2)
                for tva, Bva in ev_va.items:
                    op("dve", lambda e, tva=tva: e.memset(tva[:], 1.0), writes=[Bva])
                ev_z = Rot(fw, st, nc, "evz", [128, 1024], F32, 1)
                ev_dt = Rot(fw, st, nc, "evdt", [128, 16], F32, 2)
                Us = Rot(fw, st, nc, "U", [128, 515], F32, 2)
                accs = Rot(fw, st, nc, "acc", [128, 512], F32, 4)
                silf = Rot(fw, st, nc, "silf", [128, 512], F32, 3)
                xtok = Rot(fw, st, nc, "xtok", [128, 4, 128], F32, 2)
                btok = Rot(fw, st, nc, "btok", [128, 4, 128], BF16, 2)
                halo = sb(st, "halo", [128, 12, 3], F32); Bhalo = fw.buf("halo")
                op("dve", lambda e: e.memset(halo[:], 0.0), writes=[Bhalo])
                ffe = sb(st, "ffe", [8, 512], F32); Bffe = fw.buf("ffe")
                ones8 = sb(st, "ones8", [8, 512], F32); Bones8 = fw.buf("ones8")
                op("dve", lambda e: e.memset(ones8[:], 1.0), writes=[Bones8])
                CSs = Rot(fw, st, nc, "CSb", [8, 512], F32, 2)
                cks = Rot(fw, st, nc, "ck", [8, 3, 512], BF16, 1)
                cqs = Rot(fw, st, nc, "cq", [8, 3, 512], BF16, 1)
                carry = sb(st, "carry", [8, 1], F32); Bcarry = fw.buf("carry")
                prot_fm = PRot([0, 1, 2])
                prot_tm = PRot([3, 4, 5])

                def p1_norm(tb):
                    hT, BhT = hTs.next()
                    for tt in range(4):
                        ti = tb * 4 + tt
                        xt, Bx = xts.next()
                        dma("sp", xt[:], x_cur[ti * 128:(ti + 1) * 128, :], reads=[B_xcur], writes=[Bx])
                        ss, Bss = sss.next()
                        hb, Bhb = hbs.next()
                        rmsnorm_T(None, xt, Bx, ss, Bss, junk, Bjunk, hb, Bhb, hT, BhT, tt, prot_t)
                    return hT, BhT

                nxt = p1_norm(0)
                for tb in range(NB):
                    hT, BhT = nxt
                    if tb + 1 < NB:
                        nxt = p1_norm(tb + 1)
                    tsl = slice(tb * 512, (tb + 1) * 512)

                    def fm_mm(col0, M):
                        ps, bps = prot_fm.next()
                        op("pe", [lambda e, k=k: e.matmul(ps[0:M, :], lhsT=Wb[:, k, col0:col0 + M], rhs=hT[:, k, :],
                                                          start=(k == 0), stop=(k == 7)) for k in range(8)],
                           reads=[BW, BhT], writes=[bps])
                        return ps, bps

                    for (col, dst, Bdst, scale) in ((C_FQ, QF, B_QF, 0.125), (C_FK, KF, B_KF, 1.0),
                                                   (C_SQ, QS, B_QS, 0.125), (C_SK, KS, B_KS, 1.0)):
                        for j in range(4):
                            ps, bps = fm_mm(col + j * 128, 128)
                            t, Bt = ev_b.next()
                            op("act", lambda e, t=t, ps=ps, scale=scale: e.activation(out=t[:], in_=ps[:], func=AF.Identity,
                                                                                     scale=scale),
                               reads=[bps], writes=[Bt])
                            for hh in range(2):
                                dma("sp", dst[2 * j + hh, 0:64, tsl], t[hh * 64:(hh + 1) * 64, :], reads=[Bt], writes=[Bdst])
                    ps, bps = fm_mm(C_FF, 8)
                    op("act", lambda e, ps=ps: e.activation(out=ffe[:], in_=ps[0:8, :], func=AF.Exp, scale=-1.0,
                                                            bias=nfb[:, 0:1]), reads=[bps, Bnfb], writes=[Bffe])
                    op("act", lambda e: e.activation(out=ffe[:], in_=ffe[:], func=AF.Ln, bias=1.0),
                       reads=[Bffe], writes=[Bffe])
                    CSb, BCS = CSs.next()
                    init = 0.0 if tb == 0 else carry[:, 0:1]
                    op("dve", lambda e, init=init, CSb=CSb: e.tensor_tensor_scan(
                        out=CSb[:], data0=ones8[:], data1=ffe[:], initial=init, op0=ALU.mult, op1=ALU.add),
                       reads=[Bffe, Bones8, Bcarry], writes=[BCS])
                    op("dve", lambda e, CSb=CSb: e.tensor_copy(carry[:], CSb[:, 511:512]), reads=[BCS], writes=[Bcarry])
                    ck, Bck = cks.next()
                    cq, Bcq = cqs.next()
                    for i in range(3):
                        op("dve", lambda e, i=i, ck=ck, CSb=CSb: e.tensor_copy(ck[:, i, :], CSb[:]), reads=[BCS], writes=[Bck])
                        if i < 2:
                            op("dve", lambda e, i=i, ck=ck, CSb=CSb: e.tensor_tensor(out=CSb[:], in0=CSb[:], in1=ck[:, i, :],
                                                                                   op=ALU.subtract),
                               reads=[BCS, Bck], writes=[BCS])
                    op("dve", lambda e, ck=ck, cq=cq: e.tensor_scalar(out=cq[:], in0=ck[:], scalar1=-1.0, scalar2=None,
                                                                    op0=ALU.mult), reads=[Bck], writes=[Bcq])
                    for i in range(3):
                        dma("sp", QF[:, 64 + i, tsl], cq[:, i, :], reads=[Bcq], writes=[B_QF])
                        dma("sp", KF[:, 67 + i, tsl], ck[:, i, :], reads=[Bck], writes=[B_KF])
                    pend = []

                    def conv_tail(cc, acc, Bacc):
                        if cc < 8:
                            sf, Bsf = silf.next()
                            op("act", lambda e: e.activation(out=sf[:], in_=acc[:], func=AF.Silu), reads=[Bacc], writes=[Bsf])

                            def t2():
                                ps2, bps2 = prot_t.next()
                                op("pe", [lambda e, q=q: e.transpose(ps2[:, q * 128:(q + 1) * 128], sf[:, q * 128:(q + 1) * 128], identf[:])
                                          for q in range(4)], reads=[Bsf, B_C], writes=[bps2])
                                xk, Bxk = xtok.next()
                                op("dve", lambda e: e.tensor_copy(xk[:], ps2[:].rearrange("p (q c) -> p q c", c=128)),
                                   reads=[bps2], writes=[Bxk])
                                dma("sp", XS[tsl, cc * 128:(cc + 1) * 128].rearrange("(q p) c -> p q c", p=128), xk[:],
                                    reads=[Bxk], writes=[B_XS])
                            return t2
                        t, Bt = ev_b.next()
                        op("act", lambda e: e.activation(out=t[:], in_=acc[:], func=AF.Silu), reads=[Bacc], writes=[Bt])
                        g = (cc - 8) % 2
                        if cc >= 10:
                            dma("sp", CT[g, :, tsl], t[:], reads=[Bt], writes=[B_CT])
                            return None
                        dma("sp", BT[g, :, tsl], t[:], reads=[Bt], writes=[B_BT])

                        def t2():
                            ps2, bps2 = prot_t.next()
                            ps2b = ps2[:].bitcast(BF16)
                            op("pe", [lambda e, q=q: e.transpose(ps2b[:, q * 128:(q + 1) * 128], t[:, q * 128:(q + 1) * 128], identb[:])
                                      for q in range(4)], reads=[Bt, B_C], writes=[bps2])
                            bk, Bbk = btok.next()
                            op("dve", lambda e: e.tensor_copy(bk[:], ps2b[:, 0:512].rearrange("p (q c) -> p q c", c=128)),
                               reads=[bps2], writes=[Bbk])
                            dma("sp", BTOK[tsl, g, :].rearrange("(q p) c -> p q c", p=128), bk[:], reads=[Bbk], writes=[B_BTOK])
                        return t2

                    def conv_tick():
                        todo = [p for p in pend if p[0] <= 0]
                        for p in todo:
                            pend.remove(p)
                        for p in pend:
                            p[0] -= 1
                        for p in todo:
                            r = p[1]()
                            if r is not None:
                                pend.append([0, r])

                    for cc in range(12):
                        ps, bps = fm_mm(C_XBC + cc * 128, 128)
                        U, BU = Us.next()
                        op("act", lambda e, U=U, ps=ps: e.copy(U[:, 3:515], ps[:]), reads=[bps], writes=[BU])
                        op("act", lambda e, U=U, cc=cc: e.copy(U[:, 0:3], halo[:, cc, :]), reads=[Bhalo], writes=[BU])
                        op("act", lambda e, U=U, cc=cc: e.copy(halo[:, cc, :], U[:, 512:515]), reads=[BU], writes=[Bhalo])
                        acc, Bacc = accs.next()
                        op("act", lambda e, ps=ps, acc=acc, cc=cc: e.activation(
                            out=acc[:], in_=ps[:], func=AF.Identity, scale=cwT[:, cc, 3:4], bias=cwT[:, cc, 4:5]),
                           reads=[bps, Bcw], writes=[Bacc])
                        for kk in (2, 1, 0):
                            op("dve", lambda e, U=U, acc=acc, cc=cc, kk=kk: e.scalar_tensor_tensor(
                                out=acc[:], in0=U[:, kk:kk + 512], scalar=cwT[:, cc, kk:kk + 1], in1=acc[:],
                                op0=ALU.mult, op1=ALU.add), reads=[BU, Bcw, Bacc], writes=[Bacc])
                        conv_tick()
                        pend.append([0, lambda cc=cc, acc=acc, Bacc=Bacc: conv_tail(cc, acc, Bacc)])
                    while pend:
                        conv_tick()
                    for tt in range(4):
                        ti = tb * 4 + tt
                        rows = slice(ti * 128, (ti + 1) * 128)

                        def tm_mm(col0, N):
                            ps, bps = prot_tm.next()
                            op("pe", [lambda e, k=k: e.matmul(ps[:, 0:N], lhsT=hT[:, k, tt * 128:(tt + 1) * 128],
                                                              rhs=Wb[:, k, col0:col0 + N], start=(k == 0), stop=(k == 7))
                                      for k in range(8)], reads=[BW, BhT], writes=[bps])
                            return ps, bps

                        ps, bps = tm_mm(C_FV, 512)
                        va, Bva = ev_va.next()
                        op("dve", lambda e, va=va, ps=ps: e.tensor_copy(va[:, :, 0:64],
                                                                         ps[:].rearrange("p (h c) -> p h c", c=64)),
                           reads=[bps], writes=[Bva])
                        dma("sp", VF[rows, :, :], va[:], reads=[Bva], writes=[B_VF])
                        ps, bps = tm_mm(C_SV, 512)
                        t, Bt = ev_b.next()
                        op("act", lambda e, t=t, ps=ps: e.copy(t[:], ps[:]), reads=[bps], writes=[Bt])
                        dma("sp", VS[rows, :], t[:], reads=[Bt], writes=[B_VS])
                        zt, Bzt = ev_z.next()
                        for hh in range(2):
                            ps, bps = tm_mm(C_Z + hh * 512, 512)
                            op("act", lambda e, zt=zt, ps=ps, hh=hh: e.activation(out=zt[:, hh * 512:(hh + 1) * 512], in_=ps[:],
                                                                                 func=AF.Silu), reads=[bps], writes=[Bzt])
                        dma("sp", ZS[rows, :], zt[:], reads=[Bzt], writes=[B_ZS])
                        ps, bps = tm_mm(C_DT, 16)
                        dtt, Bdtt = ev_dt.next()
                        op("dve", lambda e, dtt=dtt, ps=ps: e.tensor_tensor(out=dtt[:], in0=ps[:, 0:16], in1=dtb[:], op=ALU.add),
                           reads=[bps, Bdtb], writes=[Bdtt])
                        dma("sp", DTS[rows, :], dtt[:], reads=[Bdtt], writes=[B_DTS])
                fw.barrier()
            if stop_after == "P1":
                break
            with ExitStack() as st:
                prot_t = PRot([6, 7])
                gfx, Bgfx = load_T(st, "gfx", [fox_g[L].rearrange("(h c) -> h c", c=64)], 64, 64, prot_t)
                KAs = Rot(fw, st, nc, "KA", [70, S], BF16, 2)
                QAs = Rot(fw, st, nc, "QA", [70, 512], BF16, 3)
                VAs = Rot(fw, st, nc, "VA", [128, NT, 65], BF16, 2)
                Pbs = Rot(fw, st, nc, "Pb", [128, 512], BF16, 4)
                sqs = Rot(fw, st, nc, "sq", [65, 512], F32, 2)
                rss = Rot(fw, st, nc, "rs", [64, 512], F32, 2)
                yos = Rot(fw, st, nc, "yo", [64, 512], BF16, 2)
                osbs = Rot(fw, st, nc, "osb", [64, 512], F32, 2)
                prot_s = PRot([0, 1, 2])
                prot_o = PRot([3, 4])
                prot_n = PRot([5])
                groups = [(h, qb) for h in range(PROF_HEADS) for qb in range(NB)]
                G = {}

                def f_loads(gi):
                    if gi >= len(groups):
                        return
                    h, qb = groups[gi]
                    d = {"h": h, "qb": qb}
                    if qb == 0:
                        KA, BKA = KAs.next()
                        dma("sp", KA[:], KF[h], reads=[B_KF], writes=[BKA])
                        VA, BVA = VAs.next()
                        for q4 in range(4):
                            k0, k1 = q4 * NT // 4, (q4 + 1) * NT // 4
                            dma("sp", VA[:, k0:k1, :], VF[k0 * 128:k1 * 128, h, :].rearrange("(kb p) c -> p kb c", p=128),
                                reads=[B_VF], writes=[BVA])
                        d.update(KA=KA, BKA=BKA, VA=VA, BVA=BVA)
                    else:
                        p = G[gi - 1]
                        d.update(KA=p["KA"], BKA=p["BKA"], VA=p["VA"], BVA=p["BVA"])
                    QA, BQA = QAs.next()
                    dma("sp", QA[:], QF[h, :, qb * 512:(qb + 1) * 512], reads=[B_QF], writes=[BQA])
                    d.update(QA=QA, BQA=BQA)
                    G[gi] = d

                blocks = []
                for gi, (h, qb) in enumerate(groups):
                    nkb = 4 * qb + 4
                    for kb in range(nkb):
                        blocks.append(dict(gi=gi, kb=kb, first=(kb == 0), last=(kb == nkb - 1), c0=max(0, kb - 4 * qb) * 128,
                                           diag=(kb - 4 * qb >= 0)))
                deferred = []

                def defer(k, fn):
                    deferred.append([k, fn])

                def run_deferred():
                    todo = [d for d in deferred if d[0] <= 0]
                    for d in todo:
                        deferred.remove(d)
                    for d in deferred:
                        d[0] -= 1
                    for d in todo:
                        d[1]()

                def f_A(b):
                    g = G[b["gi"]]
                    if b["first"]:
                        f_loads(b["gi"] + 2)
                        g["O"], g["BO"] = prot_o.next()
                    Sp, BS = prot_s.next()
                    b["Sp"], b["BS"] = Sp, BS
                    c0, kb = b["c0"], b["kb"]
                    fns = [lambda e: e.matmul(Sp[:, c0:512], lhsT=g["KA"][:, kb * 128:(kb + 1) * 128], rhs=g["QA"][:, c0:512],
                                              start=True, stop=not b["diag"])]
                    if b["diag"]:
                        fns.append(lambda e: e.matmul(Sp[:, c0:c0 + 128], lhsT=negi[:], rhs=ustr[:], start=False, stop=True))
                    op("pe", fns, reads=[g["BKA"], g["BQA"], B_C], writes=[BS])

                def f_B(b):
                    Sp, BS, c0 = b["Sp"], b["BS"], b["c0"]
                    Pb, BP = Pbs.next()
                    b["Pb"], b["BP"] = Pb, BP
                    op("act", lambda e: e.activation(out=Pb[:, c0:512], in_=Sp[:, c0:512], func=AF.Exp), reads=[BS], writes=[BP])

                def f_F(b):
                    g = G[b["gi"]]
                    O, BO, c0, kb = g["O"], g["BO"], b["c0"], b["kb"]
                    Pb, BP = b["Pb"], b["BP"]
                    op("pe", lambda e: e.matmul(O[0:65, c0:512], lhsT=g["VA"][:, kb, :], rhs=Pb[:, c0:512], start=b["first"],
                                                stop=b["last"]), reads=[g["BVA"], BP], writes=[BO], signal=b["last"])
                    if b["last"]:
                        h, qb = g["h"], g["qb"]
                        sq, Bsq = sqs.next()
                        Np, BN = prot_n.next()
                        rs, Brs = rss.next()
                        yo, Byo = yos.next()
                        op("act", lambda e: e.activation(out=sq[:], in_=O[0:65, :], func=AF.Square), reads=[BO], writes=[Bsq])
                        osb, Bosb = osbs.next()
                        op("act", lambda e: e.copy(osb[:], O[0:64, :]), reads=[BO], writes=[Bosb])

                        def e2():
                            op("pe", lambda e: e.matmul(Np[0:64, :], lhsT=nwf[0:65, :], rhs=sq[0:65, :], start=True, stop=True),
                               reads=[Bsq, B_C], writes=[BN])

                        def e3():
                            op("act", lambda e: e.activation(out=rs[:], in_=Np[0:64, :], func=AF.Ln), reads=[BN], writes=[Brs])
                            op("act", lambda e: e.activation(out=rs[:], in_=rs[:], func=AF.Exp, scale=-0.5), reads=[Brs], writes=[Brs])

                        def e4():
                            op("dve", lambda e: e.scalar_tensor_tensor(out=yo[:], in0=osb[:], scalar=gfx[:, 0, h:h + 1], in1=rs[:],
                                                                       op0=ALU.mult, op1=ALU.mult), reads=[Bosb, Brs, Bgfx], writes=[Byo])
                            dma("sp", YT[h * 64:(h + 1) * 64, qb * 512:(qb + 1) * 512], yo[:], reads=[Byo], writes=[B_YT])
                        defer(0, e2)
                        defer(1, e3)
                        defer(2, e4)

                f_loads(0)
                f_loads(1)
                nblk = len(blocks)
                f_A(blocks[0])
                for i in range(nblk + 4):
                    if i + 1 < nblk:
                        f_A(blocks[i + 1])
                    if i < nblk:
                        f_B(blocks[i])
                    run_deferred()
                    if 1 <= i <= nblk:
                        f_F(blocks[i - 1])
                while deferred:
                    run_deferred()
                fw.barrier()
            if stop_after == "P2":
                break
            with ExitStack() as st:
                prot_t = PRot([6, 7])
                gsb, Bgsb = load_T(st, "gsb", [sb_g[L].rearrange("(h c) -> h c", c=64)], 64, 64, prot_t)
                KAs = Rot(fw, st, nc, "KAs", [128, S], BF16, 2)
                QAs = Rot(fw, st, nc, "QAs", [128, 512], BF16, 4)
                for KA_, BKA_ in KAs.items:
                    op("dve", lambda e, KA_=KA_: e.tensor_copy(KA_[0:64, :], sel[:, 0:NT, :].rearrange("k a s -> k (a s)")),
                       reads=[B_C], writes=[BKA_])
                VAs = Rot(fw, st, nc, "VAs", [128, NT, 64], BF16, 2)
                Efs = Rot(fw, st, nc, "Ef", [128, 512], F32, 2)
                SPAs = [[(sb(st, f"SPA{r}_{k}", [128, 512], BF16), fw.buf(f"SPA{r}_{k}")) for k in range(NT)] for r in range(3)]
                Wbs = Rot(fw, st, nc, "Wsb", [128, 512], BF16, 4)
                rtmp = Rot(fw, st, nc, "rtmp", [32, 512], F32, 2)
                sqs = Rot(fw, st, nc, "sqs", [64, 512], F32, 2)
                rss = Rot(fw, st, nc, "rss", [64, 512], F32, 2)
                yos = Rot(fw, st, nc, "yos", [64, 512], BF16, 2)
                osbs = Rot(fw, st, nc, "osbs", [64, 512], F32, 2)
                prot_z = PRot([0, 1, 2])
                prot_o = PRot([3, 4])
                prot_r = PRot([5, 6])
                prot_n = PRot([7])
                groups = [(h, qb) for h in range(PROF_HEADS) for qb in range(NB)]
                G = {}

                def s_loads(gi):
                    if gi >= len(groups):
                        return
                    h, qb = groups[gi]
                    d = {"h": h, "qb": qb, "nkb": 4 * qb + 4, "spa": SPAs[gi % 3]}
                    if qb == 0:
                        KA, BKA = KAs.next()
                        dma("sp", KA[64:128, :], KS[h], reads=[B_KS], writes=[BKA])
                        VA, BVA = VAs.next()
                        for q4 in range(4):
                            k0, k1 = q4 * NT // 4, (q4 + 1) * NT // 4
                            dma("sp", VA[:, k0:k1, :], VS[k0 * 128:k1 * 128, h * 64:(h + 1) * 64].rearrange("(kb p) c -> p kb c", p=128),
                                reads=[B_VS], writes=[BVA])
                        d.update(KA=KA, BKA=BKA, VA=VA, BVA=BVA)
                    else:
                        p = G[gi - 1]
                        d.update(KA=p["KA"], BKA=p["BKA"], VA=p["VA"], BVA=p["BVA"])
                    QA, BQA = QAs.next()
                    dma("sp", QA[64:128, :], QS[h, :, qb * 512:(qb + 1) * 512], reads=[B_QS], writes=[BQA])
                    d.update(QA=QA, BQA=BQA)
                    G[gi] = d

                def mk_blocks(gi, ps):
                    h, qb = groups[gi]
                    nkb = 4 * qb + 4
                    return [dict(ps=ps, gi=gi, kb=kb, first=(kb == 0), last=(kb == nkb - 1), c0=max(0, kb - 4 * qb) * 128,
                                 diag=(kb - 4 * qb >= 0)) for kb in range(nkb)]

                def merge(a, b):
                    out_, i, j = [], 0, 0
                    while i < len(a) or j < len(b):
                        if i < len(a) and (j >= len(b) or i * len(b) <= j * len(a)):
                            out_.append(a[i]); i += 1
                        else:
                            out_.append(b[j]); j += 1
                    return out_

                tasks = mk_blocks(0, 1)
                for gi in range(len(groups)):
                    nxt1 = mk_blocks(gi + 1, 1) if gi + 1 < len(groups) else [dict(ps=0), dict(ps=0)]
                    tasks += nxt1[:2] + merge(nxt1[2:], mk_blocks(gi, 2))
                deferred = []

                def defer(k, fn):
                    deferred.append([k, fn])

                def run_deferred():
                    todo = [d for d in deferred if d[0] <= 0]
                    for d in todo:
                        deferred.remove(d)
                    for d in deferred:
                        d[0] -= 1
                    for d in todo:
                        d[1]()

                def s_A(b):
                    if b["ps"] == 0:
                        return
                    g = G[b["gi"]]
                    c0, kb = b["c0"], b["kb"]
                    if b["ps"] == 1 and b["first"]:
                        s_loads(b["gi"] + 2)
                        g["RP"], g["BRP"] = prot_r.next()
                    if b["ps"] == 2 and b["first"]:
                        g["O"], g["BO"] = prot_o.next()
                    Zp, BZ = prot_z.next()
                    b["Zp"], b["BZ"] = Zp, BZ
                    r0 = 64 if b["ps"] == 1 else 0
                    fns = [lambda e: e.matmul(Zp[:, c0:512], lhsT=g["KA"][r0:128, kb * 128:(kb + 1) * 128], rhs=g["QA"][r0:128, c0:512],
                                              start=True, stop=True)]
                    if b["diag"]:
                        fns.append(lambda e: e.matmul(Zp[:, c0:c0 + 128], lhsT=negi[:], rhs=uge[:], start=False, stop=True,
                                                      skip_group_check=True))
                    reads = [g["BKA"], g["BQA"], B_C]
                    if b["ps"] == 2:
                        SPb, BSP = g["spa"][kb]
                        fns.append(lambda e: e.matmul(Zp[:, c0:512], lhsT=ntri[:], rhs=SPb[:, c0:512], start=False, stop=True,
                                                      skip_group_check=True))
                        reads += [BSP]
                    op("pe", fns, reads=reads, writes=[BZ])

                def s_B(b):
                    if b["ps"] == 0:
                        return
                    g = G[b["gi"]]
                    Zp, BZ, c0, kb = b["Zp"], b["BZ"], b["c0"], b["kb"]
                    if b["ps"] == 1:
                        Ef, BEf = Efs.next()
                        SPb, BSP = g["spa"][kb]
                        op("act", lambda e: e.activation(out=Ef[:, c0:512], in_=Zp[:, c0:512], func=AF.Exp), reads=[BZ], writes=[BEf])
                        op("act", lambda e: e.activation(out=SPb[:, c0:512], in_=Ef[:, c0:512], func=AF.Ln, bias=1.0),
                           reads=[BEf], writes=[BSP])
                    else:
                        Wt, BWt = Wbs.next()
                        b["Wt"], b["BWt"] = Wt, BWt
                        op("act", lambda e: e.activation(out=Wt[:, c0:512], in_=Zp[:, c0:512], func=AF.Exp), reads=[BZ], writes=[BWt])

                def s_C(b):
                    if b["ps"] == 0:
                        return
                    g = G[b["gi"]]
                    c0, kb = b["c0"], b["kb"]
                    if b["ps"] == 1:
                        RP, BRP = g["RP"], g["BRP"]
                        SPb, BSP = g["spa"][kb]
                        op("pe", lambda e: e.matmul(RP[0:32, c0:512], lhsT=ult[:, kb, :], rhs=SPb[:, c0:512], start=b["first"], stop=True,
                                                    skip_group_check=True), reads=[BSP, B_C], writes=[BRP], signal=b["last"])
                        if b["last"]:
                            rt, Brt = rtmp.next()
                            QA, BQA = g["QA"], g["BQA"]
                            op("dve", lambda e: e.tensor_copy(QA[0:32, :], RP[0:32, :]), reads=[BRP], writes=[BQA])
                            op("dve", lambda e: e.tensor_tensor(out=rt[:], in0=RP[0:32, :], in1=QA[0:32, :], op=ALU.subtract),
                               reads=[BRP, BQA], writes=[Brt])
                            op("dve", lambda e: e.tensor_copy(QA[32:64, :], rt[:]), reads=[Brt], writes=[BQA])
                    else:
                        O, BO = g["O"], g["BO"]
                        Wt, BWt = b["Wt"], b["BWt"]
                        op("pe", lambda e: e.matmul(O[0:64, c0:512], lhsT=g["VA"][:, kb, :], rhs=Wt[:, c0:512], start=b["first"], stop=True,
                                                    skip_group_check=True), reads=[g["BVA"], BWt], writes=[BO], signal=b["last"])
                        if b["last"]:
                            h, qb = g["h"], g["qb"]
                            sq, Bsq = sqs.next()
                            Np, BN = prot_n.next()
                            rs, Brs = rss.next()
                            yo, Byo = yos.next()
                            osb, Bosb = osbs.next()
                            op("act", lambda e: e.activation(out=sq[:], in_=O[0:64, :], func=AF.Square), reads=[BO], writes=[Bsq])
                            op("act", lambda e: e.copy(osb[:], O[0:64, :]), reads=[BO], writes=[Bosb])

                            def e2():
                                op("pe", lambda e: e.matmul(Np[0:64, :], lhsT=nwf[0:64, :], rhs=sq[0:64, :], start=True, stop=True),
                                   reads=[Bsq, B_C], writes=[BN])

                            def e3():
                                op("act", lambda e: e.activation(out=rs[:], in_=Np[0:64, :], func=AF.Ln, bias=EPS), reads=[BN], writes=[Brs])
                                op("act", lambda e: e.activation(out=rs[:], in_=rs[:], func=AF.Exp, scale=-0.5), reads=[Brs], writes=[Brs])

                            def e4():
                                op("dve", lambda e: e.scalar_tensor_tensor(out=yo[:], in0=osb[:], scalar=gsb[:, 0, h:h + 1], in1=rs[:],
                                                                           op0=ALU.mult, op1=ALU.mult), reads=[Bosb, Brs, Bgsb], writes=[Byo])
                                dma("sp", YT[512 + h * 64:512 + (h + 1) * 64, qb * 512:(qb + 1) * 512], yo[:], reads=[Byo], writes=[B_YT])
                            defer(0, e2)
                            defer(1, e3)
                            defer(2, e4)

                s_loads(0)
                s_loads(1)
                nt_ = len(tasks)
                s_A(tasks[0])
                for i in range(nt_ + 1):
                    if 1 <= i <= nt_:
                        s_C(tasks[i - 1])
                    if i + 1 < nt_:
                        s_A(tasks[i + 1])
                    if i < nt_:
                        s_B(tasks[i])
                    run_deferred()
                for _ in range(5):
                    run_deferred()
                fw.barrier()
            if stop_after == "P3":
                break
            with ExitStack() as st:
                abc, Babc = load_bc(st, "abc", a_log[L:L + 1, :], 16)
                op("act", lambda e: e.activation(out=abc[:], in_=abc[:], func=AF.Exp), reads=[Babc], writes=[Babc])
                op("dve", lambda e: e.tensor_scalar(out=abc[:], in0=abc[:], scalar1=-1.0, scalar2=None, op0=ALU.mult),
                   reads=[Babc], writes=[Babc])
                dbc, Bdbc = load_bc(st, "dbc", ssd_d[L:L + 1, :], 16)
                Dfull = sb(st, "Dfull", [128, 1024], F32); BDf = fw.buf("Dfull")
                op("dve", lambda e: e.tensor_copy(Dfull[:].rearrange("p (h c) -> p h c", c=64),
                                                   dbc[:].unsqueeze(2).to_broadcast([128, 16, 64])), reads=[Bdbc], writes=[BDf])
                NG, BNG = load_bc(st, "NG", ssd_ng[L:L + 1, :], 1024)
                prev = sb(st, "prev", [128, 2, 512], F32); Bprev = fw.buf("prev")
                prevb = sb(st, "prevb", [128, 2, 512], BF16); Bprevb = fw.buf("prevb")
                op("dve", lambda e: e.memset(prev[:], 0.0), writes=[Bprev])
                op("dve", lambda e: e.memset(prevb[:], 0.0), writes=[Bprevb])
                R3 = 3
                xss = Rot(fw, st, nc, "xs", [128, 1024], F32, R3)
                zss = Rot(fw, st, nc, "zs", [128, 1024], F32, R3)
                dtrs = Rot(fw, st, nc, "dtr", [128, 16], F32, R3)
                bts = Rot(fw, st, nc, "bt", [128, 2, 128], BF16, R3)
                cts = Rot(fw, st, nc, "ct", [128, 2, 128], BF16, R3)
                btks = Rot(fw, st, nc, "btk", [128, 2, 128], BF16, R3)
                dts = Rot(fw, st, nc, "dt", [128, 16], F32, R3)
                pass_ = Rot(fw, st, nc, "pas", [128, 32], F32, R3)
                dAs = Rot(fw, st, nc, "dA", [128, 16], F32, R3)
                dAbs = Rot(fw, st, nc, "dAb", [128, 16, 128], F32, R3)
                nacss = Rot(fw, st, nc, "nacs", [128, 16], F32, R3)
                eats = Rot(fw, st, nc, "eat", [128, 16], F32, R3)
                decs = Rot(fw, st, nc, "dec", [128, 16], F32, R3)
                des = Rot(fw, st, nc, "de", [128, 16], F32, R3)
                xcs = Rot(fw, st, nc, "xc", [128, 1024], BF16, R3)
                xds = Rot(fw, st, nc, "xd", [128, 1024], BF16, R3)
                cbs = Rot(fw, st, nc, "cb", [128, 128], F32, 2)
                segs = Rot(fw, st, nc, "seg", [128, 128], F32, 4)
                Mts = Rot(fw, st, nc, "Mt", [128, 4, 128], BF16, 2)
                yds = Rot(fw, st, nc, "ydsb", [128, 512], F32, 4)
                t1s = Rot(fw, st, nc, "t1", [128, 512], F32, 4)
                t2s = Rot(fw, st, nc, "t2", [128, 512], F32, 2)
                ssn = Rot(fw, st, nc, "ssn", [128, 1], F32, 2)
                jk = sb(st, "jk", [128, 512], BF16); Bjk = fw.buf("jk")
                yns = Rot(fw, st, nc, "yn", [128, 512], BF16, 4)
                yts = Rot(fw, st, nc, "ytt", [128, 4, 128], BF16, 2)
                prot_a = PRot([0, 1])
                prot_R = PRot([2, 3])
                prot_yd = PRot([4, 5])
                prot_yo = PRot([6])
                prot_st = PRot([7])
                CH = {}

                def ssd_A(c):
                    rows = slice(c * 128, (c + 1) * 128)
                    xs, Bxs = xss.next(); zs, Bzs = zss.next(); dtr, Bdtr = dtrs.next()
                    bt, Bbt = bts.next(); ct, Bct = cts.next(); btk, Bbtk = btks.next()
                    dt, Bdt = dts.next(); dA, BdA = dAs.next(); dAb, BdAb = dAbs.next()
                    pas, Bpas = pass_.next(); nacs, Bnacs = nacss.next(); eat, Beat = eats.next()
                    dec, Bdec = decs.next(); de, Bde = des.next(); xc, Bxc = xcs.next(); xd, Bxd = xds.next()
                    CH[c] = dict(rows=rows, xs=xs, Bxs=Bxs, zs=zs, Bzs=Bzs, bt=bt, Bbt=Bbt, ct=ct, Bct=Bct, btk=btk, Bbtk=Bbtk,
                                 dAb=dAb, BdAb=BdAb, nacs=nacs, Bnacs=Bnacs, eat=eat, Beat=Beat, dec=dec, Bdec=Bdec,
                                 xc=xc, Bxc=Bxc, xd=xd, Bxd=Bxd, yd=[None, None])
                    st_ = {}

                    def a0():
                        dma("sp", xs[:], XS[rows, :], reads=[B_XS], writes=[Bxs])
                        dma("sp", zs[:], ZS[rows, :], reads=[B_ZS], writes=[Bzs])
                        dma("sp", dtr[:], DTS[rows, :], reads=[B_DTS], writes=[Bdtr])
                        dma("sp", bt[:], BT[:, :, rows].rearrange("g n t -> n g t"), reads=[B_BT], writes=[Bbt])
                        dma("sp", ct[:], CT[:, :, rows].rearrange("g n t -> n g t"), reads=[B_CT], writes=[Bct])
                        dma("sp", btk[:], BTOK[rows, :, :], reads=[B_BTOK], writes=[Bbtk])
                        op("act", lambda e: e.activation(out=dt[:], in_=dtr[:], func=AF.Exp), reads=[Bdtr], writes=[Bdt])
                        op("act", lambda e: e.activation(out=dt[:], in_=dt[:], func=AF.Ln, bias=1.0), reads=[Bdt], writes=[Bdt])

                    def a1():
                        op("dve", lambda e: e.tensor_tensor(out=dA[:], in0=dt[:], in1=abc[:], op=ALU.mult), reads=[Bdt, Babc], writes=[BdA])
                        op("dve", lambda e: e.tensor_copy(dAb[:], dA[:].unsqueeze(2).to_broadcast([128, 16, 128])),
                           reads=[BdA], writes=[BdAb])

                    def a2():
                        pa, Bpa = prot_a.next()
                        st_["pa"], st_["Bpa"] = pa, Bpa
                        op("pe", [lambda e: e.matmul(pa[:, 0:16], lhsT=tle[:], rhs=dA[:], start=True, stop=True),
                                  lambda e: e.matmul(pa[:, 16:32], lhsT=onesf[:], rhs=dA[:], start=True, stop=True)],
                           reads=[BdA, B_C], writes=[Bpa])

                    def a3():
                        pa, Bpa = st_["pa"], st_["Bpa"]
                        op("act", lambda e: e.copy(pas[:], pa[:, 0:32]), reads=[Bpa], writes=[Bpas])

                    def a4():
                        op("dve", lambda e: e.tensor_scalar(out=nacs[:], in0=pas[:, 0:16], scalar1=-1.0, scalar2=None, op0=ALU.mult),
                           reads=[Bpas], writes=[Bnacs])
                        op("dve", lambda e: e.tensor_tensor(out=de[:], in0=pas[:, 16:32], in1=nacs[:], op=ALU.add),
                           reads=[Bpas, Bnacs], writes=[Bde])

                    def a5():
                        op("act", lambda e: e.activation(out=eat[:], in_=pas[:, 0:16], func=AF.Exp), reads=[Bpas], writes=[Beat])
                        op("act", lambda e: e.activation(out=dec[:], in_=pas[:, 16:32], func=AF.Exp), reads=[Bpas], writes=[Bdec])
                        op("act", lambda e: e.activation(out=de[:], in_=de[:], func=AF.Exp), reads=[Bde], writes=[Bde])

                    def a6():
                        op("dve", lambda e: e.tensor_tensor(out=de[:], in0=de[:], in1=dt[:], op=ALU.mult), reads=[Bde, Bdt], writes=[Bde])
                        op("dve", lambda e: e.tensor_tensor(
                            out=xc[:].rearrange("p (h c) -> p h c", c=64), in0=xs[:].rearrange("p (h c) -> p h c", c=64),
                            in1=dt[:].unsqueeze(2).to_broadcast([128, 16, 64]), op=ALU.mult), reads=[Bxs, Bdt], writes=[Bxc])
                        op("dve", lambda e: e.tensor_tensor(
                            out=xd[:].rearrange("p (h c) -> p h c", c=64), in0=xs[:].rearrange("p (h c) -> p h c", c=64),
                            in1=de[:].unsqueeze(2).to_broadcast([128, 16, 64]), op=ALU.mult), reads=[Bxs, Bde], writes=[Bxd])
                    return [a0, a1, a2, a3, a4, a5, a6]

                a_steps = []

                def tick():
                    if a_steps:
                        a_steps.pop(0)()

                def ssd_B(c):
                    d = CH[c]
                    bt, Bbt, ct, Bct, dAb, BdAb, nacs, Bnacs, xc, Bxc = (d[k] for k in (
                        "bt", "Bbt", "ct", "Bct", "dAb", "BdAb", "nacs", "Bnacs", "xc", "Bxc"))
                    cbl = []
                    for g in range(2):
                        pc, Bpc = prot_a.next()
                        op("pe", lambda e, pc=pc, g=g: e.matmul(pc[:, 0:128], lhsT=bt[:, g, :], rhs=ct[:, g, :], start=True, stop=True),
                           reads=[Bbt, Bct], writes=[Bpc])
                        cb, Bcb = cbs.next()
                        op("act", lambda e, cb=cb, pc=pc: e.copy(cb[:], pc[:, 0:128]), reads=[Bpc], writes=[Bcb])
                        cbl.append((cb, Bcb))
                    tick()
                    Yds = [prot_yd.next(), prot_yd.next()]
                    units = [(g, hq) for g in range(2) for hq in range(2)]

                    def emit_R(g, hq):
                        R, BR = prot_R.next()
                        fns = []
                        for h4 in range(4):
                            h = g * 8 + hq * 4 + h4
                            fns.append(lambda e, R=R, h4=h4, h=h: e.matmul(
                                R[:, h4 * 128:(h4 + 1) * 128], lhsT=dAb[:, h, :], rhs=tle[:], start=True, stop=False))
                            fns.append(lambda e, R=R, h4=h4: e.matmul(
                                R[:, h4 * 128:(h4 + 1) * 128], lhsT=negi[:], rhs=ustr[:], start=False, stop=True))
                        op("pe", fns, reads=[BdAb, B_C], writes=[BR])
                        return R, BR

                    def emit_rest(g, hq, R, BR):
                        cb, Bcb = cbl[g]
                        Yd, BYd = Yds[g]
                        Mt, BMt = Mts.next()
                        for h4 in range(4):
                            h = g * 8 + hq * 4 + h4
                            seg, Bseg = segs.next()
                            op("act", lambda e, seg=seg, h4=h4, h=h: e.activation(
                                out=seg[:], in_=R[:, h4 * 128:(h4 + 1) * 128], func=AF.Exp, bias=nacs[:, h:h + 1]),
                               reads=[BR, Bnacs], writes=[Bseg])
                            op("dve", lambda e, seg=seg, h4=h4: e.tensor_tensor(
                                out=Mt[:, h4, :], in0=seg[:], in1=cb[:], op=ALU.mult), reads=[Bseg, Bcb], writes=[BMt])
                        op("pe", [lambda e, h4=h4: e.matmul(
                            Yd[:, (hq * 4 + h4) * 64:(hq * 4 + h4 + 1) * 64], lhsT=Mt[:, h4, :],
                            rhs=xc[:, (g * 8 + hq * 4 + h4) * 64:(g * 8 + hq * 4 + h4 + 1) * 64], start=True, stop=True)
                            for h4 in range(4)], reads=[BMt, Bxc], writes=[BYd])
                        if hq == 1:
                            yd, Byd = yds.next()
                            op("act", lambda e: e.copy(yd[:], Yd[:, :]), reads=[BYd], writes=[Byd])
                            d["yd"][g] = (yd, Byd)

                    cur = emit_R(*units[0])
                    for i, (g, hq) in enumerate(units):
                        nxt = emit_R(*units[i + 1]) if i + 1 < len(units) else None
                        emit_rest(g, hq, *cur)
                        cur = nxt
                        tick()

                def ssd_C(c):
                    d = CH[c]
                    rows = d["rows"]
                    xs, Bxs, zs, Bzs, ct, Bct, btk, Bbtk, eat, Beat, dec, Bdec, xd, Bxd = (d[k] for k in (
                        "xs", "Bxs", "zs", "Bzs", "ct", "Bct", "btk", "Bbtk", "eat", "Beat", "dec", "Bdec", "xd", "Bxd"))
                    mm = []
                    for g in range(2):
                        Yo, BYo = prot_yo.next()
                        op("pe", lambda e, Yo=Yo, g=g: e.matmul(Yo[:, :], lhsT=ct[:, g, :], rhs=prevb[:, g, :], start=True, stop=True),
                           reads=[Bct, Bprevb], writes=[BYo])
                        St, BSt = prot_st.next()
                        op("pe", lambda e, St=St, g=g: e.matmul(St[:, :], lhsT=btk[:, g, :], rhs=xd[:, g * 512:(g + 1) * 512],
                                                                start=True, stop=True), reads=[Bbtk, Bxd], writes=[BSt])
                        t1, Bt1 = t1s.next()
                        op("dve", lambda e, t1=t1, Yo=Yo, g=g: e.tensor_tensor(
                            out=t1[:].rearrange("p (h c) -> p h c", c=64), in0=Yo[:, :].rearrange("p (h c) -> p h c", c=64),
                            in1=eat[:, g * 8:(g + 1) * 8].unsqueeze(2).to_broadcast([128, 8, 64]), op=ALU.mult),
                           reads=[BYo, Beat], writes=[Bt1])
                        op("dve", lambda e, g=g: e.tensor_tensor(
                            out=prev[:, g, :].rearrange("p (h c) -> p h c", c=64), in0=prev[:, g, :].rearrange("p (h c) -> p h c", c=64),
                            in1=dec[:, g * 8:(g + 1) * 8].unsqueeze(2).to_broadcast([128, 8, 64]), op=ALU.mult),
                           reads=[Bprev, Bdec], writes=[Bprev])
                        op("dve", lambda e, St=St, g=g: e.tensor_tensor(out=prev[:, g, :], in0=prev[:, g, :], in1=St[:, :], op=ALU.add),
                           reads=[Bprev, BSt], writes=[Bprev])
                        op("act", lambda e, g=g: e.copy(prevb[:, g, :], prev[:, g, :]), reads=[Bprev], writes=[Bprevb])
                        mm.append((t1, Bt1))
                    tick()
                    outs = []
                    for g in range(2):
                        yd, Byd = d["yd"][g]
                        t1, Bt1 = mm[g]
                        op("dve", lambda e, t1=t1, yd=yd: e.tensor_tensor(out=t1[:], in0=t1[:], in1=yd[:], op=ALU.add),
                           reads=[Bt1, Byd], writes=[Bt1])
                        t2, Bt2 = t2s.next()
                        op("dve", lambda e, t2=t2, g=g: e.tensor_tensor(out=t2[:], in0=xs[:, g * 512:(g + 1) * 512],
                                                                        in1=Dfull[:, g * 512:(g + 1) * 512], op=ALU.mult),
                           reads=[Bxs, BDf], writes=[Bt2])
                        op("dve", lambda e, t1=t1, t2=t2: e.tensor_tensor(out=t1[:], in0=t1[:], in1=t2[:], op=ALU.add),
                           reads=[Bt1, Bt2], writes=[Bt1])
                        op("dve", lambda e, t1=t1, g=g: e.tensor_tensor(out=t1[:], in0=t1[:], in1=zs[:, g * 512:(g + 1) * 512], op=ALU.mult),
                           reads=[Bt1, Bzs], writes=[Bt1])
                        sn, Bsn = ssn.next()
                        op("act", lambda e, t1=t1, sn=sn: e.activation(out=jk[:], in_=t1[:], func=AF.Square, accum_out=sn[:]),
                           reads=[Bt1], writes=[Bjk, Bsn])
                        op("act", lambda e, sn=sn: e.activation(out=sn[:], in_=sn[:], func=AF.Ln, scale=1.0 / 512, bias=EPS),
                           reads=[Bsn], writes=[Bsn])
                        op("act", lambda e, sn=sn: e.activation(out=sn[:], in_=sn[:], func=AF.Exp, scale=-0.5), reads=[Bsn], writes=[Bsn])
                        yn, Byn = yns.next()
                        op("dve", lambda e, yn=yn, t1=t1, sn=sn, g=g: e.scalar_tensor_tensor(
                            out=yn[:], in0=t1[:], scalar=sn[:, 0:1], in1=NG[:, g * 512:(g + 1) * 512], op0=ALU.mult, op1=ALU.mult),
                           reads=[Bt1, Bsn, BNG], writes=[Byn])
                        outs.append((g, yn, Byn))
                    del CH[c]

                    def tail():
                        for g, yn, Byn in outs:
                            pt, Bpt = prot_a.next()
                            ptb = pt[:].bitcast(BF16)
                            op("pe", [lambda e, ptb=ptb, yn=yn, q=q: e.transpose(ptb[:, q * 128:(q + 1) * 128], yn[:, q * 128:(q + 1) * 128],
                                                                               identb[:]) for q in range(4)], reads=[Byn, B_C], writes=[Bpt])
                            ytt, Bytt = yts.next()
                            op("act", lambda e, ytt=ytt, ptb=ptb: e.copy(ytt[:], ptb[:, 0:512].rearrange("p (q t) -> p q t", t=128)),
                               reads=[Bpt], writes=[Bytt])
                            dma("sp", YT[1024 + g * 512:1024 + (g + 1) * 512, rows].rearrange("(q p) t -> p q t", p=128), ytt[:],
                                reads=[Bytt], writes=[B_YT])
                    return tail

                for f in ssd_A(0):
                    f()
                if NT > 1:
                    for f in ssd_A(1):
                        f()
                ssd_B(0)
                tail_prev = None
                for c in range(NT):
                    if c + 2 < NT:
                        a_steps.extend(ssd_A(c + 2))
                        tick()
                    if c + 1 < NT:
                        ssd_B(c + 1)
                    tl = ssd_C(c)
                    while a_steps:
                        tick()
                    if tail_prev is not None:
                        tail_prev()
                    tail_prev = tl
                tail_prev()
                fw.barrier()
            if stop_after == "P4":
                break
            stw = ExitStack()
            stw.__enter__()
            g2T, Bg2 = load_T(stw, "g2T", [ffn_g[L].rearrange("(a b) -> a b", b=128)], 128, 128, PRot([6, 7]))
            Wu, BWu, wu_steps = weight_steps(stw, "Wu", w_up[L], D, 2 * DFF, g2T[:, 0, :], Bg2, piece=512)

            def wtick(n=1):
                for _ in range(n):
                    if wu_steps:
                        wu_steps.pop(0)()

            with ExitStack() as st:
                Wo, BWo = load_weight(st, "Wo", w_out[L], 2048, D)
                yts = Rot(fw, st, nc, "yT", [128, 16, 512], BF16, 2)
                xts = Rot(fw, st, nc, "xt5", [128, D], F32, 2)
                x1s = Rot(fw, st, nc, "x1t", [128, D], F32, 2)
                prot = PRot([0, 1, 2, 3])
                def load_yT(tb):
                    tsl_ = slice(tb * 512, (tb + 1) * 512)
                    yT_, ByT_ = yts.next()
                    for eh in range(4):
                        dma("sp", yT_[:, eh * 4:(eh + 1) * 4, :], YT[eh * 512:(eh + 1) * 512, tsl_].rearrange("(e p) t -> p e t", p=128),
                            reads=[B_YT], writes=[ByT_])
                    return yT_, ByT_

                nxt_y = load_yT(0)
                for tb in range(NB):
                    tsl = slice(tb * 512, (tb + 1) * 512)
                    yT, ByT = nxt_y
                    if tb + 1 < NB:
                        nxt_y = load_yT(tb + 1)
                    for tt in range(4):
                        ti = tb * 4 + tt
                        rows = slice(ti * 128, (ti + 1) * 128)
                        xt, Bx = xts.next()
                        dma("sp", xt[:], x_cur[rows, :], reads=[B_xcur], writes=[Bx])
                        x1t, Bx1 = x1s.next()
                        for half in range(2):
                            ps, bps = prot.next()
                            op("pe", [lambda e, ps=ps, e_=e_, tt=tt, half=half, yT=yT: e.matmul(
                                ps[:, :], lhsT=yT[:, e_, tt * 128:(tt + 1) * 128], rhs=Wo[:, e_, half * 512:(half + 1) * 512],
                                start=(e_ == 0), stop=(e_ == 15)) for e_ in range(16)], reads=[ByT, BWo], writes=[bps])
                            op("dve", lambda e, ps=ps, x1t=x1t, xt=xt, half=half: e.tensor_tensor(
                                out=x1t[:, half * 512:(half + 1) * 512], in0=ps[:, :], in1=xt[:, half * 512:(half + 1) * 512], op=ALU.add),
                               reads=[bps, Bx], writes=[Bx1])
                            wtick(2)
                        dma("sp", X1[rows, :], x1t[:], reads=[Bx1], writes=[B_X1])
                wtick(1000)
                fw.barrier()
            if stop_after == "P5a":
                stw.close()
                break
            with ExitStack() as st:
                prot_t = PRot([6, 7])
                fcT, Bfc = load_T(st, "fcT", [fconv_w[L], fconv_b[L:L + 1, :]], 2 * DFF, 128, prot_t)
                xts = Rot(fw, st, nc, "xt6", [128, D], F32, 2)
                junk = sb(st, "junk6", [128, D], BF16); Bjunk = fw.buf("junk6")
                sss = Rot(fw, st, nc, "ss6", [128, 1], F32, 2)
                hbs = Rot(fw, st, nc, "hb6", [128, D], BF16, 2)
                hT = sb(st, "hT6", [128, 8, 512], BF16); BhT = fw.buf("hT6")
                gts = Rot(fw, st, nc, "gt6", [128, 512], BF16, 3)
                Us = Rot(fw, st, nc, "U6", [128, 514], F32, 2)
                accs = Rot(fw, st, nc, "acc6", [128, 512], F32, 6)
                ffn_pend = []
                sgs = Rot(fw, st, nc, "sg6", [128, 512], F32, 2)
                halo2 = sb(st, "halo2", [128, 44, 2], F32); Bh2 = fw.buf("halo2")
                op("dve", lambda e: e.memset(halo2[:], 0.0), writes=[Bh2])
                prot_u = PRot([0, 1, 2])
                for tb in range(NB):
                    tsl = slice(tb * 512, (tb + 1) * 512)
                    for tt in range(4):
                        ti = tb * 4 + tt
                        xt, Bx = xts.next()
                        dma("sp", xt[:], X1[ti * 128:(ti + 1) * 128, :], reads=[B_X1], writes=[Bx])
                        ss, Bss = sss.next()
                        hb, Bhb = hbs.next()
                        rmsnorm_T(None, xt, Bx, ss, Bss, junk, Bjunk, hb, Bhb, hT, BhT, tt, prot_t)
                    for c in range(22):
                        pair = []
                        for cc in (c, c + 22):
                            ps, bps = prot_u.next()
                            op("pe", [lambda e, ps=ps, k=k, cc=cc: e.matmul(ps[:, :], lhsT=Wu[:, k, cc * 128:(cc + 1) * 128], rhs=hT[:, k, :],
                                                                          start=(k == 0), stop=(k == 7)) for k in range(8)],
                               reads=[BWu, BhT], writes=[bps])
                            U, BU = Us.next()
                            op("act", lambda e, U=U, ps=ps: e.copy(U[:, 2:514], ps[:, :]), reads=[bps], writes=[BU])
                            op("act", lambda e, U=U, cc=cc: e.copy(U[:, 0:2], halo2[:, cc, :]), reads=[Bh2], writes=[BU])
                            op("act", lambda e, U=U, cc=cc: e.copy(halo2[:, cc, :], U[:, 512:514]), reads=[BU], writes=[Bh2])
                            acc, Bacc = accs.next()
                            op("act", lambda e, ps=ps, acc=acc, cc=cc: e.activation(
                                out=acc[:], in_=ps[:, :], func=AF.Identity, scale=fcT[:, cc, 2:3], bias=fcT[:, cc, 3:4]),
                               reads=[bps, Bfc], writes=[Bacc])
                            for kk in (1, 0):
                                op("dve", lambda e, U=U, acc=acc, cc=cc, kk=kk: e.scalar_tensor_tensor(
                                    out=acc[:], in0=U[:, kk:kk + 512], scalar=fcT[:, cc, kk:kk + 1], in1=acc[:],
                                    op0=ALU.mult, op1=ALU.add), reads=[BU, Bfc, Bacc], writes=[Bacc])
                            pair.append((acc, Bacc))
                        (ag, Bag), (av, Bav) = pair

                        def ffn_tail(c=c, ag=ag, Bag=Bag, av=av, Bav=Bav, tsl=tsl):
                            sg, Bsg = sgs.next()
                            op("act", lambda e: e.activation(out=sg[:], in_=ag[:], func=AF.Silu), reads=[Bag], writes=[Bsg])
                            gt, Bgt = gts.next()
                            op("dve", lambda e: e.tensor_tensor(out=gt[:], in0=sg[:], in1=av[:], op=ALU.mult),
                               reads=[Bsg, Bav], writes=[Bgt])
                            dma("sp", GTS[c, :, tsl], gt[:], reads=[Bgt], writes=[B_GTS])
                        if ffn_pend:
                            ffn_pend.pop(0)()
                        ffn_pend.append(ffn_tail)
                while ffn_pend:
                    ffn_pend.pop(0)()
                fw.barrier()
            stw.close()
            with ExitStack() as st:
                if last:
                    FG, BFG = load_bc(st, "FG", fin_g.rearrange("(a b) -> a b", a=1), D)
                Wd, BWd = load_weight(st, "Wd", w_down[L], DFF, D, piece=512)
                xts = Rot(fw, st, nc, "xt7", [128, D], F32, 2)
                x2s = Rot(fw, st, nc, "x2t", [128, D], F32, 2)
                GTs = Rot(fw, st, nc, "GT", [128, 22, 512], BF16, 2)
                junk = sb(st, "junk7", [128, D], BF16); Bjunk = fw.buf("junk7")
                sss = Rot(fw, st, nc, "ss7", [128, 1], F32, 2)
                prot_d = PRot([0, 1, 2, 3])
                def load_GT(tb):
                    tsl_ = slice(tb * 512, (tb + 1) * 512)
                    GT_, BGT_ = GTs.next()
                    for c4 in range(0, 22, 2):
                        dma("sp", GT_[:, c4:c4 + 2, :], GTS[c4:c4 + 2, :, tsl_].rearrange("c p t -> p c t"), reads=[B_GTS], writes=[BGT_])
                    return GT_, BGT_

                nxt_g = load_GT(0)
                for tb in range(NB):
                    tsl = slice(tb * 512, (tb + 1) * 512)
                    GT, BGT = nxt_g
                    if tb + 1 < NB:
                        nxt_g = load_GT(tb + 1)
                    for tt in range(4):
                        ti = tb * 4 + tt
                        rows = slice(ti * 128, (ti + 1) * 128)
                        xt, Bx = xts.next()
                        dma("sp", xt[:], X1[rows, :], reads=[B_X1], writes=[Bx])
                        x2t, Bx2 = x2s.next()
                        for half in range(2):
                            ps, bps = prot_d.next()
                            op("pe", [lambda e, ps=ps, c=c, tt=tt, half=half: e.matmul(
                                ps[:, :], lhsT=GT[:, c, tt * 128:(tt + 1) * 128], rhs=Wd[:, c, half * 512:(half + 1) * 512],
                                start=(c == 0), stop=(c == 21)) for c in range(22)], reads=[BGT, BWd], writes=[bps])
                            op("dve", lambda e, ps=ps, x2t=x2t, xt=xt, half=half: e.tensor_tensor(
                                out=x2t[:, half * 512:(half + 1) * 512], in0=ps[:, :], in1=xt[:, half * 512:(half + 1) * 512], op=ALU.add),
                               reads=[bps, Bx], writes=[Bx2])
                        if not last:
                            dma("sp", X2[rows, :], x2t[:], reads=[Bx2], writes=[B_X2])
                        else:
                            ss, Bss = sss.next()
                            op("act", lambda e, x2t=x2t, ss=ss: e.activation(out=junk[:], in_=x2t[:], func=AF.Square, accum_out=ss[:]),
                               reads=[Bx2], writes=[Bjunk, Bss])
                            op("act", lambda e, ss=ss: e.activation(out=ss[:], in_=ss[:], func=AF.Ln, scale=1.0 / D, bias=EPS),
                               reads=[Bss], writes=[Bss])
                            op("act", lambda e, ss=ss: e.activation(out=ss[:], in_=ss[:], func=AF.Exp, scale=-0.5), reads=[Bss], writes=[Bss])
                            op("dve", lambda e, x2t=x2t, ss=ss: e.scalar_tensor_tensor(
                                out=x2t[:], in0=x2t[:], scalar=ss[:, 0:1], in1=FG[:], op0=ALU.mult, op1=ALU.mult),
                               reads=[Bx2, Bss, BFG], writes=[Bx2])
                            dma("sp", out[rows, :], x2t[:], reads=[Bx2], writes=[B_OUT])
                fw.barrier()
            x_cur = X2
            B_xcur = B_X2
        fw.finish([B_OUT, B_QF, B_KF, B_VF, B_QS, B_KS, B_VS, B_ZS, B_XS, B_DTS, B_BT, B_CT, B_BTOK, B_YT, B_X1, B_X2])
    return nc


_INPUT_NAMES = ["x", "mix_norm_g", "w_in", "fox_f_bias", "fox_out_g", "sb_out_g", "ssd_conv_w", "ssd_conv_b",
                "ssd_dt_bias", "ssd_a_log", "ssd_d", "ssd_norm_g", "w_out", "ffn_norm_g", "w_up", "ffn_conv_w",
                "ffn_conv_b", "w_down", "final_norm_g"]


def kernel(**inputs):
    nc = build(depth=2)
    shared = {k: np.ascontiguousarray(np.asarray(inputs[k], dtype=np.float32)) for k in _INPUT_NAMES if k != "x"}
    x = np.asarray(inputs["x"], dtype=np.float32)
    in_maps = []
    for c in range(8):
        m = dict(shared)
        m["x"] = np.ascontiguousarray(x[c])
        in_maps.append(m)
    res = run_bass_kernel_spmd(nc, in_maps, core_ids=list(range(8)))
    return np.stack([np.asarray(r["out"], dtype=np.float32) for r in res.results], axis=0)
```
